# Optimizing a Trainium2 kernel written in Bass

```python
import math
import jax, jax.numpy as jnp
from jax import lax
import numpy as np


D_MODEL = 1024
BATCH = 8
SEQ = 2048
DEPTH = 4

N_A = DEPTH // 2
N_B = DEPTH - N_A
RET_HEADS = 4
RET_QK_DIM = D_MODEL // RET_HEADS
RET_V_DIM = 2 * RET_QK_DIM
RET_CHUNK = 128
RET_ROPE_BASE = 10000.0
DIFF_HEAD_DIM = 64
DIFF_HEADS = D_MODEL // (2 * DIFF_HEAD_DIM)
DIFF_V_DIM = 2 * DIFF_HEAD_DIM
ROPE_THETA = 500000.0
ROPE_DIM = DIFF_HEAD_DIM // 4
Q_BLOCK = 128
D_FF = 4 * D_MODEL
EPS = 1e-6

kernel_name = 'yoco_retention_diffattn_sandwich_adaln'


def _rms(x, g=None):
    xf = x.astype(jnp.float32)
    y = xf * lax.rsqrt(jnp.mean(xf * xf, axis=-1, keepdims=True) + EPS)
    if g is not None:
        y = y * g.astype(jnp.float32)
    return y.astype(x.dtype)


def _rope_tables(positions, rot_dim, base, dtype):
    inv = base ** (-jnp.arange(0, rot_dim, 2, dtype=jnp.float32) / rot_dim)
    ang = positions.astype(jnp.float32)[..., None] * inv
    return jnp.cos(ang)[:, :, None, :].astype(dtype), jnp.sin(ang)[:, :, None, :].astype(dtype)


def _rope(x, cos, sin):
    r = 2 * cos.shape[-1]
    x1 = x[..., : r // 2]
    x2 = x[..., r // 2: r]
    return jnp.concatenate([x1 * cos - x2 * sin, x2 * cos + x1 * sin, x[..., r:]], axis=-1)


def _retention(h, cos, sin, w_in, w_out):
    B, S, _ = h.shape
    H, dk, dv, C = RET_HEADS, RET_QK_DIM, RET_V_DIM, RET_CHUNK
    N = S // C
    proj = h @ w_in
    q, k, v, g = jnp.split(proj, [H * dk, 2 * H * dk, 2 * H * dk + H * dv], axis=-1)
    q = _rope(q.reshape(B, S, H, dk), cos, sin)
    k = _rope(k.reshape(B, S, H, dk), cos, sin) * (dk ** -0.5)
    v = v.reshape(B, S, H, dv)
    to_chunks = lambda t: t.reshape(B, N, C, H, t.shape[-1]).transpose(1, 0, 3, 2, 4)
    qs, ks, vs = to_chunks(q), to_chunks(k), to_chunks(v)
    log_g = jnp.log1p(-jnp.exp2(-5.0 - jnp.arange(H, dtype=jnp.float32)))
    idx = jnp.arange(C, dtype=jnp.float32)
    diff = idx[:, None] - idx[None, :]
    dmask = jnp.where(diff >= 0, jnp.exp(jnp.maximum(diff, 0.0)[None] * log_g[:, None, None]), 0.0).astype(h.dtype)
    xi = jnp.exp((idx + 1.0)[None] * log_g[:, None]).astype(h.dtype)
    zeta = jnp.exp((C - 1.0 - idx)[None] * log_g[:, None]).astype(h.dtype)
    chunk_decay = jnp.exp(C * log_g).astype(h.dtype)
    intra = jnp.einsum('nbhcm,nbhme->nbhce', jnp.einsum('nbhcd,nbhmd->nbhcm', qs, ks) * dmask, vs)

    def step(R, inp):
        qc, kc, vc = inp
        cross = jnp.einsum('bhcd,bhde->bhce', qc, R) * xi[None, :, :, None]
        R = R * chunk_decay[None, :, None, None] + jnp.einsum('bhcd,bhce->bhde', kc, vc * zeta[None, :, :, None])
        return R, cross

    R0 = jnp.zeros((B, H, dk, dv), h.dtype)
    _, cross = lax.scan(step, R0, (qs, ks, vs))
    o = (intra + cross).transpose(1, 0, 3, 2, 4).reshape(B, S, H, dv)
    o = _rms(o).reshape(B, S, H * dv)
    return (jax.nn.silu(g) * o) @ w_out


def _shared_kv(x, c_act, g, ada_w, ada_b, w_kv, cos, sin):
    B, S, _ = x.shape
    shift, scale = jnp.split((c_act @ ada_w + ada_b)[:, None, :], 2, axis=-1)
    kv = (_rms(x, g) * (1 + scale) + shift) @ w_kv
    k, v = jnp.split(kv, 2, axis=-1)
    k = _rope(k.reshape(B, S, 2 * DIFF_HEADS, DIFF_HEAD_DIM), cos, sin)
    v = v.reshape(B, S, DIFF_HEADS, DIFF_V_DIM)
    return k, v


def _diff_attention(h, k, v, cos, sin, w_q, w_o, lam, subln_g, lambda_init):
    B, S, _ = h.shape
    H, d, QB = DIFF_HEADS, DIFF_HEAD_DIM, Q_BLOCK
    NB = S // QB
    q = _rope((h @ w_q).reshape(B, S, 2 * H, d), cos, sin) * (d ** -0.5)
    lf = lam.astype(jnp.float32)
    lam_full = jnp.exp(jnp.sum(lf[0] * lf[1])) - jnp.exp(jnp.sum(lf[2] * lf[3])) + lambda_init
    q_blocks = q.reshape(B, NB, QB, 2 * H, d).transpose(1, 0, 2, 3, 4)
    starts = jnp.arange(NB, dtype=jnp.int32) * QB
    key_idx = jnp.arange(S, dtype=jnp.int32)

    def block(args):
        qb, s0 = args
        s = jnp.einsum('bqhd,bkhd->bhqk', qb, k).astype(jnp.float32)
        causal = (s0 + jnp.arange(QB, dtype=jnp.int32))[:, None] >= key_idx[None, :]
        p = jax.nn.softmax(jnp.where(causal, s, -jnp.inf), axis=-1).reshape(B, H, 2, QB, S)
        w = (p[:, :, 0] - lam_full * p[:, :, 1]).astype(v.dtype)
        o = jnp.einsum('bhqk,bkhe->bqhe', w, v)
        return _rms(o, subln_g) * (1.0 - lambda_init)

    o = lax.map(block, (q_blocks, starts))
    return o.transpose(1, 0, 2, 3, 4).reshape(B, S, H * DIFF_V_DIM) @ w_o


def setup_inputs(seed: int = 0) -> dict:
    key = jax.random.key(seed)
    ks = jax.random.split(key, 18)
    D = D_MODEL
    nrm = lambda k, shape, s: jax.random.normal(k, shape, jnp.float32) * s
    x = nrm(ks[0], (BATCH, SEQ, D), 1.0)
    c = nrm(ks[1], (BATCH, D), 1.0)
    offsets = jax.random.randint(ks[2], (BATCH, 1), 0, 1024, dtype=jnp.int32)
    positions = offsets + jnp.arange(SEQ, dtype=jnp.int32)[None, :]
    return {
        'x': x,
        'c': c,
        'positions': positions,
        'norm_g': 1.0 + nrm(ks[3], (DEPTH, 4, D), 0.02),
        'ada_w': nrm(ks[4], (DEPTH, D, 6 * D), 0.5 * D ** -0.5),
        'ada_b': nrm(ks[5], (DEPTH, 6 * D), 0.01),
        'ret_w_in': nrm(ks[6], (N_A, D, 2 * RET_HEADS * RET_QK_DIM + 2 * RET_HEADS * RET_V_DIM), D ** -0.5),
        'ret_w_out': nrm(ks[7], (N_A, RET_HEADS * RET_V_DIM, D), (RET_HEADS * RET_V_DIM) ** -0.5),
        'kv_norm_g': 1.0 + nrm(ks[8], (D,), 0.02),
        'kv_ada_w': nrm(ks[9], (D, 2 * D), 0.5 * D ** -0.5),
        'kv_ada_b': nrm(ks[10], (2 * D,), 0.01),
        'kv_w': nrm(ks[11], (D, 2 * DIFF_HEADS * DIFF_HEAD_DIM + DIFF_HEADS * DIFF_V_DIM), D ** -0.5),
        'diff_w_q': nrm(ks[12], (N_B, D, 2 * DIFF_HEADS * DIFF_HEAD_DIM), D ** -0.5),
        'diff_w_o': nrm(ks[13], (N_B, DIFF_HEADS * DIFF_V_DIM, D), (DIFF_HEADS * DIFF_V_DIM) ** -0.5),
        'diff_lam': nrm(ks[14], (N_B, 4, DIFF_HEAD_DIM), 0.1),
        'diff_subln_g': 1.0 + nrm(ks[15], (N_B, DIFF_V_DIM), 0.02),
        'mlp_w1': nrm(ks[16], (DEPTH, D, D_FF), D ** -0.5),
        'mlp_w2': nrm(ks[17], (DEPTH, D_FF, D), D_FF ** -0.5),
    }


def reference(x, c, positions, norm_g, ada_w, ada_b, ret_w_in, ret_w_out, kv_norm_g, kv_ada_w, kv_ada_b,
              kv_w, diff_w_q, diff_w_o, diff_lam, diff_subln_g, mlp_w1, mlp_w2):
    c_act = jax.nn.silu(c)
    ret_cos, ret_sin = _rope_tables(positions, RET_QK_DIM, RET_ROPE_BASE, x.dtype)
    cos, sin = _rope_tables(positions, ROPE_DIM, ROPE_THETA, x.dtype)
    k_sh, v_sh = None, None
    for l in range(DEPTH):
        if l == N_A:
            k_sh, v_sh = _shared_kv(x, c_act, kv_norm_g, kv_ada_w, kv_ada_b, kv_w, cos, sin)
        mod = (c_act @ ada_w[l] + ada_b[l])[:, None, :]
        sh_a, sc_a, ga_a, sh_m, sc_m, ga_m = jnp.split(mod, 6, axis=-1)
        h = _rms(x, norm_g[l, 0]) * (1 + sc_a) + sh_a
        if l < N_A:
            y = _retention(h, ret_cos, ret_sin, ret_w_in[l], ret_w_out[l])
        else:
            j = l - N_A
            y = _diff_attention(h, k_sh, v_sh, cos, sin, diff_w_q[j], diff_w_o[j], diff_lam[j],
                                diff_subln_g[j], 0.8 - 0.6 * math.exp(-0.3 * l))
        x = x + (1 + ga_a) * _rms(y, norm_g[l, 1])
        h = _rms(x, norm_g[l, 2]) * (1 + sc_m) + sh_m
        y = jnp.square(jax.nn.relu(h @ mlp_w1[l])) @ mlp_w2[l]
        x = x + (1 + ga_m) * _rms(y, norm_g[l, 3])
    return x
```

```python
import math
import numpy as np
import concourse.bass as bass
import concourse.mybir as mybir
from concourse.bass_utils import run_bass_kernel_spmd

F32 = mybir.dt.float32
BF16 = mybir.dt.bfloat16
I32 = mybir.dt.int32
AF = mybir.ActivationFunctionType
ALU = mybir.AluOpType

S = 2048
D = 1024
T = 512
NT = S // T
DFF = 4096
EPS = 1e-6
RH = 4
DEPTH = 4
NSLOT = 3
PG = 256

X0 = 0
U0 = 65536
SL0 = 131072
C0 = SL0 + NSLOT * 8192
WK0 = C0 + 6144
SB_BYTES = 207 * 1024
WK_BYTES = SB_BYTES - WK0


class Opd:
    __slots__ = ("ap", "keys")

    def __init__(self, ap, keys):
        self.ap = ap
        self.keys = keys


class Buf:
    def __init__(self, sbt, off, n, dt):
        self.esz = 4 if dt in (F32, I32) else 2
        assert off % 4 == 0
        self.off = off
        self.n = n
        self.dt = dt
        w0 = off // 4
        w1 = (off + n * self.esz + 3) // 4
        ap = sbt[:, w0:w1]
        if dt != F32:
            ap = ap.bitcast(dt)
        self.ap = ap

    def s(self, a=0, b=None, p0=0, p1=128):
        if b is None:
            b = self.n
        k0 = (self.off + a * self.esz) // PG
        k1 = (self.off + b * self.esz - 1) // PG
        return Opd(self.ap[p0:p1, a:b], [("sb", k) for k in range(k0, k1 + 1)])


class PBank:
    def __init__(self, pt, idx):
        self.ap = pt[:]
        self.idx = idx
        self.apb = pt[:].bitcast(BF16)

    def s(self, a=0, b=512, p0=0, p1=128):
        return Opd(self.ap[p0:p1, a:b], [("ps", self.idx)])

    def sb16(self, a=0, b=1024, p0=0, p1=128):
        return Opd(self.apb[p0:p1, a:b], [("ps", self.idx)])


class Sched:
    ENG = ("pe", "act", "dve", "pool", "sp")

    def __init__(self):
        self.ops = []
        self.lastw = {}
        self.readers = {}
        self.dma_cum = {}

    def add(self, eng, fn, r=(), w=(), dma=None):
        deps = set()
        for o in r:
            for k in o.keys:
                lw = self.lastw.get(k)
                if lw is not None:
                    deps.add(lw)
                if k[0] == "ps":
                    rd = self.readers.get(k)
                    if rd:
                        deps.update(x for x in rd if self.ops[x]["eng"] != eng)
        for o in w:
            for k in o.keys:
                lw = self.lastw.get(k)
                if lw is not None:
                    deps.add(lw)
                rd = self.readers.get(k)
                if rd:
                    deps.update(rd)
        oid = len(self.ops)
        cdeps = {}
        ddeps = {}
        for d in deps:
            od = self.ops[d]
            if od["dma"] is not None:
                sk = od["dma"]
                ddeps[sk] = self.dma_cum[sk]
            else:
                if od["eng"] == "pe" and eng == "pe":
                    continue
                e = od["eng"]
                if e not in cdeps or cdeps[e] < d:
                    cdeps[e] = d
        if dma is not None:
            self.dma_cum[dma] = self.dma_cum.get(dma, 0) + 16
        self.ops.append(dict(eng=eng, fn=fn, cdeps=cdeps, ddeps=ddeps, dma=dma, sig=False))
        for o in r:
            for k in o.keys:
                self.readers.setdefault(k, set()).add(oid)
        for o in w:
            for k in o.keys:
                self.lastw[k] = oid
                self.readers[k] = set()
        return oid

    def emit(self, nc, sems, dsems, block):
        ops = self.ops
        for op in ops:
            for e, d in op["cdeps"].items():
                ops[d]["sig"] = True
        cnt = {e: 0 for e in self.ENG}
        for op in ops:
            if op["dma"] is None and op["sig"]:
                cnt[op["eng"]] += 1
                op["sigval"] = cnt[op["eng"]]
        self.sigcounts = dict(cnt)
        per = {e: [] for e in self.ENG}
        for op in ops:
            per[op["eng"]].append(op)

        def run(ename, eng):
            waited = {}
            for op in per[ename]:
                for e, d in op["cdeps"].items():
                    v = ops[d]["sigval"]
                    if waited.get(e, 0) < v:
                        eng.wait_ge(sems[e], v)
                        waited[e] = v
                for sk, v in op["ddeps"].items():
                    if waited.get(sk, 0) < v:
                        eng.wait_ge(dsems[sk], v)
                        waited[sk] = v
                ins = op["fn"](eng)
                if op["dma"] is not None:
                    ins.then_inc(dsems[op["dma"]], 16)
                elif op["sig"]:
                    ins.then_inc(sems[ename], 1)
            if ename == "sp":
                for sk, v in self.dma_cum.items():
                    if sk.startswith("out"):
                        eng.wait_ge(dsems[sk], v)

        @block.tensor
        def _(e):
            run("pe", e)

        @block.scalar
        def _(e):
            run("act", e)

        @block.vector
        def _(e):
            run("dve", e)

        @block.gpsimd
        def _(e):
            run("pool", e)

        @block.sync
        def _(e):
            run("sp", e)


def _consts():
    c = {}
    c["identf"] = np.eye(128, dtype=np.float32)
    tri = (np.arange(128)[None, :] >= np.arange(128)[:, None]).astype(np.float32)
    c["tri"] = tri
    perm = np.zeros((128, 128), np.float32)
    invd = np.zeros((128, 1), np.float32)
    sgn = np.ones((128, 1), np.float32)
    fr = (500000.0 ** (-np.arange(0, 16, 2, dtype=np.float32) / np.float32(16))).astype(np.float32)
    for p in range(128):
        dd = p % 64
        if dd < 8:
            perm[p + 8, p] = 1.0
            invd[p, 0] = fr[dd]
            sgn[p, 0] = -1.0
        elif dd < 16:
            perm[p - 8, p] = 1.0
            invd[p, 0] = fr[dd - 8]
            sgn[p, 0] = 1.0
    c["perm"] = perm
    c["invd"] = invd
    c["sgn"] = sgn
    c["invr"] = (10000.0 ** (-np.arange(0, 256, 2, dtype=np.float32) / np.float32(256))).astype(np.float32).reshape(128, 1)
    gam = 1.0 - 2.0 ** (-5.0 - np.arange(RH, dtype=np.float64))
    lg = np.log(gam)
    idx = np.arange(128, dtype=np.float64)
    xi = np.exp((idx + 1.0)[None, :] * lg[:, None])
    zeta = np.exp((127.0 - idx)[None, :] * lg[:, None])
    c["gC"] = [float(np.exp(128.0 * lg[h])) for h in range(RH)]
    xirep = np.tile(xi[:, None, :], (1, 4, 1)).reshape(RH * 512)
    c["xirep"] = np.broadcast_to(xirep[None, :], (128, RH * 512)).astype(np.float32).copy()
    c["zeta"] = (zeta.T / 16.0).astype(np.float32).copy()
    m2 = np.zeros((128, RH, 128), np.float64)
    for h in range(RH):
        m2[:, h, :] = tri * np.exp(-(idx + 1.0) * lg[h])[:, None] / 16.0
    c["mask2"] = m2.reshape(128, RH * 128).astype(np.float32)
    return c


TWO_PI = 2.0 * math.pi
CW1 = 6.28125
CW2 = TWO_PI - CW1
PI_LO = 3.1415925


def build(nlayers=DEPTH):
    CN = _consts()
    nc = bass.Bass("TRN2", target_bir_lowering=False)
    dram = {}

    def din(name, shape, dt=F32):
        dram[name] = nc.dram_tensor(name, list(shape), dt, kind="ExternalInput").ap()
        return dram[name]

    x_d = din("x", [S, D])
    c_d = din("c", [128, 8])
    pos_d = din("pos", [1, S], I32)
    ng_d = din("norm_g", [128, 16 * 8])
    adaw_d = din("ada_w", [DEPTH, D, 6 * D])
    adab_d = din("ada_b", [DEPTH, 6 * D])
    win_d = din("ret_w_in", [2, D, 6144])
    wout_d = din("ret_w_out", [2, 2048, D])
    kvg_d = din("kv_norm_g", [128, 8])
    kvaw_d = din("kv_ada_w", [D, 2 * D])
    kvab_d = din("kv_ada_b", [1, 2 * D])
    kvw_d = din("kv_w", [D, 2 * D])
    wq_d = din("diff_w_q", [2, D, D])
    wo_d = din("diff_w_o", [2, D, D])
    lam_d = din("diff_lam", [2, 256])
    sub_d = din("diff_subln_g", [128, 2])
    w1_d = din("mlp_w1", [DEPTH, D, DFF])
    w2_d = din("mlp_w2", [DEPTH, DFF, D])
    k_identf = din("k_identf", [128, 128])
    k_tri = din("k_tri", [128, 128])
    k_perm = din("k_perm", [128, 128])
    k_small = din("k_small", [128, 8])
    k_xirep = din("k_xirep", [128, RH * 512])
    k_mask2 = din("k_mask2", [128, RH * 128])
    out_d = nc.dram_tensor("out", [S, D], F32, kind="ExternalOutput").ap()

    sc = Sched()
    import contextlib
    es = contextlib.ExitStack()
    with es:
        sbt = es.enter_context(nc.sbuf_tensor("SB", [128, SB_BYTES // 4], F32))
        psb = [PBank(es.enter_context(nc.psum_tensor(f"ps{i}", [128, 512], F32)), i) for i in range(8)]
        sems = {e: es.enter_context(nc.semaphore("s_" + e)) for e in Sched.ENG}
        dkeys = [f"slot{i}" for i in range(NSLOT)] + ["misc", "stage0", "stage1", "pos", "bias0", "bias1", "out0", "out1"]
        dsems = {k: es.enter_context(nc.semaphore("d_" + k)) for k in dkeys}
        block = es.enter_context(nc.Block())

        def B(off, n, dt=F32):
            return Buf(sbt, off, n, dt)

        xT = [[B(X0 + (fc * S + t * T) * 4, T) for t in range(NT)] for fc in range(8)]
        slots = [B(SL0 + i * 8192, 4096, BF16) for i in range(NSLOT)]
        identb = B(C0 + 0, 128, BF16)
        onesb = B(C0 + 256, 128, BF16)
        trib = B(C0 + 512, 128, BF16)
        permb = B(C0 + 768, 128, BF16)
        identf = B(C0 + 1024, 128)
        onef = B(C0 + 1536, 4)
        ksm = B(C0 + 1552, 8)
        cT = B(C0 + 1584, 8, BF16)
        c32 = B(C0 + 1600, 8)
        modcol = B(C0 + 1664, 48)
        ngb = B(C0 + 1856, 128)
        g0sc = B(C0 + 2368, 8)
        gga = B(C0 + 2400, 8)
        g2sc = B(C0 + 2432, 8)
        ggm = B(C0 + 2464, 8)
        kvgb = B(C0 + 2560, 8)
        kvgsc = B(C0 + 2592, 8)
        subg = B(C0 + 2720, 2)
        subsc = B(C0 + 2728, 1)
        neglam = B(C0 + 2732, 1)
        lamb = B(C0 + 3072, 256)
        lamt = B(C0 + 4096, 128)
        lams = B(C0 + 4608, 4)
        stg32 = B(C0 + 4864, 128)
        cosT = B(U0, S)
        sinT = B(U0 + 8192, S)
        xirep = B(U0 + 16384, RH * 512)
        R32 = [[B(U0 + 24576 + (h * 2 + dc) * 2048, 512) for dc in range(2)] for h in range(RH)]
        Rbf = [[B(U0 + 40960 + (h * 2 + dc) * 1024, 512, BF16) for dc in range(2)] for h in range(RH)]
        mask2 = B(U0 + 49152, RH * 128)
        o_sb = [B(U0 + 51200 + i * 2048, 512) for i in range(2)]
        osq = [B(U0 + 55296 + i * 1024, 512, BF16) for i in range(2)]
        PTb = [B(U0 + 57344 + i * 256, 128, BF16) for i in range(2)]
        ortmp = B(U0 + 57856, 128)
        orstd = [B(U0 + 58368 + i * 512, 128) for i in range(2)]
        osd = B(U0 + 59392, 128)
        kTsh = [B(U0 + i * 4096, S, BF16) for i in range(8)]
        vsh = [B(U0 + 32768 + kb * 2048, 1024, BF16) for kb in range(16)]

        def mm(out, lhsT, rhs, start, stop):
            sc.add("pe", lambda e: e.matmul(out.ap, lhsT=lhsT.ap, rhs=rhs.ap, start=start, stop=stop),
                   r=[lhsT, rhs], w=[out])

        def tr(out, in_, ident):
            sc.add("pe", lambda e: e.transpose(out.ap, in_.ap, ident.ap), r=[in_, ident], w=[out])

        def act(out, in_, func, scale=None, bias=None, eng="act"):
            rr = [in_]
            kw = {}
            if scale is not None:
                if isinstance(scale, Opd):
                    rr.append(scale)
                    kw["scale"] = scale.ap
                else:
                    kw["scale"] = scale
            if bias is not None:
                if isinstance(bias, Opd):
                    rr.append(bias)
                    kw["bias"] = bias.ap
                else:
                    kw["bias"] = bias
            sc.add("act", lambda e: e.activation(out=out.ap, in_=in_.ap, func=func, **kw), r=rr, w=[out])

        def tt(out, in0, in1, op, eng="dve"):
            sc.add(eng, lambda e: e.tensor_tensor(out=out.ap, in0=in0.ap, in1=in1.ap, op=op), r=[in0, in1], w=[out])

        def ts(out, in0, s1, s2, op0, op1=None, eng="dve"):
            rr = [in0]
            a1 = s1
            if isinstance(s1, Opd):
                rr.append(s1)
                a1 = s1.ap
            a2 = s2
            if isinstance(s2, Opd):
                rr.append(s2)
                a2 = s2.ap
            if op1 is None:
                sc.add(eng, lambda e: e.tensor_single_scalar(out=out.ap, in_=in0.ap, scalar=a1, op=op0), r=rr, w=[out])
            else:
                sc.add(eng, lambda e: e.tensor_scalar(out=out.ap, in0=in0.ap, scalar1=a1, scalar2=a2, op0=op0, op1=op1),
                       r=rr, w=[out])

        def stt(out, in0, scalar, in1, op0, op1, eng="dve"):
            rr = [in0, in1]
            a = scalar
            if isinstance(scalar, Opd):
                rr.append(scalar)
                a = scalar.ap
            sc.add(eng, lambda e: e.scalar_tensor_tensor(out=out.ap, in0=in0.ap, scalar=a, in1=in1.ap, op0=op0, op1=op1),
                   r=rr, w=[out])

        def cp(out, in_, eng="dve"):
            if eng == "act":
                sc.add("act", lambda e: e.activation(out=out.ap, in_=in_.ap, func=AF.Identity), r=[in_], w=[out])
            else:
                sc.add(eng, lambda e: e.tensor_copy(out=out.ap, in_=in_.ap), r=[in_], w=[out])

        def recip(out, in_):
            sc.add("dve", lambda e: e.reciprocal(out=out.ap, in_=in_.ap), r=[in_], w=[out])

        def memset(out, val, eng="dve"):
            sc.add(eng, lambda e: e.memset(out.ap, val), w=[out])

        def dma(queue, semkey, out_ap, in_ap, r=(), w=()):
            sc.add(queue, lambda e: e.dma_start(out=out_ap, in_=in_ap), r=list(r), w=list(w), dma=semkey)

        piece_ctr = [0]

        def load_piece(srcs):
            i = piece_ctr[0]
            piece_ctr[0] += 1
            sl = slots[i % NSLOT]
            for (src, kcn, W, off, n) in srcs:
                dst = sl.ap.rearrange("p (k n) -> p k n", k=kcn)[:, :, off:off + n]
                dma("pool", f"slot{i % NSLOT}", dst, src, w=[sl.s(k * W + off, k * W + off + n) for k in range(kcn)])
            return sl

        def wview(d2, kcn):
            return d2.rearrange("(k p) n -> p k n", p=128)

        def small_load(dst, src_ap):
            dma("sp", "misc", dst.ap, src_ap, w=[dst.s()])

        small_load(identf, k_identf)
        small_load(ksm, k_small)
        small_load(c32, c_d)
        small_load(ngb, ng_d)
        small_load(kvgb, kvg_d)
        small_load(subg, sub_d)
        small_load(xirep, k_xirep)
        small_load(mask2, k_mask2)
        small_load(stg32, k_tri)
        cp(trib.s(), stg32.s())
        small_load(stg32, k_perm)
        cp(permb.s(), stg32.s())
        cp(identb.s(), identf.s())
        memset(onesb.s(), 1.0)
        memset(onef.s(), 1.0)
        act(cT.s(), c32.s(), AF.Silu)
        invr = ksm.s(0, 1)
        invd = ksm.s(1, 2)
        sgn = ksm.s(2, 3)

        import os
        SKIP = set(os.environ.get("KSKIP", "").split(","))
        stage = [B(WK0 + i * 4096, 1024) for i in range(2)]
        for tc in range(16 if "xload" not in SKIP else 0):
            st = stage[tc % 2]
            dma("sp", f"stage{tc % 2}", st.ap, x_d[tc * 128:(tc + 1) * 128, :], w=[st.s()])
            t, cc = tc // 4, tc % 4
            for half in range(2):
                pb = psb[(tc * 2 + half) % 4]
                for q in range(4):
                    fc = half * 4 + q
                    if "xtr" not in SKIP:
                        tr(pb.s(q * 128, (q + 1) * 128), st.s(fc * 128, (fc + 1) * 128), identf.s())
                    else:
                        mm(pb.s(q * 128, (q + 1) * 128), st.s(fc * 128, (fc + 1) * 128), identf.s(), True, True)
                for q in range(4):
                    fc = half * 4 + q
                    if "xcp" not in SKIP:
                        cp(xT[fc][t].s(cc * 128, (cc + 1) * 128), pb.s(q * 128, (q + 1) * 128),
                           eng="dve" if "xdve" in SKIP else ("act" if ("xact" in SKIP or q % 2) else "dve"))

        def sincos(posf, inv, ncol, cos_out, sin_out, tmpa, tmpb, tmpi, sin_scale=None):
            ang = tmpa
            ts(ang.s(0, ncol), posf.s(0, ncol), inv, None, ALU.mult)
            ts(tmpb.s(0, ncol), ang.s(0, ncol), 1.0 / TWO_PI, None, ALU.mult)
            cp(tmpi.s(0, ncol), tmpb.s(0, ncol))
            cp(tmpb.s(0, ncol), tmpi.s(0, ncol))
            stt(ang.s(0, ncol), tmpb.s(0, ncol), -CW1, ang.s(0, ncol), ALU.mult, ALU.add)
            stt(ang.s(0, ncol), tmpb.s(0, ncol), -CW2, ang.s(0, ncol), ALU.mult, ALU.add)
            ts(tmpb.s(0, ncol), ang.s(0, ncol), -PI_LO, PI_LO, ALU.max, ALU.min)
            act(sin_out.s(0, ncol), tmpb.s(0, ncol), AF.Sin, scale=sin_scale)
            ts(ang.s(0, ncol), ang.s(0, ncol), math.pi / 2.0, None, ALU.add)
            ts(tmpb.s(0, ncol), ang.s(0, ncol), math.pi, TWO_PI, ALU.is_gt, ALU.mult)
            tt(ang.s(0, ncol), ang.s(0, ncol), tmpb.s(0, ncol), ALU.subtract)
            ts(tmpb.s(0, ncol), ang.s(0, ncol), -PI_LO, PI_LO, ALU.max, ALU.min)
            act(cos_out.s(0, ncol), tmpb.s(0, ncol), AF.Sin)

        posi = B(WK0 + 8192, S, I32)
        posf = B(WK0 + 16384, S)
        tA = B(WK0 + 24576, S)
        tB = B(WK0 + 32768, S)
        if "tables" not in SKIP:
            dma("sp", "pos", posi.ap, pos_d.partition_broadcast(128), w=[posi.s()])
            cp(posf.s(), posi.s())
            sincos(posf, invr, S, cosT, sinT, tA, tB, posi)

        def compute_mod(wd, bd_row, ncols, dst):
            rowc = [B(WK0 + i * 2048, 512) for i in range(2)]
            biasc = [B(WK0 + 4096 + i * 2048, 512) for i in range(2)]
            pcol = psb[7]
            wv = wview(wd, 8)
            for n in range(ncols // 512):
                sl = load_piece([(wv[:, :, n * 512:(n + 1) * 512], 8, 512, 0, 512)])
                bb = biasc[n % 2]
                dma("sp", f"bias{n % 2}", bb.ap[0:1, :], bd_row[0:1, n * 512:(n + 1) * 512], w=[bb.s(p0=0, p1=1)])
                pr = psb[n % 2]
                for kc in range(8):
                    mm(pr.s(p0=0, p1=1), cT.s(kc, kc + 1), sl.s(kc * 512, (kc + 1) * 512), kc == 0, kc == 7)
                rc_ = rowc[n % 2]
                tt(rc_.s(p0=0, p1=1), pr.s(p0=0, p1=1), bb.s(p0=0, p1=1), ALU.add)
                for j in range(4):
                    col = n * 4 + j
                    mm(Opd(pcol.ap[:, col:col + 1], pcol.s(0, 128).keys), rc_.s(j * 128, (j + 1) * 128, 0, 1),
                       onef.s(0, 1, 0, 1), True, True)
            cp(dst.s(0, ncols // 128), pcol.s(0, ncols // 128))

        def layer_mod(l):
            compute_mod(adaw_d[l], adab_d[l:l + 1, :], 6 * D, modcol)
            base = l * 32
            stt(g0sc.s(), modcol.s(8, 16), 1.0, ngb.s(base + 0, base + 8), ALU.add, ALU.mult)
            stt(gga.s(), modcol.s(16, 24), 1.0, ngb.s(base + 8, base + 16), ALU.add, ALU.mult)
            stt(g2sc.s(), modcol.s(32, 40), 1.0, ngb.s(base + 16, base + 24), ALU.add, ALU.mult)
            stt(ggm.s(), modcol.s(40, 48), 1.0, ngb.s(base + 24, base + 32), ALU.add, ALU.mult)

        sha = modcol

        nsq = [B(WK0 + i * 1024, T, BF16) for i in range(2)]
        nsd = B(WK0 + 2048, T)
        nrstd = B(WK0 + 4096, T)
        ntmp = [B(WK0 + 6144 + i * 2048, T) for i in range(2)]
        HT0 = WK0 + 10240
        hT = [B(HT0 + fc * 1024, T, BF16) for fc in range(8)]
        WK1 = WK0 + 18432

        def rms_stats(srcs, pbank, dnorm):
            n = len(srcs)
            for i, s_ in enumerate(srcs):
                q = nsq[i % 2]
                act(q.s(), s_, AF.Square)
                mm(pbank.s(), onesb.s(), q.s(), i == 0, i == n - 1)
            act(nsd.s(), pbank.s(), AF.Sqrt, scale=1.0 / dnorm, bias=EPS)
            recip(nrstd.s(), nsd.s())

        def make_h(t, gsc, shv, sh0):
            rms_stats([xT[fc][t].s() for fc in range(8)], psb[3], D)
            for fc in range(8):
                tm = ntmp[fc % 2]
                stt(tm.s(), xT[fc][t].s(), gsc.s(fc, fc + 1), nrstd.s(), ALU.mult, ALU.mult)
                act(hT[fc].s(), tm.s(), AF.Identity, bias=shv.s(sh0 + fc, sh0 + fc + 1))

        def resid_update(t, ybuf, gg):
            rms_stats([ybuf[oc].s() for oc in range(8)], psb[3], D)
            for oc in range(8):
                tm = ntmp[oc % 2]
                stt(tm.s(), ybuf[oc].s(), gg.s(oc, oc + 1), nrstd.s(), ALU.mult, ALU.mult)
                tt(xT[oc][t].s(), xT[oc][t].s(), tm.s(), ALU.add)

        def mlp_tile(l, t):
            make_h(t, g2sc, modcol, 24)
            uT = [B(WK1 + i * 1024, T, BF16) for i in range(8)]
            yb = [B(WK1 + 8192 + oc * 2048, T) for oc in range(8)]
            rtmp = [B(WK1 + 24576 + i * 2048, T) for i in range(2)]
            w1v = wview(w1_d[l], 8)
            for qf in range(4):
                for half in range(2):
                    c0 = qf * 1024 + half * 512
                    sl = load_piece([(w1v[:, :, c0:c0 + 512], 8, 512, 0, 512)])
                    for j in range(4):
                        pb = psb[(half * 4 + j) % 3]
                        for kc in range(8):
                            mm(pb.s(), sl.s(kc * 512 + j * 128, kc * 512 + (j + 1) * 128), hT[kc].s(), kc == 0, kc == 7)
                        rt = rtmp[j % 2]
                        act(rt.s(), pb.s(), AF.Relu)
                        act(uT[half * 4 + j].s(), rt.s(), AF.Square)
                w2v = w2_d[l][qf * 1024:(qf + 1) * 1024, :].rearrange("(k p) n -> p k n", p=128)
                for half in range(2):
                    sl = load_piece([(w2v[:, :, half * 512:(half + 1) * 512], 8, 512, 0, 512)])
                    for j in range(4):
                        oc = half * 4 + j
                        pb = psb[4 + (oc % 3)]
                        for fk in range(8):
                            mm(pb.s(), sl.s(fk * 512 + j * 128, fk * 512 + (j + 1) * 128), uT[fk].s(), fk == 0, fk == 7)
                        if qf == 0:
                            cp(yb[oc].s(), pb.s())
                        else:
                            tt(yb[oc].s(), yb[oc].s(), pb.s(), ALU.add)
            resid_update(t, yb, ggm)

        def ret_tile(l, t):
            make_h(t, g0sc, modcol, 0)
            qT_ = [B(WK1 + dc * 1024, T, BF16) for dc in range(2)]
            kT_ = [B(WK1 + 2048 + dc * 1024, T, BF16) for dc in range(2)]
            ktok = B(WK1 + 4096, 1024, BF16)
            vb = [B(WK1 + 6144 + c * 1024, 512, BF16) for c in range(4)]
            vz = [B(U0 + 59904 + c * 1024, 512, BF16) for c in range(4)]
            sg = [B(WK1 + 10240 + ec * 1024, T, BF16) for ec in range(4)]
            rp = ntmp
            GT0 = WK1 + 14336
            gatedT = [B(GT0 + i * 1024, T, BF16) for i in range(16)]
            winv = wview(win_d[l], 8)
            cs = cosT.s(t * T, (t + 1) * T)
            sn = sinT.s(t * T, (t + 1) * T)
            for hd in range(RH):
                sl = load_piece([(winv[:, :, hd * 256:(hd + 1) * 256], 8, 512, 0, 256),
                                 (winv[:, :, 1024 + hd * 256:1024 + (hd + 1) * 256], 8, 512, 256, 256)])
                for which, dst in ((0, qT_), (1, kT_)):
                    pbs = [psb[0], psb[1]]
                    for dc in range(2):
                        c0 = which * 256 + dc * 128
                        for kc in range(8):
                            mm(pbs[dc].s(), sl.s(kc * 512 + c0, kc * 512 + c0 + 128), hT[kc].s(), kc == 0, kc == 7)
                    x1, x2 = pbs[0].s(), pbs[1].s()
                    tt(rp[0].s(), x1, cs, ALU.mult)
                    tt(rp[1].s(), x2, sn, ALU.mult)
                    tt(dst[0].s(), rp[0].s(), rp[1].s(), ALU.subtract)
                    tt(rp[0].s(), x2, cs, ALU.mult)
                    tt(rp[1].s(), x1, sn, ALU.mult)
                    tt(dst[1].s(), rp[0].s(), rp[1].s(), ALU.add)
                pbt = psb[2]
                for c in range(4):
                    for dc in range(2):
                        o0 = (c * 2 + dc) * 128
                        tr(pbt.sb16(o0, o0 + 128), kT_[dc].s(c * 128, (c + 1) * 128), identb.s())
                cp(ktok.s(), pbt.sb16(), eng="act")
                sl = load_piece([(winv[:, :, 2048 + hd * 512:2048 + (hd + 1) * 512], 8, 512, 0, 512)])
                for c in range(4):
                    pb = psb[c % 2]
                    for kc in range(8):
                        mm(pb.s(), hT[kc].s(c * 128, (c + 1) * 128), sl.s(kc * 512, (kc + 1) * 512), kc == 0, kc == 7)
                    cp(vb[c].s(), pb.s(), eng="act")
                    act(vz[c].s(), pb.s(), AF.Copy, scale=ksm.s(3 + hd, 4 + hd))
                sl = load_piece([(winv[:, :, 4096 + hd * 512:4096 + (hd + 1) * 512], 8, 512, 0, 512)])
                for ec in range(4):
                    pb = psb[ec % 2]
                    for kc in range(8):
                        mm(pb.s(), sl.s(kc * 512 + ec * 128, kc * 512 + (ec + 1) * 128), hT[kc].s(), kc == 0, kc == 7)
                    act(sg[ec].s(), pb.s(), AF.Silu)
                gC = CN["gC"][hd]
                for c in range(4):
                    n = t * 4 + c
                    csl = (c * 128, (c + 1) * 128)
                    ps_s = psb[2 + (n % 2)].s(0, 128)
                    for dc in range(2):
                        mm(ps_s, kT_[dc].s(*csl), qT_[dc].s(*csl), dc == 0, dc == 1)
                    PT = PTb[n % 2]
                    tt(PT.s(), ps_s, mask2.s(hd * 128, (hd + 1) * 128), ALU.mult)
                    po = psb[4 + (n % 2)]
                    for ec in range(4):
                        osl = po.s(ec * 128, (ec + 1) * 128)
                        mm(osl, vb[c].s(ec * 128, (ec + 1) * 128), PT.s(), True, n == 0)
                        if n > 0:
                            for dc in range(2):
                                mm(osl, Rbf[hd][dc].s(ec * 128, (ec + 1) * 128), qT_[dc].s(*csl), False, dc == 1)
                    if n < 15:
                        for dc in range(2):
                            pa = psb[6 + dc]
                            mm(pa.s(), ktok.s((c * 2 + dc) * 128, (c * 2 + dc + 1) * 128), vz[c].s(), True, True)
                            if n == 0:
                                cp(R32[hd][dc].s(), pa.s())
                            else:
                                stt(R32[hd][dc].s(), R32[hd][dc].s(), gC, pa.s(), ALU.mult, ALU.add)
                            cp(Rbf[hd][dc].s(), R32[hd][dc].s(), eng="act")
                    ob = o_sb[n % 2]
                    tt(ob.s(), po.s(), xirep.s(hd * 512, (hd + 1) * 512), ALU.mult)
                    oq = osq[n % 2]
                    act(oq.s(), ob.s(), AF.Square)
                    pst = psb[n % 2].s(0, 128)
                    for ec in range(4):
                        mm(pst, onesb.s(), oq.s(ec * 128, (ec + 1) * 128), ec == 0, ec == 3)
                    act(osd.s(), pst, AF.Sqrt, scale=1.0 / 512.0, bias=EPS)
                    rs = orstd[n % 2]
                    recip(rs.s(), osd.s())
                    for ec in range(4):
                        tt(ortmp.s(), ob.s(ec * 128, (ec + 1) * 128), rs.s(), ALU.mult)
                        tt(gatedT[hd * 4 + ec].s(*csl), ortmp.s(), sg[ec].s(*csl), ALU.mult)
            yb = [B(HT0 + oc * 2048, T) for oc in range(8)]
            woutv = wout_d[l].rearrange("(k p) n -> p k n", p=128)
            for pi_ in range(4):
                sl = load_piece([(woutv[:, :, pi_ * 256:(pi_ + 1) * 256], 16, 256, 0, 256)])
                for j in range(2):
                    oc = pi_ * 2 + j
                    pb = psb[oc % 3]
                    for kc in range(16):
                        mm(pb.s(), sl.s(kc * 256 + j * 128, kc * 256 + (j + 1) * 128), gatedT[kc].s(), kc == 0, kc == 15)
                    cp(yb[oc].s(), pb.s())
            resid_update(t, yb, gga)

        DT0 = WK1
        dC = B(DT0, T)
        dS = B(DT0 + 2048, T)
        dposi = B(DT0 + 4096, T, I32)
        dposf = B(DT0 + 6144, T)
        dtA = B(DT0 + 8192, T)
        dtB = B(DT0 + 10240, T)
        qraw = [B(DT0 + 4096 + i * 1024, T, BF16) for i in range(2)]
        drp = [B(DT0 + 8192 + i * 2048, T) for i in range(2)]

        def diff_tables(t):
            dma("sp", "pos", dposi.ap, pos_d[0:1, t * T:(t + 1) * T].partition_broadcast(128), w=[dposi.s()])
            cp(dposf.s(), dposi.s())
            sincos(dposf, invd, T, dC, dS, dtA, dtB, dposi, sin_scale=sgn)

        def rope_diff(pb, dst, idx):
            qr = qraw[idx % 2]
            cp(qr.s(), pb.s(), eng="act")
            p2 = psb[3]
            mm(p2.s(), permb.s(), qr.s(), True, True)
            tt(drp[0].s(), pb.s(), dC.s(), ALU.mult)
            tt(drp[1].s(), p2.s(), dS.s(), ALU.mult)
            tt(dst, drp[0].s(), drp[1].s(), ALU.add)

        def kv_tile(t):
            make_h(t, kvgsc, kvmod, 0)
            diff_tables(t)
            kvv = wview(kvw_d, 8)
            for half in range(2):
                sl = load_piece([(kvv[:, :, half * 512:(half + 1) * 512], 8, 512, 0, 512)])
                for j in range(4):
                    i = half * 4 + j
                    pb = psb[i % 3]
                    for kc in range(8):
                        mm(pb.s(), sl.s(kc * 512 + j * 128, kc * 512 + (j + 1) * 128), hT[kc].s(), kc == 0, kc == 7)
                    rope_diff(pb, kTsh[i].s(t * T, (t + 1) * T), i)
            for half in range(2):
                sl = load_piece([(kvv[:, :, 1024 + half * 512:1024 + (half + 1) * 512], 8, 512, 0, 512)])
                for c in range(4):
                    pb = psb[c % 3]
                    for kc in range(8):
                        mm(pb.s(), hT[kc].s(c * 128, (c + 1) * 128), sl.s(kc * 512, (kc + 1) * 512), kc == 0, kc == 7)
                    cp(vsh[t * 4 + c].s(half * 512, (half + 1) * 512), pb.s(), eng="act" if c % 2 else "dve")

        def diff_tile(l, t):
            j_ = l - 2
            make_h(t, g0sc, modcol, 0)
            diff_tables(t)
            QT0 = DT0 + 12288
            qT_ = [B(QT0 + i * 1024, T, BF16) for i in range(8)]
            OT_ = [B(QT0 + 8192 + i * 1024, T, BF16) for i in range(8)]
            E0 = DT0
            Eb = [B(E0 + i * 1024, T, BF16) for i in range(4)]
            Zr = [B(E0 + 4096 + i * 2048, T) for i in range(2)]
            t01 = [B(E0 + 8192 + i * 2048, T) for i in range(2)]
            ofp = ntmp[0]
            wqv = wview(wq_d[j_], 8)
            for half in range(2):
                sl = load_piece([(wqv[:, :, half * 512:(half + 1) * 512], 8, 512, 0, 512)])
                for j in range(4):
                    i = half * 4 + j
                    pb = psb[i % 3]
                    for kc in range(8):
                        mm(pb.s(), sl.s(kc * 512 + j * 128, kc * 512 + (j + 1) * 128), hT[kc].s(), kc == 0, kc == 7)
                    rope_diff(pb, qT_[i].s(), i)
            nkb = 4 * t + 4
            ectr = 0
            for i in range(8):
                Ob = [psb[4], psb[5]]
                Zb = [psb[6], psb[7]]
                for kb in range(nkb):
                    r = kb - 4 * t
                    q0 = 0 if r < 0 else r * 128
                    for a in range(2):
                        pS = psb[(kb * 2 + a) % 4] if False else psb[(ectr) % 3]
                        Et = Eb[ectr % 4]
                        ectr += 1
                        mm(pS.s(q0, T), kTsh[i].s(kb * 128, (kb + 1) * 128, 64 * a, 64 * a + 64),
                           qT_[i].s(q0, T, 64 * a, 64 * a + 64), True, True)
                        act(Et.s(q0, T), pS.s(q0, T), AF.Exp, scale=0.125)
                        if r >= 0:
                            tt(Et.s(q0, q0 + 128), Et.s(q0, q0 + 128), trib.s(), ALU.mult)
                        mm(Ob[a].s(q0, T), vsh[kb].s(i * 128, (i + 1) * 128), Et.s(q0, T), kb == 0, kb == nkb - 1)
                        mm(Zb[a].s(q0, T), onesb.s(), Et.s(q0, T), kb == 0, kb == nkb - 1)
                for a in range(2):
                    recip(Zr[a].s(), Zb[a].s())
                    tt(t01[a].s(), Ob[a].s(), Zr[a].s(), ALU.mult)
                stt(ofp.s(), t01[1].s(), neglam.s(), t01[0].s(), ALU.mult, ALU.add)
                q = nsq[i % 2]
                act(q.s(), ofp.s(), AF.Square)
                pst = psb[3]
                mm(pst.s(), onesb.s(), q.s(), True, True)
                act(nsd.s(), pst.s(), AF.Sqrt, scale=1.0 / 128.0, bias=EPS)
                recip(nrstd.s(), nsd.s())
                stt(OT_[i].s(), ofp.s(), subsc.s(), nrstd.s(), ALU.mult, ALU.mult)
            Y0 = DT0
            yb = [B(Y0 + oc * 2048, T) for oc in range(8)]
            wov = wview(wo_d[j_], 8)
            for half in range(2):
                sl = load_piece([(wov[:, :, half * 512:(half + 1) * 512], 8, 512, 0, 512)])
                for j in range(4):
                    oc = half * 4 + j
                    pb = psb[oc % 3]
                    for kc in range(8):
                        mm(pb.s(), sl.s(kc * 512 + j * 128, kc * 512 + (j + 1) * 128), OT_[kc].s(), kc == 0, kc == 7)
                    cp(yb[oc].s(), pb.s())
            resid_update(t, yb, gga)

        kvmod = B(C0 + 2624, 16)
        for l in range(min(nlayers, 2)):
            layer_mod(l)
            for t in range(NT):
                ret_tile(l, t)
                mlp_tile(l, t)
        if nlayers > 2:
            compute_mod(kvaw_d, kvab_d, 2 * D, kvmod)
            stt(kvgsc.s(), kvmod.s(8, 16), 1.0, kvgb.s(), ALU.add, ALU.mult)
            for t in range(NT):
                kv_tile(t)
            for l in range(2, nlayers):
                j_ = l - 2
                layer_mod(l)
                lam_init = 0.8 - 0.6 * math.exp(-0.3 * l)
                dma("sp", "misc", lamb.ap, lam_d[j_:j_ + 1, :].partition_broadcast(128), w=[lamb.s()])
                tt(lamt.s(0, 64), lamb.s(0, 64), lamb.s(64, 128), ALU.mult)
                tt(lamt.s(64, 128), lamb.s(128, 192), lamb.s(192, 256), ALU.mult)
                sc.add("dve", lambda e: e.reduce_sum(out=lams.ap[:, 0:1], in_=lamt.ap[:, 0:64], axis=mybir.AxisListType.X),
                       r=[lamt.s()], w=[lams.s()])
                sc.add("dve", lambda e: e.reduce_sum(out=lams.ap[:, 1:2], in_=lamt.ap[:, 64:128], axis=mybir.AxisListType.X),
                       r=[lamt.s()], w=[lams.s()])
                act(lams.s(2, 4), lams.s(0, 2), AF.Exp)
                tt(neglam.s(), lams.s(3, 4), lams.s(2, 3), ALU.subtract)
                ts(neglam.s(), neglam.s(), -lam_init, None, ALU.add)
                ts(subsc.s(), subg.s(j_, j_ + 1), 1.0 - lam_init, None, ALU.mult)
                for t in range(NT):
                    diff_tile(l, t)
                    mlp_tile(l, t)

        for tc in range(16):
            st = stage[tc % 2]
            t, cc = tc // 4, tc % 4
            for half in range(2):
                pb = psb[(tc * 2 + half) % 4]
                for q in range(4):
                    fc = half * 4 + q
                    tr(pb.s(q * 128, (q + 1) * 128), xT[fc][t].s(cc * 128, (cc + 1) * 128), identf.s())
                cp(st.s(half * 512, (half + 1) * 512), pb.s(), eng="act" if half else "dve")
            dma("sp", f"out{tc % 2}", out_d[tc * 128:(tc + 1) * 128, :], st.ap, r=[st.s()])

        sc.emit(nc, sems, dsems, block)
    return nc, sc


_NC_CACHE = {}


def _prep_inputs(inputs, b, CN):
    f = lambda a: np.ascontiguousarray(np.asarray(a, dtype=np.float32))
    m = {}
    m["x"] = f(inputs["x"][b])
    m["c"] = f(np.asarray(inputs["c"][b]).reshape(8, 128).T)
    m["pos"] = np.ascontiguousarray(np.asarray(inputs["positions"][b], dtype=np.int32).reshape(1, S))
    ng = np.asarray(inputs["norm_g"], dtype=np.float32).reshape(4, 4, 8, 128)
    m["norm_g"] = f(ng.transpose(3, 0, 1, 2).reshape(128, 128))
    m["ada_w"] = f(inputs["ada_w"])
    m["ada_b"] = f(inputs["ada_b"])
    m["ret_w_in"] = f(inputs["ret_w_in"])
    m["ret_w_out"] = f(inputs["ret_w_out"])
    m["kv_norm_g"] = f(np.asarray(inputs["kv_norm_g"]).reshape(8, 128).T)
    m["kv_ada_w"] = f(inputs["kv_ada_w"])
    m["kv_ada_b"] = f(np.asarray(inputs["kv_ada_b"]).reshape(1, 2 * D))
    m["kv_w"] = f(inputs["kv_w"])
    m["diff_w_q"] = f(inputs["diff_w_q"])
    m["diff_w_o"] = f(inputs["diff_w_o"])
    m["diff_lam"] = f(np.asarray(inputs["diff_lam"]).reshape(2, 256))
    m["diff_subln_g"] = f(np.asarray(inputs["diff_subln_g"]).T)
    m["mlp_w1"] = f(inputs["mlp_w1"])
    m["mlp_w2"] = f(inputs["mlp_w2"])
    m["k_identf"] = CN["identf"]
    m["k_tri"] = CN["tri"]
    m["k_perm"] = CN["perm"]
    sm = np.zeros((128, 8), np.float32)
    sm[:, 0:1] = CN["invr"]
    sm[:, 1:2] = CN["invd"]
    sm[:, 2:3] = CN["sgn"]
    sm[:, 3:7] = CN["zeta"]
    m["k_small"] = sm
    m["k_xirep"] = CN["xirep"]
    m["k_mask2"] = CN["mask2"]
    return m


def kernel(**inputs):
    CN = _consts()
    if "nc" not in _NC_CACHE:
        _NC_CACHE["nc"] = build(DEPTH)[0]
    nc = _NC_CACHE["nc"]
    shared = None
    in_maps = []
    for b in range(8):
        m = _prep_inputs(inputs, b, CN) if shared is None else None
        if shared is None:
            shared = m
            in_maps.append(m)
        else:
            mm_ = dict(shared)
            mm_["x"] = np.ascontiguousarray(np.asarray(inputs["x"][b], dtype=np.float32))
            mm_["c"] = np.ascontiguousarray(np.asarray(inputs["c"][b], dtype=np.float32).reshape(8, 128).T)
            mm_["pos"] = np.ascontiguousarray(np.asarray(inputs["positions"][b], dtype=np.int32).reshape(1, S))
            in_maps.append(mm_)
    res = run_bass_kernel_spmd(nc, in_maps, core_ids=list(range(8)))
    out = np.stack([np.asarray(res.results[b]["out"], dtype=np.float32) for b in range(8)], axis=0)
    return out
```

```python
import math
import numpy as np
import concourse.bass as bass
import concourse.mybir as mybir
from concourse.bass_utils import run_bass_kernel_spmd

F32 = mybir.dt.float32
BF16 = mybir.dt.bfloat16
I32 = mybir.dt.int32
AF = mybir.ActivationFunctionType
ALU = mybir.AluOpType

S = 2048
D = 1024
T = 512
NT = S // T
DFF = 4096
EPS = 1e-6
RH = 4
DEPTH = 4
NSLOT = 3
PG = 256

X0 = 0
U0 = 65536
SL0 = 131072
C0 = SL0 + NSLOT * 8192
WK0 = C0 + 6144
SB_BYTES = 207 * 1024
WK_BYTES = SB_BYTES - WK0


class Opd:
    __slots__ = ("ap", "keys")

    def __init__(self, ap, keys):
        self.ap = ap
        self.keys = keys


class Buf:
    def __init__(self, sbt, off, n, dt):
        self.esz = 4 if dt in (F32, I32) else 2
        assert off % 4 == 0
        self.off = off
        self.n = n
        self.dt = dt
        w0 = off // 4
        w1 = (off + n * self.esz + 3) // 4
        ap = sbt[:, w0:w1]
        if dt != F32:
            ap = ap.bitcast(dt)
        self.ap = ap

    def s(self, a=0, b=None, p0=0, p1=128):
        if b is None:
            b = self.n
        k0 = (self.off + a * self.esz) // PG
        k1 = (self.off + b * self.esz - 1) // PG
        return Opd(self.ap[p0:p1, a:b], [("sb", k) for k in range(k0, k1 + 1)])


class PBank:
    def __init__(self, pt, idx):
        self.ap = pt[:]
        self.idx = idx
        self.apb = pt[:].bitcast(BF16)

    def s(self, a=0, b=512, p0=0, p1=128):
        return Opd(self.ap[p0:p1, a:b], [("ps", self.idx)])

    def sb16(self, a=0, b=1024, p0=0, p1=128):
        return Opd(self.apb[p0:p1, a:b], [("ps", self.idx)])


class Sched:
    ENG = ("pe", "act", "dve", "pool", "sp")

    def __init__(self):
        self.ops = []
        self.lastw = {}
        self.readers = {}
        self.dma_cum = {}
        self.phase = "pro"

    def add(self, eng, fn, r=(), w=(), dma=None):
        deps = set()
        for o in r:
            for k in o.keys:
                lw = self.lastw.get(k)
                if lw is not None:
                    deps.add(lw)
                if k[0] == "ps":
                    rd = self.readers.get(k)
                    if rd:
                        deps.update(x for x in rd if self.ops[x]["eng"] != eng)
        for o in w:
            for k in o.keys:
                lw = self.lastw.get(k)
                if lw is not None:
                    deps.add(lw)
                rd = self.readers.get(k)
                if rd:
                    deps.update(rd)
        oid = len(self.ops)
        cdeps = {}
        ddeps = {}
        for d in deps:
            od = self.ops[d]
            if od["dma"] is not None:
                sk = od["dma"]
                ddeps[sk] = self.dma_cum[sk]
            else:
                if od["eng"] == "pe" and eng == "pe":
                    continue
                e = od["eng"]
                if e not in cdeps or cdeps[e] < d:
                    cdeps[e] = d
        if dma is not None:
            self.dma_cum[dma] = self.dma_cum.get(dma, 0) + 16
        self.ops.append(dict(eng=eng, fn=fn, cdeps=cdeps, ddeps=ddeps, dma=dma, sig=False, ph=self.phase))
        for o in r:
            for k in o.keys:
                self.readers.setdefault(k, set()).add(oid)
        for o in w:
            for k in o.keys:
                self.lastw[k] = oid
                self.readers[k] = set()
        return oid

    def emit(self, nc, sems, dsems, block):
        ops = self.ops
        for op in ops:
            for e, d in op["cdeps"].items():
                ops[d]["sig"] = True
        cnt = {e: 0 for e in self.ENG}
        for op in ops:
            if op["dma"] is None and op["sig"]:
                cnt[op["eng"]] += 1
                op["sigval"] = cnt[op["eng"]]
        self.sigcounts = dict(cnt)
        per = {e: [] for e in self.ENG}
        for op in ops:
            per[op["eng"]].append(op)

        def run(ename, eng):
            waited = {}
            for op in per[ename]:
                for e, d in op["cdeps"].items():
                    v = ops[d]["sigval"]
                    if waited.get(e, 0) < v:
                        eng.wait_ge(sems[e], v)
                        waited[e] = v
                for sk, v in op["ddeps"].items():
                    if waited.get(sk, 0) < v:
                        eng.wait_ge(dsems[sk], v)
                        waited[sk] = v
                ins = op["fn"](eng)
                if op["dma"] is not None:
                    ins.then_inc(dsems[op["dma"]], 16)
                elif op["sig"]:
                    ins.then_inc(sems[ename], 1)
            if ename == "sp":
                for sk, v in self.dma_cum.items():
                    if sk.startswith("out"):
                        eng.wait_ge(dsems[sk], v)

        @block.tensor
        def _(e):
            run("pe", e)

        @block.scalar
        def _(e):
            run("act", e)

        @block.vector
        def _(e):
            run("dve", e)

        @block.gpsimd
        def _(e):
            run("pool", e)

        @block.sync
        def _(e):
            run("sp", e)


def _consts():
    c = {}
    c["identf"] = np.eye(128, dtype=np.float32)
    tri = (np.arange(128)[None, :] >= np.arange(128)[:, None]).astype(np.float32)
    c["tri"] = tri
    perm = np.zeros((128, 128), np.float32)
    invd = np.zeros((128, 1), np.float32)
    sgn = np.ones((128, 1), np.float32)
    fr = (500000.0 ** (-np.arange(0, 16, 2, dtype=np.float32) / np.float32(16))).astype(np.float32)
    for p in range(128):
        dd = p % 64
        if dd < 8:
            perm[p + 8, p] = 1.0
            invd[p, 0] = fr[dd]
            sgn[p, 0] = -1.0
        elif dd < 16:
            perm[p - 8, p] = 1.0
            invd[p, 0] = fr[dd - 8]
            sgn[p, 0] = 1.0
    c["perm"] = perm
    c["invd"] = invd
    c["sgn"] = sgn
    c["invr"] = (10000.0 ** (-np.arange(0, 256, 2, dtype=np.float32) / np.float32(256))).astype(np.float32).reshape(128, 1)
    gam = 1.0 - 2.0 ** (-5.0 - np.arange(RH, dtype=np.float64))
    lg = np.log(gam)
    idx = np.arange(128, dtype=np.float64)
    xi = np.exp((idx + 1.0)[None, :] * lg[:, None])
    zeta = np.exp((127.0 - idx)[None, :] * lg[:, None])
    c["gC"] = [float(np.exp(128.0 * lg[h])) for h in range(RH)]
    xirep = np.tile(xi[:, None, :], (1, 4, 1)).reshape(RH * 512)
    c["xirep"] = np.broadcast_to(xirep[None, :], (128, RH * 512)).astype(np.float32).copy()
    c["zeta"] = (zeta.T / 16.0).astype(np.float32).copy()
    m2 = np.zeros((128, RH, 128), np.float64)
    for h in range(RH):
        m2[:, h, :] = tri * np.exp(-(idx + 1.0) * lg[h])[:, None] / 16.0
    c["mask2"] = m2.reshape(128, RH * 128).astype(np.float32)
    return c


TWO_PI = 2.0 * math.pi
CW1 = 6.28125
CW2 = TWO_PI - CW1
PI_LO = 3.1415925


def build(nlayers=DEPTH):
    CN = _consts()
    nc = bass.Bass("TRN2", target_bir_lowering=False)
    dram = {}

    def din(name, shape, dt=F32):
        dram[name] = nc.dram_tensor(name, list(shape), dt, kind="ExternalInput").ap()
        return dram[name]

    x_d = din("x", [S, D])
    c_d = din("c", [128, 8])
    pos_d = din("pos", [1, S], I32)
    ng_d = din("norm_g", [128, 16 * 8])
    adaw_d = din("ada_w", [DEPTH, D, 6 * D])
    adab_d = din("ada_b", [DEPTH, 6 * D])
    win_d = din("ret_w_in", [2, D, 6144])
    wout_d = din("ret_w_out", [2, 2048, D])
    kvg_d = din("kv_norm_g", [128, 8])
    kvaw_d = din("kv_ada_w", [D, 2 * D])
    kvab_d = din("kv_ada_b", [1, 2 * D])
    kvw_d = din("kv_w", [D, 2 * D])
    wq_d = din("diff_w_q", [2, D, D])
    wo_d = din("diff_w_o", [2, D, D])
    lam_d = din("diff_lam", [2, 256])
    sub_d = din("diff_subln_g", [128, 2])
    w1_d = din("mlp_w1", [DEPTH, D, DFF])
    w2_d = din("mlp_w2", [DEPTH, DFF, D])
    k_identf = din("k_identf", [128, 128])
    k_tri = din("k_tri", [128, 128])
    k_perm = din("k_perm", [128, 128])
    k_small = din("k_small", [128, 8])
    k_xirep = din("k_xirep", [128, RH * 512])
    k_mask2 = din("k_mask2", [128, RH * 128])
    out_d = nc.dram_tensor("out", [S, D], F32, kind="ExternalOutput").ap()

    sc = Sched()
    import contextlib
    es = contextlib.ExitStack()
    with es:
        sbt = es.enter_context(nc.sbuf_tensor("SB", [128, SB_BYTES // 4], F32))
        psb = [PBank(es.enter_context(nc.psum_tensor(f"ps{i}", [128, 512], F32)), i) for i in range(8)]
        sems = {e: es.enter_context(nc.semaphore("s_" + e)) for e in Sched.ENG}
        dkeys = [f"slot{i}" for i in range(NSLOT)] + ["misc", "stage0", "stage1", "pos", "bias0", "bias1", "out0", "out1"]
        dsems = {k: es.enter_context(nc.semaphore("d_" + k)) for k in dkeys}
        block = es.enter_context(nc.Block())

        def B(off, n, dt=F32):
            return Buf(sbt, off, n, dt)

        xT = [[B(X0 + (fc * S + t * T) * 4, T) for t in range(NT)] for fc in range(8)]
        slots = [B(SL0 + i * 8192, 4096, BF16) for i in range(NSLOT)]
        identb = B(C0 + 0, 128, BF16)
        onesb = B(C0 + 256, 128, BF16)
        trib = B(C0 + 512, 128, BF16)
        permb = B(C0 + 768, 128, BF16)
        identf = B(C0 + 1024, 128)
        onef = B(C0 + 1536, 4)
        ksm = B(C0 + 1552, 8)
        cT = B(C0 + 1584, 8, BF16)
        c32 = B(C0 + 1600, 8)
        MODS = [dict(modcol=B(C0 + 1664, 48), g0sc=B(C0 + 2368, 8), gga=B(C0 + 2400, 8), g2sc=B(C0 + 2432, 8),
                     ggm=B(C0 + 2464, 8)),
                dict(modcol=B(C0 + 5376, 48), g0sc=B(C0 + 5568, 8), gga=B(C0 + 5600, 8), g2sc=B(C0 + 5632, 8),
                     ggm=B(C0 + 5664, 8))]
        ngb = B(C0 + 1856, 128)
        kvgb = B(C0 + 2560, 8)
        kvgsc = B(C0 + 2592, 8)
        subg = B(C0 + 2720, 2)
        subsc = B(C0 + 2728, 1)
        neglam = B(C0 + 2732, 1)
        lamb = B(C0 + 3072, 256)
        lamt = B(C0 + 4096, 128)
        lams = B(C0 + 4608, 4)
        stg32 = B(C0 + 4864, 128)
        cosT = B(U0, S)
        sinT = B(U0 + 8192, S)
        xirep = B(U0 + 16384, RH * 512)
        R32 = [[B(U0 + 24576 + (h * 2 + dc) * 2048, 512) for dc in range(2)] for h in range(RH)]
        Rbf = [[B(U0 + 40960 + (h * 2 + dc) * 1024, 512, BF16) for dc in range(2)] for h in range(RH)]
        mask2 = B(U0 + 49152, RH * 128)
        o_sb = [B(U0 + 51200 + i * 2048, 512) for i in range(2)]
        osq = [B(U0 + 55296 + i * 1024, 512, BF16) for i in range(2)]
        PTb = [B(U0 + 57344 + i * 256, 128, BF16) for i in range(2)]
        ortmp = [B(U0 + 57856, 128), B(U0 + 63488 + 512, 128)]
        orstd = [B(U0 + 58368 + i * 512, 128) for i in range(2)]
        osd = B(U0 + 59392, 128)
        kTsh = [B(U0 + i * 4096, S, BF16) for i in range(8)]
        vsh = [B(U0 + 32768 + kb * 2048, 1024, BF16) for kb in range(16)]

        def mm(out, lhsT, rhs, start, stop):
            sc.add("pe", lambda e: e.matmul(out.ap, lhsT=lhsT.ap, rhs=rhs.ap, start=start, stop=stop),
                   r=[lhsT, rhs], w=[out])

        def tr(out, in_, ident):
            sc.add("pe", lambda e: e.transpose(out.ap, in_.ap, ident.ap), r=[in_, ident], w=[out])

        def act(out, in_, func, scale=None, bias=None, eng="act"):
            rr = [in_]
            kw = {}
            if scale is not None:
                if isinstance(scale, Opd):
                    rr.append(scale)
                    kw["scale"] = scale.ap
                else:
                    kw["scale"] = scale
            if bias is not None:
                if isinstance(bias, Opd):
                    rr.append(bias)
                    kw["bias"] = bias.ap
                else:
                    kw["bias"] = bias
            sc.add("act", lambda e: e.activation(out=out.ap, in_=in_.ap, func=func, **kw), r=rr, w=[out])

        def tt(out, in0, in1, op, eng="dve"):
            sc.add(eng, lambda e: e.tensor_tensor(out=out.ap, in0=in0.ap, in1=in1.ap, op=op), r=[in0, in1], w=[out])

        def ts(out, in0, s1, s2, op0, op1=None, eng="dve"):
            rr = [in0]
            a1 = s1
            if isinstance(s1, Opd):
                rr.append(s1)
                a1 = s1.ap
            a2 = s2
            if isinstance(s2, Opd):
                rr.append(s2)
                a2 = s2.ap
            if op1 is None:
                sc.add(eng, lambda e: e.tensor_single_scalar(out=out.ap, in_=in0.ap, scalar=a1, op=op0), r=rr, w=[out])
            else:
                sc.add(eng, lambda e: e.tensor_scalar(out=out.ap, in0=in0.ap, scalar1=a1, scalar2=a2, op0=op0, op1=op1),
                       r=rr, w=[out])

        def stt(out, in0, scalar, in1, op0, op1, eng="dve"):
            rr = [in0, in1]
            a = scalar
            if isinstance(scalar, Opd):
                rr.append(scalar)
                a = scalar.ap
            sc.add(eng, lambda e: e.scalar_tensor_tensor(out=out.ap, in0=in0.ap, scalar=a, in1=in1.ap, op0=op0, op1=op1),
                   r=rr, w=[out])

        def cp(out, in_, eng="dve"):
            if eng == "act":
                sc.add("act", lambda e: e.activation(out=out.ap, in_=in_.ap, func=AF.Identity), r=[in_], w=[out])
            else:
                sc.add(eng, lambda e: e.tensor_copy(out=out.ap, in_=in_.ap), r=[in_], w=[out])

        def recip(out, in_):
            sc.add("dve", lambda e: e.reciprocal(out=out.ap, in_=in_.ap), r=[in_], w=[out])

        def memset(out, val, eng="dve"):
            sc.add(eng, lambda e: e.memset(out.ap, val), w=[out])

        def dma(queue, semkey, out_ap, in_ap, r=(), w=()):
            sc.add(queue, lambda e: e.dma_start(out=out_ap, in_=in_ap), r=list(r), w=list(w), dma=semkey)

        piece_ctr = [0]

        def load_piece(srcs):
            i = piece_ctr[0]
            piece_ctr[0] += 1
            sl = slots[i % NSLOT]
            for (src, kcn, W, off, n) in srcs:
                dst = sl.ap.rearrange("p (k n) -> p k n", k=kcn)[:, :, off:off + n]
                dma("pool", f"slot{i % NSLOT}", dst, src, w=[sl.s(k * W + off, k * W + off + n) for k in range(kcn)])
            return sl

        def wview(d2, kcn):
            return d2.rearrange("(k p) n -> p k n", p=128)

        def small_load(dst, src_ap):
            dma("sp", "misc", dst.ap, src_ap, w=[dst.s()])

        small_load(identf, k_identf)
        small_load(ksm, k_small)
        small_load(c32, c_d)
        small_load(ngb, ng_d)
        small_load(kvgb, kvg_d)
        small_load(subg, sub_d)
        small_load(xirep, k_xirep)
        small_load(mask2, k_mask2)
        small_load(stg32, k_tri)
        cp(trib.s(), stg32.s())
        small_load(stg32, k_perm)
        cp(permb.s(), stg32.s())
        cp(identb.s(), identf.s())
        memset(onesb.s(), 1.0)
        memset(onef.s(), 1.0)
        act(cT.s(), c32.s(), AF.Silu)
        invr = ksm.s(0, 1)
        invd = ksm.s(1, 2)
        sgn = ksm.s(2, 3)

        import os
        SKIP = set(os.environ.get("KSKIP", "").split(","))
        stage = [B(WK0 + i * 4096, 1024) for i in range(2)]
        for tc in range(16 if "xload" not in SKIP else 0):
            st = stage[tc % 2]
            dma("sp", f"stage{tc % 2}", st.ap, x_d[tc * 128:(tc + 1) * 128, :], w=[st.s()])
            t, cc = tc // 4, tc % 4
            for half in range(2):
                pb = psb[(tc * 2 + half) % 4]
                for q in range(4):
                    fc = half * 4 + q
                    if "xtr" not in SKIP:
                        tr(pb.s(q * 128, (q + 1) * 128), st.s(fc * 128, (fc + 1) * 128), identf.s())
                    else:
                        mm(pb.s(q * 128, (q + 1) * 128), st.s(fc * 128, (fc + 1) * 128), identf.s(), True, True)
                for q in range(4):
                    fc = half * 4 + q
                    if "xcp" not in SKIP:
                        cp(xT[fc][t].s(cc * 128, (cc + 1) * 128), pb.s(q * 128, (q + 1) * 128),
                           eng="dve" if "xdve" in SKIP else ("act" if ("xact" in SKIP or q % 2) else "dve"))

        def sincos(posf, inv, ncol, cos_out, sin_out, tmpa, tmpb, tmpi, sin_scale=None):
            ang = tmpa
            ts(ang.s(0, ncol), posf.s(0, ncol), inv, None, ALU.mult)
            ts(tmpb.s(0, ncol), ang.s(0, ncol), 1.0 / TWO_PI, None, ALU.mult)
            cp(tmpi.s(0, ncol), tmpb.s(0, ncol))
            cp(tmpb.s(0, ncol), tmpi.s(0, ncol))
            stt(ang.s(0, ncol), tmpb.s(0, ncol), -CW1, ang.s(0, ncol), ALU.mult, ALU.add)
            stt(ang.s(0, ncol), tmpb.s(0, ncol), -CW2, ang.s(0, ncol), ALU.mult, ALU.add)
            ts(tmpb.s(0, ncol), ang.s(0, ncol), -PI_LO, PI_LO, ALU.max, ALU.min)
            act(sin_out.s(0, ncol), tmpb.s(0, ncol), AF.Sin, scale=sin_scale)
            ts(ang.s(0, ncol), ang.s(0, ncol), math.pi / 2.0, None, ALU.add)
            ts(tmpb.s(0, ncol), ang.s(0, ncol), math.pi, TWO_PI, ALU.is_gt, ALU.mult)
            tt(ang.s(0, ncol), ang.s(0, ncol), tmpb.s(0, ncol), ALU.subtract)
            ts(tmpb.s(0, ncol), ang.s(0, ncol), -PI_LO, PI_LO, ALU.max, ALU.min)
            act(cos_out.s(0, ncol), tmpb.s(0, ncol), AF.Sin)

        posi = B(WK0 + 8192, S, I32)
        posf = B(WK0 + 16384, S)
        tA = B(WK0 + 24576, S)
        tB = B(WK0 + 32768, S)
        if "tables" not in SKIP:
            dma("sp", "pos", posi.ap, pos_d.partition_broadcast(128), w=[posi.s()])
            cp(posf.s(), posi.s())
            sincos(posf, invr, S, cosT, sinT, tA, tB, posi)

        def compute_mod(wd, bd_row, ncols, dst):
            sc.phase = "mod"
            MS0 = WK0 + 47104
            rowc = [B(MS0 + i * 1024, 256) for i in range(2)]
            biasc = B(MS0 + 2048, 256)
            pcol = psb[7]
            wv = wview(wd, 8)
            for n in range(ncols // 512):
                sl = load_piece([(wv[:, :, n * 512:(n + 1) * 512], 8, 512, 0, 512)])
                for hh in range(2):
                    c0 = n * 512 + hh * 256
                    dma("sp", "bias0", biasc.ap[0:1, :], bd_row[0:1, c0:c0 + 256], w=[biasc.s(p0=0, p1=1)])
                    pr = psb[hh]
                    for kc in range(8):
                        mm(pr.s(0, 256, 0, 1), cT.s(kc, kc + 1), sl.s(kc * 512 + hh * 256, kc * 512 + hh * 256 + 256),
                           kc == 0, kc == 7)
                    rc_ = rowc[hh]
                    tt(rc_.s(p0=0, p1=1), pr.s(0, 256, 0, 1), biasc.s(p0=0, p1=1), ALU.add)
                    for j in range(2):
                        col = c0 // 128 + j
                        mm(Opd(pcol.ap[:, col:col + 1], pcol.s(0, 128).keys), rc_.s(j * 128, (j + 1) * 128, 0, 1),
                           onef.s(0, 1, 0, 1), True, True)
            cp(dst.s(0, ncols // 128), pcol.s(0, ncols // 128))

        def layer_mod(l):
            P = MODS[l % 2]
            modcol = P["modcol"]
            compute_mod(adaw_d[l], adab_d[l:l + 1, :], 6 * D, modcol)
            base = l * 32
            stt(P["g0sc"].s(), modcol.s(8, 16), 1.0, ngb.s(base + 0, base + 8), ALU.add, ALU.mult)
            stt(P["gga"].s(), modcol.s(16, 24), 1.0, ngb.s(base + 8, base + 16), ALU.add, ALU.mult)
            stt(P["g2sc"].s(), modcol.s(32, 40), 1.0, ngb.s(base + 16, base + 24), ALU.add, ALU.mult)
            stt(P["ggm"].s(), modcol.s(40, 48), 1.0, ngb.s(base + 24, base + 32), ALU.add, ALU.mult)


        nsq = [B(WK0 + i * 1024, T, BF16) for i in range(2)]
        nsd = B(WK0 + 2048, T)
        nrstd = B(WK0 + 4096, T)
        ntmp = [B(WK0 + 6144 + i * 2048, T) for i in range(2)]
        HT0 = WK0 + 10240
        hT = [B(HT0 + fc * 1024, T, BF16) for fc in range(8)]
        WK1 = WK0 + 18432

        def rms_stats(srcs, pbank, dnorm):
            n = len(srcs)
            for i, s_ in enumerate(srcs):
                q = nsq[i % 2]
                act(q.s(), s_, AF.Square)
                mm(pbank.s(), onesb.s(), q.s(), i == 0, i == n - 1)
            act(nsd.s(), pbank.s(), AF.Sqrt, scale=1.0 / dnorm, bias=EPS)
            recip(nrstd.s(), nsd.s())

        def make_h(t, gsc, shv, sh0):
            sc.phase = "make_h"
            rms_stats([xT[fc][t].s() for fc in range(8)], psb[3], D)
            for fc in range(8):
                tm = ntmp[fc % 2]
                stt(tm.s(), xT[fc][t].s(), gsc.s(fc, fc + 1), nrstd.s(), ALU.mult, ALU.mult)
                act(hT[fc].s(), tm.s(), AF.Identity, bias=shv.s(sh0 + fc, sh0 + fc + 1))

        def resid_update(t, ybuf, gg):
            sc.phase = "resid"
            rms_stats([ybuf[oc].s() for oc in range(8)], psb[3], D)
            for oc in range(8):
                tm = ntmp[oc % 2]
                stt(tm.s(), ybuf[oc].s(), gg.s(oc, oc + 1), nrstd.s(), ALU.mult, ALU.mult)
                tt(xT[oc][t].s(), xT[oc][t].s(), tm.s(), ALU.add)

        def mlp_tile(l, t):
            P = MODS[l % 2]
            ggm = P["ggm"]
            make_h(t, P["g2sc"], P["modcol"], 24)
            uT = [B(WK1 + i * 1024, T, BF16) for i in range(8)]
            yb = [B(WK1 + 8192 + oc * 2048, T) for oc in range(8)]
            rtmp = [B(WK1 + 24576 + i * 2048, T) for i in range(2)]
            w1v = wview(w1_d[l], 8)
            for qf in range(4):
                sc.phase = "mlp_w1"
                for half in range(2):
                    c0 = qf * 1024 + half * 512
                    sl = load_piece([(w1v[:, :, c0:c0 + 512], 8, 512, 0, 512)])
                    for j in range(4):
                        pb = psb[(half * 4 + j) % 3]
                        for kc in range(8):
                            mm(pb.s(), sl.s(kc * 512 + j * 128, kc * 512 + (j + 1) * 128), hT[kc].s(), kc == 0, kc == 7)
                        rt = rtmp[j % 2]
                        act(rt.s(), pb.s(), AF.Relu)
                        act(uT[half * 4 + j].s(), rt.s(), AF.Square)
                w2v = w2_d[l][qf * 1024:(qf + 1) * 1024, :].rearrange("(k p) n -> p k n", p=128)
                sc.phase = "mlp_w2"
                for half in range(2):
                    sl = load_piece([(w2v[:, :, half * 512:(half + 1) * 512], 8, 512, 0, 512)])
                    for j in range(4):
                        oc = half * 4 + j
                        pb = psb[4 + (oc % 3)]
                        for fk in range(8):
                            mm(pb.s(), sl.s(fk * 512 + j * 128, fk * 512 + (j + 1) * 128), uT[fk].s(), fk == 0, fk == 7)
                        if qf == 0:
                            cp(yb[oc].s(), pb.s())
                        else:
                            tt(yb[oc].s(), yb[oc].s(), pb.s(), ALU.add)
            resid_update(t, yb, ggm)

        def ret_tile(l, t):
            P = MODS[l % 2]
            gga = P["gga"]
            make_h(t, P["g0sc"], P["modcol"], 0)
            qT_ = [B(WK1 + dc * 1024, T, BF16) for dc in range(2)]
            kT_ = [B(WK1 + 2048 + dc * 1024, T, BF16) for dc in range(2)]
            ktok = B(WK1 + 4096, 1024, BF16)
            vb = [B(WK1 + 6144 + c * 1024, 512, BF16) for c in range(4)]
            vz = [B(U0 + 59904 + c * 1024, 512, BF16) for c in range(4)]
            sg = [B(WK1 + 10240 + ec * 1024, T, BF16) for ec in range(4)]
            rp = ntmp
            GT0 = WK1 + 14336
            gatedT = [B(GT0 + i * 1024, T, BF16) for i in range(16)]
            winv = wview(win_d[l], 8)
            cs = cosT.s(t * T, (t + 1) * T)
            sn = sinT.s(t * T, (t + 1) * T)
            for hd in range(RH):
                sc.phase = "ret_qk"
                sl = load_piece([(winv[:, :, hd * 256:(hd + 1) * 256], 8, 512, 0, 256),
                                 (winv[:, :, 1024 + hd * 256:1024 + (hd + 1) * 256], 8, 512, 256, 256)])
                for which, dst in ((0, qT_), (1, kT_)):
                    pbs = [psb[0], psb[1]]
                    for dc in range(2):
                        c0 = which * 256 + dc * 128
                        for kc in range(8):
                            mm(pbs[dc].s(), sl.s(kc * 512 + c0, kc * 512 + c0 + 128), hT[kc].s(), kc == 0, kc == 7)
                    x1, x2 = pbs[0].s(), pbs[1].s()
                    tt(rp[0].s(), x1, cs, ALU.mult)
                    tt(rp[1].s(), x2, sn, ALU.mult)
                    tt(dst[0].s(), rp[0].s(), rp[1].s(), ALU.subtract)
                    tt(rp[0].s(), x2, cs, ALU.mult)
                    tt(rp[1].s(), x1, sn, ALU.mult)
                    tt(dst[1].s(), rp[0].s(), rp[1].s(), ALU.add)
                sc.phase = "ret_ktr"
                pbt = psb[2]
                for c in range(4):
                    for dc in range(2):
                        o0 = (c * 2 + dc) * 128
                        tr(pbt.sb16(o0, o0 + 128), kT_[dc].s(c * 128, (c + 1) * 128), identb.s())
                cp(ktok.s(), pbt.sb16(), eng="act")
                sc.phase = "ret_v"
                sl = load_piece([(winv[:, :, 2048 + hd * 512:2048 + (hd + 1) * 512], 8, 512, 0, 512)])
                for c in range(4):
                    pb = psb[c % 2]
                    for kc in range(8):
                        mm(pb.s(), hT[kc].s(c * 128, (c + 1) * 128), sl.s(kc * 512, (kc + 1) * 512), kc == 0, kc == 7)
                    cp(vb[c].s(), pb.s(), eng="act")
                    act(vz[c].s(), pb.s(), AF.Copy, scale=ksm.s(3 + hd, 4 + hd))
                sc.phase = "ret_g"
                sl = load_piece([(winv[:, :, 4096 + hd * 512:4096 + (hd + 1) * 512], 8, 512, 0, 512)])
                for ec in range(4):
                    pb = psb[ec % 2]
                    for kc in range(8):
                        mm(pb.s(), sl.s(kc * 512 + ec * 128, kc * 512 + (ec + 1) * 128), hT[kc].s(), kc == 0, kc == 7)
                    act(sg[ec].s(), pb.s(), AF.Silu)
                sc.phase = "ret_chunk"
                gC = CN["gC"][hd]

                def finish(c, n):
                    csl = (c * 128, (c + 1) * 128)
                    ob = o_sb[n % 2]
                    oq = osq[n % 2]
                    pst = psb[n % 2].s(0, 128)
                    for ec in range(4):
                        mm(pst, onesb.s(), oq.s(ec * 128, (ec + 1) * 128), ec == 0, ec == 3)
                    act(osd.s(), pst, AF.Sqrt, scale=1.0 / 512.0, bias=EPS)
                    rs = orstd[n % 2]
                    recip(rs.s(), osd.s())
                    for ec in range(4):
                        tt(ortmp[ec % 2].s(), ob.s(ec * 128, (ec + 1) * 128), rs.s(), ALU.mult)
                        tt(gatedT[hd * 4 + ec].s(*csl), ortmp[ec % 2].s(), sg[ec].s(*csl), ALU.mult)

                for c in range(4):
                    n = t * 4 + c
                    csl = (c * 128, (c + 1) * 128)
                    ps_s = psb[2 + (n % 2)].s(0, 128)
                    for dc in range(2):
                        mm(ps_s, kT_[dc].s(*csl), qT_[dc].s(*csl), dc == 0, dc == 1)
                    PT = PTb[n % 2]
                    tt(PT.s(), ps_s, mask2.s(hd * 128, (hd + 1) * 128), ALU.mult)
                    if n < 15:
                        for dc in range(2):
                            pa = psb[6 + dc]
                            mm(pa.s(), ktok.s((c * 2 + dc) * 128, (c * 2 + dc + 1) * 128), vz[c].s(), True, True)
                    po = psb[4 + (n % 2)]
                    for ec in range(4):
                        osl = po.s(ec * 128, (ec + 1) * 128)
                        mm(osl, vb[c].s(ec * 128, (ec + 1) * 128), PT.s(), True, n == 0)
                        if n > 0:
                            for dc in range(2):
                                mm(osl, Rbf[hd][dc].s(ec * 128, (ec + 1) * 128), qT_[dc].s(*csl), False, dc == 1)
                    if n < 15:
                        for dc in range(2):
                            pa = psb[6 + dc]
                            if n == 0:
                                cp(R32[hd][dc].s(), pa.s())
                            else:
                                stt(R32[hd][dc].s(), R32[hd][dc].s(), gC, pa.s(), ALU.mult, ALU.add)
                            cp(Rbf[hd][dc].s(), R32[hd][dc].s(), eng="act")
                    if c > 0:
                        finish(c - 1, n - 1)
                    ob = o_sb[n % 2]
                    tt(ob.s(), po.s(), xirep.s(hd * 512, (hd + 1) * 512), ALU.mult)
                    act(osq[n % 2].s(), ob.s(), AF.Square)
                finish(3, t * 4 + 3)
            sc.phase = "ret_wout"
            yb = [B(HT0 + oc * 2048, T) for oc in range(8)]
            woutv = wout_d[l].rearrange("(k p) n -> p k n", p=128)
            for pi_ in range(4):
                sl = load_piece([(woutv[:, :, pi_ * 256:(pi_ + 1) * 256], 16, 256, 0, 256)])
                for j in range(2):
                    oc = pi_ * 2 + j
                    pb = psb[oc % 3]
                    for kc in range(16):
                        mm(pb.s(), sl.s(kc * 256 + j * 128, kc * 256 + (j + 1) * 128), gatedT[kc].s(), kc == 0, kc == 15)
                    cp(yb[oc].s(), pb.s())
            resid_update(t, yb, gga)

        DT0 = WK1
        dC = B(DT0, T)
        dS = B(DT0 + 2048, T)
        dposi = B(DT0 + 4096, T, I32)
        dposf = B(DT0 + 6144, T)
        dtA = B(DT0 + 8192, T)
        dtB = B(DT0 + 10240, T)
        qraw = [B(DT0 + 4096 + i * 1024, T, BF16) for i in range(2)]
        drp = [B(DT0 + 8192 + i * 2048, T) for i in range(2)]

        def diff_tables(t):
            sc.phase = "dtables"
            dma("sp", "pos", dposi.ap, pos_d[0:1, t * T:(t + 1) * T].partition_broadcast(128), w=[dposi.s()])
            cp(dposf.s(), dposi.s())
            sincos(dposf, invd, T, dC, dS, dtA, dtB, dposi, sin_scale=sgn)

        def rope_diff(pb, dst, idx):
            qr = qraw[idx % 2]
            cp(qr.s(), pb.s(), eng="act")
            p2 = psb[3]
            mm(p2.s(), permb.s(), qr.s(), True, True)
            tt(drp[0].s(), pb.s(), dC.s(), ALU.mult)
            tt(drp[1].s(), p2.s(), dS.s(), ALU.mult)
            tt(dst, drp[0].s(), drp[1].s(), ALU.add)

        def kv_tile(t):
            make_h(t, kvgsc, kvmod, 0)
            diff_tables(t)
            kvv = wview(kvw_d, 8)
            sc.phase = "kv"
            for half in range(2):
                sl = load_piece([(kvv[:, :, half * 512:(half + 1) * 512], 8, 512, 0, 512)])
                for j in range(4):
                    i = half * 4 + j
                    pb = psb[i % 3]
                    for kc in range(8):
                        mm(pb.s(), sl.s(kc * 512 + j * 128, kc * 512 + (j + 1) * 128), hT[kc].s(), kc == 0, kc == 7)
                    rope_diff(pb, kTsh[i].s(t * T, (t + 1) * T), i)
            for half in range(2):
                sl = load_piece([(kvv[:, :, 1024 + half * 512:1024 + (half + 1) * 512], 8, 512, 0, 512)])
                for c in range(4):
                    pb = psb[c % 3]
                    for kc in range(8):
                        mm(pb.s(), hT[kc].s(c * 128, (c + 1) * 128), sl.s(kc * 512, (kc + 1) * 512), kc == 0, kc == 7)
                    cp(vsh[t * 4 + c].s(half * 512, (half + 1) * 512), pb.s(), eng="act" if c % 2 else "dve")

        def diff_tile(l, t):
            j_ = l - 2
            P = MODS[l % 2]
            gga = P["gga"]
            make_h(t, P["g0sc"], P["modcol"], 0)
            diff_tables(t)
            QT0 = DT0 + 12288
            qT_ = [B(QT0 + i * 1024, T, BF16) for i in range(8)]
            OT_ = [B(QT0 + 8192 + i * 1024, T, BF16) for i in range(8)]
            E0 = DT0
            Eb = [B(E0 + i * 1024, T, BF16) for i in range(4)]
            Zr = [B(E0 + 4096 + i * 2048, T) for i in range(2)]
            t01 = [B(E0 + 8192 + i * 2048, T) for i in range(2)]
            ofp = ntmp[0]
            wqv = wview(wq_d[j_], 8)
            sc.phase = "diff_q"
            for half in range(2):
                sl = load_piece([(wqv[:, :, half * 512:(half + 1) * 512], 8, 512, 0, 512)])
                for j in range(4):
                    i = half * 4 + j
                    pb = psb[i % 3]
                    for kc in range(8):
                        mm(pb.s(), sl.s(kc * 512 + j * 128, kc * 512 + (j + 1) * 128), hT[kc].s(), kc == 0, kc == 7)
                    rope_diff(pb, qT_[i].s(), i)
            nkb = 4 * t + 4
            sc.phase = "diff_attn"
            Ob = [psb[4], psb[5]]
            Zb = [psb[6], psb[7]]
            units = [(i, kb, a) for i in range(8) for kb in range(nkb) for a in range(2)]

            def emitS(u):
                i, kb, a = units[u]
                r = kb - 4 * t
                q0 = 0 if r < 0 else r * 128
                pS = psb[u % 3]
                Et = Eb[u % 4]
                mm(pS.s(q0, T), kTsh[i].s(kb * 128, (kb + 1) * 128, 64 * a, 64 * a + 64),
                   qT_[i].s(q0, T, 64 * a, 64 * a + 64), True, True)
                act(Et.s(q0, T), pS.s(q0, T), AF.Exp, scale=0.125)
                if r >= 0:
                    tt(Et.s(q0, q0 + 128), Et.s(q0, q0 + 128), trib.s(), ALU.mult)

            def emitPV(u):
                i, kb, a = units[u]
                r = kb - 4 * t
                q0 = 0 if r < 0 else r * 128
                Et = Eb[u % 4]
                mm(Ob[a].s(q0, T), vsh[kb].s(i * 128, (i + 1) * 128), Et.s(q0, T), kb == 0, kb == nkb - 1)
                mm(Zb[a].s(q0, T), onesb.s(), Et.s(q0, T), kb == 0, kb == nkb - 1)
                if kb == nkb - 1 and a == 1:
                    combine(i)

            def combine(i):
                for a in range(2):
                    recip(Zr[a].s(), Zb[a].s())
                    tt(t01[a].s(), Ob[a].s(), Zr[a].s(), ALU.mult)
                stt(ofp.s(), t01[1].s(), neglam.s(), t01[0].s(), ALU.mult, ALU.add)
                q = nsq[i % 2]
                act(q.s(), ofp.s(), AF.Square)
                pst = psb[3]
                mm(pst.s(), onesb.s(), q.s(), True, True)
                act(nsd.s(), pst.s(), AF.Sqrt, scale=1.0 / 128.0, bias=EPS)
                recip(nrstd.s(), nsd.s())
                stt(OT_[i].s(), ofp.s(), subsc.s(), nrstd.s(), ALU.mult, ALU.mult)

            NU = len(units)
            LOOK = 2
            for u in range(min(LOOK, NU)):
                emitS(u)
            for u in range(NU):
                emitPV(u)
                if u + LOOK < NU:
                    emitS(u + LOOK)
            Y0 = DT0
            yb = [B(Y0 + oc * 2048, T) for oc in range(8)]
            wov = wview(wo_d[j_], 8)
            sc.phase = "diff_wo"
            for half in range(2):
                sl = load_piece([(wov[:, :, half * 512:(half + 1) * 512], 8, 512, 0, 512)])
                for j in range(4):
                    oc = half * 4 + j
                    pb = psb[oc % 3]
                    for kc in range(8):
                        mm(pb.s(), sl.s(kc * 512 + j * 128, kc * 512 + (j + 1) * 128), OT_[kc].s(), kc == 0, kc == 7)
                    cp(yb[oc].s(), pb.s())
            resid_update(t, yb, gga)

        kvmod = B(C0 + 2624, 16)
        layer_mod(0)
        for l in range(min(nlayers, 2)):
            for t in range(NT):
                ret_tile(l, t)
                if t == 1 and l + 1 < nlayers:
                    layer_mod(l + 1)
                mlp_tile(l, t)
        if nlayers > 2:
            compute_mod(kvaw_d, kvab_d, 2 * D, kvmod)
            stt(kvgsc.s(), kvmod.s(8, 16), 1.0, kvgb.s(), ALU.add, ALU.mult)
            for t in range(NT):
                kv_tile(t)
            for l in range(2, nlayers):
                j_ = l - 2
                lam_init = 0.8 - 0.6 * math.exp(-0.3 * l)
                dma("sp", "misc", lamb.ap, lam_d[j_:j_ + 1, :].partition_broadcast(128), w=[lamb.s()])
                tt(lamt.s(0, 64), lamb.s(0, 64), lamb.s(64, 128), ALU.mult)
                tt(lamt.s(64, 128), lamb.s(128, 192), lamb.s(192, 256), ALU.mult)
                sc.add("dve", lambda e: e.reduce_sum(out=lams.ap[:, 0:1], in_=lamt.ap[:, 0:64], axis=mybir.AxisListType.X),
                       r=[lamt.s()], w=[lams.s()])
                sc.add("dve", lambda e: e.reduce_sum(out=lams.ap[:, 1:2], in_=lamt.ap[:, 64:128], axis=mybir.AxisListType.X),
                       r=[lamt.s()], w=[lams.s()])
                act(lams.s(2, 4), lams.s(0, 2), AF.Exp)
                tt(neglam.s(), lams.s(3, 4), lams.s(2, 3), ALU.subtract)
                ts(neglam.s(), neglam.s(), -lam_init, None, ALU.add)
                ts(subsc.s(), subg.s(j_, j_ + 1), 1.0 - lam_init, None, ALU.mult)
                for t in range(NT):
                    diff_tile(l, t)
                    if t == 1 and l + 1 < nlayers:
                        layer_mod(l + 1)
                    mlp_tile(l, t)

        sc.phase = "out"
        for tc in range(16):
            st = stage[tc % 2]
            t, cc = tc // 4, tc % 4
            for half in range(2):
                pb = psb[(tc * 2 + half) % 4]
                for q in range(4):
                    fc = half * 4 + q
                    tr(pb.s(q * 128, (q + 1) * 128), xT[fc][t].s(cc * 128, (cc + 1) * 128), identf.s())
                cp(st.s(half * 512, (half + 1) * 512), pb.s(), eng="act" if half else "dve")
            dma("sp", f"out{tc % 2}", out_d[tc * 128:(tc + 1) * 128, :], st.ap, r=[st.s()])

        sc.emit(nc, sems, dsems, block)
    return nc, sc


_NC_CACHE = {}


def _prep_inputs(inputs, b, CN):
    f = lambda a: np.ascontiguousarray(np.asarray(a, dtype=np.float32))
    m = {}
    m["x"] = f(inputs["x"][b])
    m["c"] = f(np.asarray(inputs["c"][b]).reshape(8, 128).T)
    m["pos"] = np.ascontiguousarray(np.asarray(inputs["positions"][b], dtype=np.int32).reshape(1, S))
    ng = np.asarray(inputs["norm_g"], dtype=np.float32).reshape(4, 4, 8, 128)
    m["norm_g"] = f(ng.transpose(3, 0, 1, 2).reshape(128, 128))
    m["ada_w"] = f(inputs["ada_w"])
    m["ada_b"] = f(inputs["ada_b"])
    m["ret_w_in"] = f(inputs["ret_w_in"])
    m["ret_w_out"] = f(inputs["ret_w_out"])
    m["kv_norm_g"] = f(np.asarray(inputs["kv_norm_g"]).reshape(8, 128).T)
    m["kv_ada_w"] = f(inputs["kv_ada_w"])
    m["kv_ada_b"] = f(np.asarray(inputs["kv_ada_b"]).reshape(1, 2 * D))
    m["kv_w"] = f(inputs["kv_w"])
    m["diff_w_q"] = f(inputs["diff_w_q"])
    m["diff_w_o"] = f(inputs["diff_w_o"])
    m["diff_lam"] = f(np.asarray(inputs["diff_lam"]).reshape(2, 256))
    m["diff_subln_g"] = f(np.asarray(inputs["diff_subln_g"]).T)
    m["mlp_w1"] = f(inputs["mlp_w1"])
    m["mlp_w2"] = f(inputs["mlp_w2"])
    m["k_identf"] = CN["identf"]
    m["k_tri"] = CN["tri"]
    m["k_perm"] = CN["perm"]
    sm = np.zeros((128, 8), np.float32)
    sm[:, 0:1] = CN["invr"]
    sm[:, 1:2] = CN["invd"]
    sm[:, 2:3] = CN["sgn"]
    sm[:, 3:7] = CN["zeta"]
    m["k_small"] = sm
    m["k_xirep"] = CN["xirep"]
    m["k_mask2"] = CN["mask2"]
    return m


def kernel(**inputs):
    CN = _consts()
    if "nc" not in _NC_CACHE:
        _NC_CACHE["nc"] = build(DEPTH)[0]
    nc = _NC_CACHE["nc"]
    shared = None
    in_maps = []
    for b in range(8):
        m = _prep_inputs(inputs, b, CN) if shared is None else None
        if shared is None:
            shared = m
            in_maps.append(m)
        else:
            mm_ = dict(shared)
            mm_["x"] = np.ascontiguousarray(np.asarray(inputs["x"][b], dtype=np.float32))
            mm_["c"] = np.ascontiguousarray(np.asarray(inputs["c"][b], dtype=np.float32).reshape(8, 128).T)
            mm_["pos"] = np.ascontiguousarray(np.asarray(inputs["positions"][b], dtype=np.int32).reshape(1, S))
            in_maps.append(mm_)
    res = run_bass_kernel_spmd(nc, in_maps, core_ids=list(range(8)))
    out = np.stack([np.asarray(res.results[b]["out"], dtype=np.float32) for b in range(8)], axis=0)
    return out
```

```python
import math
import numpy as np
import concourse.bass as bass
import concourse.mybir as mybir
from concourse.bass_utils import run_bass_kernel_spmd

F32 = mybir.dt.float32
BF16 = mybir.dt.bfloat16
I32 = mybir.dt.int32
AF = mybir.ActivationFunctionType
ALU = mybir.AluOpType

S = 2048
D = 1024
T = 512
NT = S // T
DFF = 4096
EPS = 1e-6
RH = 4
DEPTH = 4
NSLOT = 3
PG = 256

X0 = 0
U0 = 65536
SL0 = 131072
C0 = SL0 + NSLOT * 8192
WK0 = C0 + 6144
SB_BYTES = 207 * 1024
WK_BYTES = SB_BYTES - WK0


class Opd:
    __slots__ = ("ap", "keys")

    def __init__(self, ap, keys):
        self.ap = ap
        self.keys = keys


class Buf:
    def __init__(self, sbt, off, n, dt):
        self.esz = 4 if dt in (F32, I32) else 2
        assert off % 4 == 0
        self.off = off
        self.n = n
        self.dt = dt
        w0 = off // 4
        w1 = (off + n * self.esz + 3) // 4
        ap = sbt[:, w0:w1]
        if dt != F32:
            ap = ap.bitcast(dt)
        self.ap = ap

    def s(self, a=0, b=None, p0=0, p1=128):
        if b is None:
            b = self.n
        k0 = (self.off + a * self.esz) // PG
        k1 = (self.off + b * self.esz - 1) // PG
        return Opd(self.ap[p0:p1, a:b], [("sb", k) for k in range(k0, k1 + 1)])


class PBank:
    def __init__(self, pt, idx):
        self.ap = pt[:]
        self.idx = idx
        self.apb = pt[:].bitcast(BF16)

    def s(self, a=0, b=512, p0=0, p1=128):
        return Opd(self.ap[p0:p1, a:b], [("ps", self.idx)])

    def sb16(self, a=0, b=1024, p0=0, p1=128):
        return Opd(self.apb[p0:p1, a:b], [("ps", self.idx)])


class Sched:
    ENG = ("pe", "act", "dve", "pool", "sp")

    def __init__(self):
        self.ops = []
        self.lastw = {}
        self.readers = {}
        self.dma_cum = {}
        self.phase = "pro"

    def add(self, eng, fn, r=(), w=(), dma=None):
        deps = set()
        for o in r:
            for k in o.keys:
                lw = self.lastw.get(k)
                if lw is not None:
                    deps.add(lw)
                if k[0] == "ps":
                    rd = self.readers.get(k)
                    if rd:
                        deps.update(x for x in rd if self.ops[x]["eng"] != eng)
        for o in w:
            for k in o.keys:
                lw = self.lastw.get(k)
                if lw is not None:
                    deps.add(lw)
                rd = self.readers.get(k)
                if rd:
                    deps.update(rd)
        oid = len(self.ops)
        cdeps = {}
        ddeps = {}
        for d in deps:
            od = self.ops[d]
            if od["dma"] is not None:
                sk = od["dma"]
                ddeps[sk] = self.dma_cum[sk]
            else:
                if od["eng"] == "pe" and eng == "pe":
                    continue
                e = od["eng"]
                if e not in cdeps or cdeps[e] < d:
                    cdeps[e] = d
        if dma is not None:
            self.dma_cum[dma] = self.dma_cum.get(dma, 0) + 16
        self.ops.append(dict(eng=eng, fn=fn, cdeps=cdeps, ddeps=ddeps, dma=dma, sig=False, ph=self.phase))
        for o in r:
            for k in o.keys:
                self.readers.setdefault(k, set()).add(oid)
        for o in w:
            for k in o.keys:
                self.lastw[k] = oid
                self.readers[k] = set()
        return oid

    def emit(self, nc, sems, dsems, block):
        ops = self.ops
        for op in ops:
            for e, d in op["cdeps"].items():
                ops[d]["sig"] = True
        cnt = {e: 0 for e in self.ENG}
        for op in ops:
            if op["dma"] is None and op["sig"]:
                cnt[op["eng"]] += 1
                op["sigval"] = cnt[op["eng"]]
        self.sigcounts = dict(cnt)
        per = {e: [] for e in self.ENG}
        for op in ops:
            per[op["eng"]].append(op)

        def run(ename, eng):
            waited = {}
            for op in per[ename]:
                for e, d in op["cdeps"].items():
                    v = ops[d]["sigval"]
                    if waited.get(e, 0) < v:
                        eng.wait_ge(sems[e], v)
                        waited[e] = v
                for sk, v in op["ddeps"].items():
                    if waited.get(sk, 0) < v:
                        eng.wait_ge(dsems[sk], v)
                        waited[sk] = v
                ins = op["fn"](eng)
                if op["dma"] is not None:
                    ins.then_inc(dsems[op["dma"]], 16)
                elif op["sig"]:
                    ins.then_inc(sems[ename], 1)
            if ename == "sp":
                for sk, v in self.dma_cum.items():
                    if sk.startswith("out"):
                        eng.wait_ge(dsems[sk], v)

        @block.tensor
        def _(e):
            run("pe", e)

        @block.scalar
        def _(e):
            run("act", e)

        @block.vector
        def _(e):
            run("dve", e)

        @block.gpsimd
        def _(e):
            run("pool", e)

        @block.sync
        def _(e):
            run("sp", e)


def _consts():
    c = {}
    c["identf"] = np.eye(128, dtype=np.float32)
    tri = (np.arange(128)[None, :] >= np.arange(128)[:, None]).astype(np.float32)
    c["tri"] = tri
    perm = np.zeros((128, 128), np.float32)
    invd = np.zeros((128, 1), np.float32)
    sgn = np.ones((128, 1), np.float32)
    fr = (500000.0 ** (-np.arange(0, 16, 2, dtype=np.float32) / np.float32(16))).astype(np.float32)
    for p in range(128):
        dd = p % 64
        if dd < 8:
            perm[p + 8, p] = 1.0
            invd[p, 0] = fr[dd]
            sgn[p, 0] = -1.0
        elif dd < 16:
            perm[p - 8, p] = 1.0
            invd[p, 0] = fr[dd - 8]
            sgn[p, 0] = 1.0
    c["perm"] = perm
    c["invd"] = invd
    c["sgn"] = sgn
    c["invr"] = (10000.0 ** (-np.arange(0, 256, 2, dtype=np.float32) / np.float32(256))).astype(np.float32).reshape(128, 1)
    gam = 1.0 - 2.0 ** (-5.0 - np.arange(RH, dtype=np.float64))
    lg = np.log(gam)
    idx = np.arange(128, dtype=np.float64)
    xi = np.exp((idx + 1.0)[None, :] * lg[:, None])
    zeta = np.exp((127.0 - idx)[None, :] * lg[:, None])
    c["gC"] = [float(np.exp(128.0 * lg[h])) for h in range(RH)]
    xirep = np.tile(xi[:, None, :], (1, 4, 1)).reshape(RH * 512)
    c["xirep"] = np.broadcast_to(xirep[None, :], (128, RH * 512)).astype(np.float32).copy()
    c["zeta"] = (zeta.T / 16.0).astype(np.float32).copy()
    m2 = np.zeros((128, RH, 128), np.float64)
    for h in range(RH):
        m2[:, h, :] = tri * np.exp(-(idx + 1.0) * lg[h])[:, None] / 16.0
    c["mask2"] = m2.reshape(128, RH * 128).astype(np.float32)
    return c


TWO_PI = 2.0 * math.pi
CW1 = 6.28125
CW2 = TWO_PI - CW1
PI_LO = 3.1415925


def build(nlayers=DEPTH):
    CN = _consts()
    nc = bass.Bass("TRN2", target_bir_lowering=False)
    dram = {}

    def din(name, shape, dt=F32):
        dram[name] = nc.dram_tensor(name, list(shape), dt, kind="ExternalInput").ap()
        return dram[name]

    x_d = din("x", [S, D])
    c_d = din("c", [128, 8])
    pos_d = din("pos", [1, S], I32)
    ng_d = din("norm_g", [128, 16 * 8])
    adaw_d = din("ada_w", [DEPTH, D, 6 * D])
    adab_d = din("ada_b", [DEPTH, 6 * D])
    win_d = din("ret_w_in", [2, D, 6144])
    wout_d = din("ret_w_out", [2, 2048, D])
    kvg_d = din("kv_norm_g", [128, 8])
    kvaw_d = din("kv_ada_w", [D, 2 * D])
    kvab_d = din("kv_ada_b", [1, 2 * D])
    kvw_d = din("kv_w", [D, 2 * D])
    wq_d = din("diff_w_q", [2, D, D])
    wo_d = din("diff_w_o", [2, D, D])
    lam_d = din("diff_lam", [2, 256])
    sub_d = din("diff_subln_g", [128, 2])
    w1_d = din("mlp_w1", [DEPTH, D, DFF])
    w2_d = din("mlp_w2", [DEPTH, DFF, D])
    k_identf = din("k_identf", [128, 128])
    k_tri = din("k_tri", [128, 128])
    k_perm = din("k_perm", [128, 128])
    k_small = din("k_small", [128, 8])
    k_xirep = din("k_xirep", [128, RH * 512])
    k_mask2 = din("k_mask2", [128, RH * 128])
    out_d = nc.dram_tensor("out", [S, D], F32, kind="ExternalOutput").ap()

    sc = Sched()
    import contextlib
    es = contextlib.ExitStack()
    with es:
        sbt = es.enter_context(nc.sbuf_tensor("SB", [128, SB_BYTES // 4], F32))
        psb = [PBank(es.enter_context(nc.psum_tensor(f"ps{i}", [128, 512], F32)), i) for i in range(8)]
        sems = {e: es.enter_context(nc.semaphore("s_" + e)) for e in Sched.ENG}
        dkeys = [f"slot{i}" for i in range(NSLOT)] + ["misc", "stage0", "stage1", "pos", "bias0", "bias1", "out0", "out1"]
        dsems = {k: es.enter_context(nc.semaphore("d_" + k)) for k in dkeys}
        block = es.enter_context(nc.Block())

        def B(off, n, dt=F32):
            return Buf(sbt, off, n, dt)

        xT = [[B(X0 + (fc * S + t * T) * 4, T) for t in range(NT)] for fc in range(8)]
        slots = [B(SL0 + i * 8192, 4096, BF16) for i in range(NSLOT)]
        identb = B(C0 + 0, 128, BF16)
        onesb = B(C0 + 256, 128, BF16)
        trib = B(C0 + 512, 128, BF16)
        permb = B(C0 + 768, 128, BF16)
        identf = B(C0 + 1024, 128)
        onef = B(C0 + 1536, 4)
        ksm = B(C0 + 1552, 8)
        cT = B(C0 + 1584, 8, BF16)
        c32 = B(C0 + 1600, 8)
        MODS = [dict(modcol=B(C0 + 1664, 48), g0sc=B(C0 + 2368, 8), gga=B(C0 + 2400, 8), g2sc=B(C0 + 2432, 8),
                     ggm=B(C0 + 2464, 8)),
                dict(modcol=B(C0 + 5376, 48), g0sc=B(C0 + 5568, 8), gga=B(C0 + 5600, 8), g2sc=B(C0 + 5632, 8),
                     ggm=B(C0 + 5664, 8))]
        ngb = B(C0 + 1856, 128)
        kvgb = B(C0 + 2560, 8)
        kvgsc = B(C0 + 2592, 8)
        subg = B(C0 + 2720, 2)
        subsc = B(C0 + 2728, 1)
        neglam = B(C0 + 2732, 1)
        lamb = B(C0 + 3072, 256)
        lamt = B(C0 + 4096, 128)
        lams = B(C0 + 4608, 4)
        stg32 = B(C0 + 4864, 128)
        cosT = B(U0, S)
        sinT = B(U0 + 8192, S)
        xirep = B(U0 + 16384, RH * 512)
        R32 = [[B(U0 + 24576 + (h * 2 + dc) * 2048, 512) for dc in range(2)] for h in range(RH)]
        Rbf = [[B(U0 + 40960 + (h * 2 + dc) * 1024, 512, BF16) for dc in range(2)] for h in range(RH)]
        mask2 = B(U0 + 49152, RH * 128)
        o_sb = [B(U0 + 51200 + i * 2048, 512) for i in range(2)]
        osq = [B(U0 + 55296 + i * 1024, 512, BF16) for i in range(2)]
        PTb = [B(U0 + 57344 + i * 256, 128, BF16) for i in range(2)]
        ortmp = [B(U0 + 57856, 128), B(U0 + 63488 + 512, 128)]
        orstd = [B(U0 + 58368 + i * 512, 128) for i in range(2)]
        osd = B(U0 + 59392, 128)
        kTsh = [B(U0 + i * 4096, S, BF16) for i in range(8)]
        vsh = [B(U0 + 32768 + kb * 2048, 1024, BF16) for kb in range(16)]

        def mm(out, lhsT, rhs, start, stop):
            sc.add("pe", lambda e: e.matmul(out.ap, lhsT=lhsT.ap, rhs=rhs.ap, start=start, stop=stop),
                   r=[lhsT, rhs], w=[out])

        def tr(out, in_, ident):
            sc.add("pe", lambda e: e.transpose(out.ap, in_.ap, ident.ap), r=[in_, ident], w=[out])

        def act(out, in_, func, scale=None, bias=None, eng="act"):
            rr = [in_]
            kw = {}
            if scale is not None:
                if isinstance(scale, Opd):
                    rr.append(scale)
                    kw["scale"] = scale.ap
                else:
                    kw["scale"] = scale
            if bias is not None:
                if isinstance(bias, Opd):
                    rr.append(bias)
                    kw["bias"] = bias.ap
                else:
                    kw["bias"] = bias
            sc.add("act", lambda e: e.activation(out=out.ap, in_=in_.ap, func=func, **kw), r=rr, w=[out])

        def tt(out, in0, in1, op, eng="dve"):
            sc.add(eng, lambda e: e.tensor_tensor(out=out.ap, in0=in0.ap, in1=in1.ap, op=op), r=[in0, in1], w=[out])

        def ts(out, in0, s1, s2, op0, op1=None, eng="dve"):
            rr = [in0]
            a1 = s1
            if isinstance(s1, Opd):
                rr.append(s1)
                a1 = s1.ap
            a2 = s2
            if isinstance(s2, Opd):
                rr.append(s2)
                a2 = s2.ap
            if op1 is None:
                sc.add(eng, lambda e: e.tensor_single_scalar(out=out.ap, in_=in0.ap, scalar=a1, op=op0), r=rr, w=[out])
            else:
                sc.add(eng, lambda e: e.tensor_scalar(out=out.ap, in0=in0.ap, scalar1=a1, scalar2=a2, op0=op0, op1=op1),
                       r=rr, w=[out])

        def stt(out, in0, scalar, in1, op0, op1, eng="dve"):
            rr = [in0, in1]
            a = scalar
            if isinstance(scalar, Opd):
                rr.append(scalar)
                a = scalar.ap
            sc.add(eng, lambda e: e.scalar_tensor_tensor(out=out.ap, in0=in0.ap, scalar=a, in1=in1.ap, op0=op0, op1=op1),
                   r=rr, w=[out])

        def cp(out, in_, eng="dve"):
            if eng == "act":
                sc.add("act", lambda e: e.activation(out=out.ap, in_=in_.ap, func=AF.Identity), r=[in_], w=[out])
            else:
                sc.add(eng, lambda e: e.tensor_copy(out=out.ap, in_=in_.ap), r=[in_], w=[out])

        def recip(out, in_):
            sc.add("dve", lambda e: e.reciprocal(out=out.ap, in_=in_.ap), r=[in_], w=[out])

        def memset(out, val, eng="dve"):
            sc.add(eng, lambda e: e.memset(out.ap, val), w=[out])

        def dma(queue, semkey, out_ap, in_ap, r=(), w=()):
            sc.add(queue, lambda e: e.dma_start(out=out_ap, in_=in_ap), r=list(r), w=list(w), dma=semkey)

        piece_ctr = [0]

        def load_piece(srcs):
            i = piece_ctr[0]
            piece_ctr[0] += 1
            sl = slots[i % NSLOT]
            for (src, kcn, W, off, n) in srcs:
                dst = sl.ap.rearrange("p (k n) -> p k n", k=kcn)[:, :, off:off + n]
                dma("pool", f"slot{i % NSLOT}", dst, src, w=[sl.s(k * W + off, k * W + off + n) for k in range(kcn)])
            return sl

        def wview(d2, kcn):
            return d2.rearrange("(k p) n -> p k n", p=128)

        def small_load(dst, src_ap):
            dma("sp", "misc", dst.ap, src_ap, w=[dst.s()])

        small_load(identf, k_identf)
        small_load(ksm, k_small)
        small_load(c32, c_d)
        small_load(ngb, ng_d)
        small_load(kvgb, kvg_d)
        small_load(subg, sub_d)
        small_load(xirep, k_xirep)
        small_load(mask2, k_mask2)
        small_load(stg32, k_tri)
        cp(trib.s(), stg32.s())
        small_load(stg32, k_perm)
        cp(permb.s(), stg32.s())
        cp(identb.s(), identf.s())
        memset(onesb.s(), 1.0)
        memset(onef.s(), 1.0)
        act(cT.s(), c32.s(), AF.Silu)
        invr = ksm.s(0, 1)
        invd = ksm.s(1, 2)
        sgn = ksm.s(2, 3)

        import os
        SKIP = set(os.environ.get("KSKIP", "").split(","))
        stage = [B(WK0 + i * 4096, 1024) for i in range(2)]
        for tc in range(16 if "xload" not in SKIP else 0):
            st = stage[tc % 2]
            dma("sp", f"stage{tc % 2}", st.ap, x_d[tc * 128:(tc + 1) * 128, :], w=[st.s()])
            t, cc = tc // 4, tc % 4
            for half in range(2):
                pb = psb[(tc * 2 + half) % 4]
                for q in range(4):
                    fc = half * 4 + q
                    if "xtr" not in SKIP:
                        tr(pb.s(q * 128, (q + 1) * 128), st.s(fc * 128, (fc + 1) * 128), identf.s())
                    else:
                        mm(pb.s(q * 128, (q + 1) * 128), st.s(fc * 128, (fc + 1) * 128), identf.s(), True, True)
                for q in range(4):
                    fc = half * 4 + q
                    if "xcp" not in SKIP:
                        cp(xT[fc][t].s(cc * 128, (cc + 1) * 128), pb.s(q * 128, (q + 1) * 128),
                           eng="dve" if "xdve" in SKIP else ("act" if ("xact" in SKIP or q % 2) else "dve"))

        def sincos(posf, inv, ncol, cos_out, sin_out, tmpa, tmpb, tmpi, sin_scale=None):
            ang = tmpa
            ts(ang.s(0, ncol), posf.s(0, ncol), inv, None, ALU.mult)
            ts(tmpb.s(0, ncol), ang.s(0, ncol), 1.0 / TWO_PI, None, ALU.mult)
            cp(tmpi.s(0, ncol), tmpb.s(0, ncol))
            cp(tmpb.s(0, ncol), tmpi.s(0, ncol))
            stt(ang.s(0, ncol), tmpb.s(0, ncol), -CW1, ang.s(0, ncol), ALU.mult, ALU.add)
            stt(ang.s(0, ncol), tmpb.s(0, ncol), -CW2, ang.s(0, ncol), ALU.mult, ALU.add)
            ts(tmpb.s(0, ncol), ang.s(0, ncol), -PI_LO, PI_LO, ALU.max, ALU.min)
            act(sin_out.s(0, ncol), tmpb.s(0, ncol), AF.Sin, scale=sin_scale)
            ts(ang.s(0, ncol), ang.s(0, ncol), math.pi / 2.0, None, ALU.add)
            ts(tmpb.s(0, ncol), ang.s(0, ncol), math.pi, TWO_PI, ALU.is_gt, ALU.mult)
            tt(ang.s(0, ncol), ang.s(0, ncol), tmpb.s(0, ncol), ALU.subtract)
            ts(tmpb.s(0, ncol), ang.s(0, ncol), -PI_LO, PI_LO, ALU.max, ALU.min)
            act(cos_out.s(0, ncol), tmpb.s(0, ncol), AF.Sin)

        posi = B(WK0 + 8192, S, I32)
        posf = B(WK0 + 16384, S)
        tA = B(WK0 + 24576, S)
        tB = B(WK0 + 32768, S)
        if "tables" not in SKIP:
            dma("sp", "pos", posi.ap, pos_d.partition_broadcast(128), w=[posi.s()])
            cp(posf.s(), posi.s())
            sincos(posf, invr, S, cosT, sinT, tA, tB, posi)

        def compute_mod(wd, bd_row, ncols, dst):
            sc.phase = "mod"
            MS0 = WK0 + 47104
            rowc = [B(MS0 + i * 1024, 256) for i in range(2)]
            biasc = B(MS0 + 2048, 256)
            pcol = psb[7]
            wv = wview(wd, 8)
            for n in range(ncols // 512):
                sl = load_piece([(wv[:, :, n * 512:(n + 1) * 512], 8, 512, 0, 512)])
                for hh in range(2):
                    c0 = n * 512 + hh * 256
                    dma("sp", "bias0", biasc.ap[0:1, :], bd_row[0:1, c0:c0 + 256], w=[biasc.s(p0=0, p1=1)])
                    pr = psb[hh]
                    for kc in range(8):
                        mm(pr.s(0, 256, 0, 1), cT.s(kc, kc + 1), sl.s(kc * 512 + hh * 256, kc * 512 + hh * 256 + 256),
                           kc == 0, kc == 7)
                    rc_ = rowc[hh]
                    tt(rc_.s(p0=0, p1=1), pr.s(0, 256, 0, 1), biasc.s(p0=0, p1=1), ALU.add)
                    for j in range(2):
                        col = c0 // 128 + j
                        mm(Opd(pcol.ap[:, col:col + 1], pcol.s(0, 128).keys), rc_.s(j * 128, (j + 1) * 128, 0, 1),
                           onef.s(0, 1, 0, 1), True, True)
            cp(dst.s(0, ncols // 128), pcol.s(0, ncols // 128))

        def layer_mod(l):
            P = MODS[l % 2]
            modcol = P["modcol"]
            compute_mod(adaw_d[l], adab_d[l:l + 1, :], 6 * D, modcol)
            base = l * 32
            stt(P["g0sc"].s(), modcol.s(8, 16), 1.0, ngb.s(base + 0, base + 8), ALU.add, ALU.mult)
            stt(P["gga"].s(), modcol.s(16, 24), 1.0, ngb.s(base + 8, base + 16), ALU.add, ALU.mult)
            stt(P["g2sc"].s(), modcol.s(32, 40), 1.0, ngb.s(base + 16, base + 24), ALU.add, ALU.mult)
            stt(P["ggm"].s(), modcol.s(40, 48), 1.0, ngb.s(base + 24, base + 32), ALU.add, ALU.mult)


        nsq = [B(WK0 + i * 1024, T, BF16) for i in range(2)]
        nsd = B(WK0 + 2048, T)
        nrstd = B(WK0 + 4096, T)
        ntmp = [B(WK0 + 6144 + i * 2048, T) for i in range(2)]
        HT0 = WK0 + 10240
        hT = [B(HT0 + fc * 1024, T, BF16) for fc in range(8)]
        WK1 = WK0 + 18432

        def rms_stats(srcs, pbank, dnorm):
            n = len(srcs)
            for i, s_ in enumerate(srcs):
                q = nsq[i % 2]
                act(q.s(), s_, AF.Square)
                mm(pbank.s(), onesb.s(), q.s(), i == 0, i == n - 1)
            act(nsd.s(), pbank.s(), AF.Ln, scale=1.0 / dnorm, bias=EPS)
            act(nrstd.s(), nsd.s(), AF.Exp, scale=-0.5)

        def make_h(t, gsc, shv, sh0):
            sc.phase = "make_h"
            rms_stats([xT[fc][t].s() for fc in range(8)], psb[3], D)
            for fc in range(8):
                tm = ntmp[fc % 2]
                stt(tm.s(), xT[fc][t].s(), gsc.s(fc, fc + 1), nrstd.s(), ALU.mult, ALU.mult)
                act(hT[fc].s(), tm.s(), AF.Identity, bias=shv.s(sh0 + fc, sh0 + fc + 1))

        def resid_update(t, ybuf, gg):
            sc.phase = "resid"
            rms_stats([ybuf[oc].s() for oc in range(8)], psb[3], D)
            for oc in range(8):
                tm = ntmp[oc % 2]
                stt(tm.s(), ybuf[oc].s(), gg.s(oc, oc + 1), nrstd.s(), ALU.mult, ALU.mult)
                tt(xT[oc][t].s(), xT[oc][t].s(), tm.s(), ALU.add)

        def mlp_tile(l, t):
            P = MODS[l % 2]
            ggm = P["ggm"]
            make_h(t, P["g2sc"], P["modcol"], 24)
            uT = [B(WK1 + i * 1024, T, BF16) for i in range(8)]
            yb = [B(WK1 + 8192 + oc * 2048, T) for oc in range(8)]
            rtmp = [B(WK1 + 24576 + i * 2048, T) for i in range(2)]
            w1v = wview(w1_d[l], 8)
            for qf in range(4):
                sc.phase = "mlp_w1"
                for half in range(2):
                    c0 = qf * 1024 + half * 512
                    sl = load_piece([(w1v[:, :, c0:c0 + 512], 8, 512, 0, 512)])
                    for j in range(4):
                        pb = psb[(half * 4 + j) % 3]
                        for kc in range(8):
                            mm(pb.s(), sl.s(kc * 512 + j * 128, kc * 512 + (j + 1) * 128), hT[kc].s(), kc == 0, kc == 7)
                        rt = rtmp[j % 2]
                        act(rt.s(), pb.s(), AF.Relu)
                        act(uT[half * 4 + j].s(), rt.s(), AF.Square)
                w2v = w2_d[l][qf * 1024:(qf + 1) * 1024, :].rearrange("(k p) n -> p k n", p=128)
                sc.phase = "mlp_w2"
                for half in range(2):
                    sl = load_piece([(w2v[:, :, half * 512:(half + 1) * 512], 8, 512, 0, 512)])
                    for j in range(4):
                        oc = half * 4 + j
                        pb = psb[4 + (oc % 3)]
                        for fk in range(8):
                            mm(pb.s(), sl.s(fk * 512 + j * 128, fk * 512 + (j + 1) * 128), uT[fk].s(), fk == 0, fk == 7)
                        if qf == 0:
                            cp(yb[oc].s(), pb.s())
                        else:
                            tt(yb[oc].s(), yb[oc].s(), pb.s(), ALU.add)
            resid_update(t, yb, ggm)

        def ret_tile(l, t):
            P = MODS[l % 2]
            gga = P["gga"]
            make_h(t, P["g0sc"], P["modcol"], 0)
            qT_ = [B(WK1 + dc * 1024, T, BF16) for dc in range(2)]
            kT_ = [B(WK1 + 2048 + dc * 1024, T, BF16) for dc in range(2)]
            ktok = B(WK1 + 4096, 1024, BF16)
            vb = [B(WK1 + 6144 + c * 1024, 512, BF16) for c in range(4)]
            vz = [B(U0 + 59904 + c * 1024, 512, BF16) for c in range(4)]
            sg = [B(WK1 + 10240 + ec * 1024, T, BF16) for ec in range(4)]
            rp = ntmp
            GT0 = WK1 + 14336
            gatedT = [B(GT0 + i * 1024, T, BF16) for i in range(16)]
            winv = wview(win_d[l], 8)
            cs = cosT.s(t * T, (t + 1) * T)
            sn = sinT.s(t * T, (t + 1) * T)
            for hd in range(RH):
                sc.phase = "ret_qk"
                sl = load_piece([(winv[:, :, hd * 256:(hd + 1) * 256], 8, 512, 0, 256),
                                 (winv[:, :, 1024 + hd * 256:1024 + (hd + 1) * 256], 8, 512, 256, 256)])
                for which, dst in ((0, qT_), (1, kT_)):
                    pbs = [psb[0], psb[1]]
                    for dc in range(2):
                        c0 = which * 256 + dc * 128
                        for kc in range(8):
                            mm(pbs[dc].s(), sl.s(kc * 512 + c0, kc * 512 + c0 + 128), hT[kc].s(), kc == 0, kc == 7)
                    x1, x2 = pbs[0].s(), pbs[1].s()
                    tt(rp[0].s(), x1, cs, ALU.mult)
                    tt(rp[1].s(), x2, sn, ALU.mult)
                    tt(dst[0].s(), rp[0].s(), rp[1].s(), ALU.subtract)
                    tt(rp[0].s(), x2, cs, ALU.mult)
                    tt(rp[1].s(), x1, sn, ALU.mult)
                    tt(dst[1].s(), rp[0].s(), rp[1].s(), ALU.add)
                sc.phase = "ret_ktr"
                pbt = psb[2]
                for c in range(4):
                    for dc in range(2):
                        o0 = (c * 2 + dc) * 128
                        tr(pbt.sb16(o0, o0 + 128), kT_[dc].s(c * 128, (c + 1) * 128), identb.s())
                cp(ktok.s(), pbt.sb16(), eng="act")
                sc.phase = "ret_v"
                sl = load_piece([(winv[:, :, 2048 + hd * 512:2048 + (hd + 1) * 512], 8, 512, 0, 512)])
                for c in range(4):
                    pb = psb[c % 2]
                    for kc in range(8):
                        mm(pb.s(), hT[kc].s(c * 128, (c + 1) * 128), sl.s(kc * 512, (kc + 1) * 512), kc == 0, kc == 7)
                    cp(vb[c].s(), pb.s(), eng="act")
                    act(vz[c].s(), pb.s(), AF.Copy, scale=ksm.s(3 + hd, 4 + hd))
                sc.phase = "ret_g"
                sl = load_piece([(winv[:, :, 4096 + hd * 512:4096 + (hd + 1) * 512], 8, 512, 0, 512)])
                for ec in range(4):
                    pb = psb[ec % 2]
                    for kc in range(8):
                        mm(pb.s(), sl.s(kc * 512 + ec * 128, kc * 512 + (ec + 1) * 128), hT[kc].s(), kc == 0, kc == 7)
                    act(sg[ec].s(), pb.s(), AF.Silu)
                sc.phase = "ret_chunk"
                gC = CN["gC"][hd]

                def finish(c, n):
                    csl = (c * 128, (c + 1) * 128)
                    ob = o_sb[n % 2]
                    oq = osq[n % 2]
                    pst = psb[n % 2].s(0, 128)
                    for ec in range(4):
                        mm(pst, onesb.s(), oq.s(ec * 128, (ec + 1) * 128), ec == 0, ec == 3)
                    act(osd.s(), pst, AF.Ln, scale=1.0 / 512.0, bias=EPS)
                    rs = orstd[n % 2]
                    act(rs.s(), osd.s(), AF.Exp, scale=-0.5)
                    for ec in range(4):
                        tt(ortmp[ec % 2].s(), ob.s(ec * 128, (ec + 1) * 128), rs.s(), ALU.mult)
                        tt(gatedT[hd * 4 + ec].s(*csl), ortmp[ec % 2].s(), sg[ec].s(*csl), ALU.mult)

                for c in range(4):
                    n = t * 4 + c
                    csl = (c * 128, (c + 1) * 128)
                    ps_s = psb[2 + (n % 2)].s(0, 128)
                    for dc in range(2):
                        mm(ps_s, kT_[dc].s(*csl), qT_[dc].s(*csl), dc == 0, dc == 1)
                    PT = PTb[n % 2]
                    tt(PT.s(), ps_s, mask2.s(hd * 128, (hd + 1) * 128), ALU.mult)
                    if n < 15:
                        for dc in range(2):
                            pa = psb[6 + dc]
                            mm(pa.s(), ktok.s((c * 2 + dc) * 128, (c * 2 + dc + 1) * 128), vz[c].s(), True, True)
                    po = psb[4 + (n % 2)]
                    for ec in range(4):
                        osl = po.s(ec * 128, (ec + 1) * 128)
                        mm(osl, vb[c].s(ec * 128, (ec + 1) * 128), PT.s(), True, n == 0)
                        if n > 0:
                            for dc in range(2):
                                mm(osl, Rbf[hd][dc].s(ec * 128, (ec + 1) * 128), qT_[dc].s(*csl), False, dc == 1)
                    if n < 15:
                        for dc in range(2):
                            pa = psb[6 + dc]
                            if n == 0:
                                cp(R32[hd][dc].s(), pa.s())
                            else:
                                stt(R32[hd][dc].s(), R32[hd][dc].s(), gC, pa.s(), ALU.mult, ALU.add)
                            cp(Rbf[hd][dc].s(), R32[hd][dc].s(), eng="act")
                    if c > 0:
                        finish(c - 1, n - 1)
                    ob = o_sb[n % 2]
                    tt(ob.s(), po.s(), xirep.s(hd * 512, (hd + 1) * 512), ALU.mult)
                    act(osq[n % 2].s(), ob.s(), AF.Square)
                finish(3, t * 4 + 3)
            sc.phase = "ret_wout"
            yb = [B(HT0 + oc * 2048, T) for oc in range(8)]
            woutv = wout_d[l].rearrange("(k p) n -> p k n", p=128)
            for pi_ in range(4):
                sl = load_piece([(woutv[:, :, pi_ * 256:(pi_ + 1) * 256], 16, 256, 0, 256)])
                for j in range(2):
                    oc = pi_ * 2 + j
                    pb = psb[oc % 3]
                    for kc in range(16):
                        mm(pb.s(), sl.s(kc * 256 + j * 128, kc * 256 + (j + 1) * 128), gatedT[kc].s(), kc == 0, kc == 15)
                    cp(yb[oc].s(), pb.s())
            resid_update(t, yb, gga)

        DT0 = WK1
        dC = B(DT0, T)
        dS = B(DT0 + 2048, T)
        dposi = B(DT0 + 4096, T, I32)
        dposf = B(DT0 + 6144, T)
        dtA = B(DT0 + 8192, T)
        dtB = B(DT0 + 10240, T)
        qraw = [B(DT0 + 4096 + i * 1024, T, BF16) for i in range(2)]
        drp = [B(DT0 + 8192 + i * 2048, T) for i in range(2)]

        def diff_tables(t):
            sc.phase = "dtables"
            dma("sp", "pos", dposi.ap, pos_d[0:1, t * T:(t + 1) * T].partition_broadcast(128), w=[dposi.s()])
            cp(dposf.s(), dposi.s())
            sincos(dposf, invd, T, dC, dS, dtA, dtB, dposi, sin_scale=sgn)

        def rope_diff(pb, dst, idx):
            qr = qraw[idx % 2]
            cp(qr.s(), pb.s(), eng="act")
            p2 = psb[3]
            mm(p2.s(), permb.s(), qr.s(), True, True)
            tt(drp[0].s(), pb.s(), dC.s(), ALU.mult)
            tt(drp[1].s(), p2.s(), dS.s(), ALU.mult)
            tt(dst, drp[0].s(), drp[1].s(), ALU.add)

        def kv_tile(t):
            make_h(t, kvgsc, kvmod, 0)
            diff_tables(t)
            kvv = wview(kvw_d, 8)
            sc.phase = "kv"
            for half in range(2):
                sl = load_piece([(kvv[:, :, half * 512:(half + 1) * 512], 8, 512, 0, 512)])
                for j in range(4):
                    i = half * 4 + j
                    pb = psb[i % 3]
                    for kc in range(8):
                        mm(pb.s(), sl.s(kc * 512 + j * 128, kc * 512 + (j + 1) * 128), hT[kc].s(), kc == 0, kc == 7)
                    rope_diff(pb, kTsh[i].s(t * T, (t + 1) * T), i)
            for half in range(2):
                sl = load_piece([(kvv[:, :, 1024 + half * 512:1024 + (half + 1) * 512], 8, 512, 0, 512)])
                for c in range(4):
                    pb = psb[c % 3]
                    for kc in range(8):
                        mm(pb.s(), hT[kc].s(c * 128, (c + 1) * 128), sl.s(kc * 512, (kc + 1) * 512), kc == 0, kc == 7)
                    cp(vsh[t * 4 + c].s(half * 512, (half + 1) * 512), pb.s(), eng="act" if c % 2 else "dve")

        def diff_tile(l, t):
            j_ = l - 2
            P = MODS[l % 2]
            gga = P["gga"]
            make_h(t, P["g0sc"], P["modcol"], 0)
            diff_tables(t)
            QT0 = DT0 + 12288
            qT_ = [B(QT0 + i * 1024, T, BF16) for i in range(8)]
            OT_ = [B(QT0 + 8192 + i * 1024, T, BF16) for i in range(8)]
            E0 = DT0
            Eb = [B(E0 + i * 1024, T, BF16) for i in range(4)]
            Zr = [B(E0 + 4096 + i * 2048, T) for i in range(2)]
            t01 = [B(E0 + 8192 + i * 2048, T) for i in range(2)]
            ofp = ntmp[0]
            wqv = wview(wq_d[j_], 8)
            sc.phase = "diff_q"
            for half in range(2):
                sl = load_piece([(wqv[:, :, half * 512:(half + 1) * 512], 8, 512, 0, 512)])
                for j in range(4):
                    i = half * 4 + j
                    pb = psb[i % 3]
                    for kc in range(8):
                        mm(pb.s(), sl.s(kc * 512 + j * 128, kc * 512 + (j + 1) * 128), hT[kc].s(), kc == 0, kc == 7)
                    rope_diff(pb, qT_[i].s(), i)
            nkb = 4 * t + 4
            sc.phase = "diff_attn"
            Ob = [psb[4], psb[5]]
            Zb = [psb[6], psb[7]]
            pairs = [(i, kb) for i in range(8) for kb in range(nkb)]

            def emitS(pi_):
                i, kb = pairs[pi_]
                r = kb - 4 * t
                q0 = 0 if r < 0 else r * 128
                for a in range(2):
                    pS = psb[2 * (pi_ % 2) + a]
                    mm(pS.s(q0, T), kTsh[i].s(kb * 128, (kb + 1) * 128, 64 * a, 64 * a + 64),
                       qT_[i].s(q0, T, 64 * a, 64 * a + 64), True, True)
                for a in range(2):
                    pS = psb[2 * (pi_ % 2) + a]
                    Et = Eb[2 * (pi_ % 2) + a]
                    act(Et.s(q0, T), pS.s(q0, T), AF.Exp, scale=0.125)
                    if r >= 0:
                        tt(Et.s(q0, q0 + 128), Et.s(q0, q0 + 128), trib.s(), ALU.mult)

            def emitPV(pi_):
                i, kb = pairs[pi_]
                r = kb - 4 * t
                q0 = 0 if r < 0 else r * 128
                for a in range(2):
                    Et = Eb[2 * (pi_ % 2) + a]
                    mm(Ob[a].s(q0, T), vsh[kb].s(i * 128, (i + 1) * 128), Et.s(q0, T), kb == 0, kb == nkb - 1)
                    mm(Zb[a].s(q0, T), onesb.s(), Et.s(q0, T), kb == 0, kb == nkb - 1)
                if kb == nkb - 1:
                    combine(i)

            def combine(i):
                for a in range(2):
                    act(Zr[a].s(), Zb[a].s(), AF.Ln)
                    act(Zr[a].s(), Zr[a].s(), AF.Exp, scale=-1.0)
                    tt(t01[a].s(), Ob[a].s(), Zr[a].s(), ALU.mult)
                stt(ofp.s(), t01[1].s(), neglam.s(), t01[0].s(), ALU.mult, ALU.add)
                q = nsq[i % 2]
                act(q.s(), ofp.s(), AF.Square)
                pst = psb[3]
                mm(pst.s(), onesb.s(), q.s(), True, True)
                act(nsd.s(), pst.s(), AF.Ln, scale=1.0 / 128.0, bias=EPS)
                act(nrstd.s(), nsd.s(), AF.Exp, scale=-0.5)
                stt(OT_[i].s(), ofp.s(), subsc.s(), nrstd.s(), ALU.mult, ALU.mult)

            NU = len(pairs)
            LOOK = 1
            for u in range(min(LOOK, NU)):
                emitS(u)
            for u in range(NU):
                emitPV(u)
                if u + LOOK < NU:
                    emitS(u + LOOK)
            Y0 = DT0
            yb = [B(Y0 + oc * 2048, T) for oc in range(8)]
            wov = wview(wo_d[j_], 8)
            sc.phase = "diff_wo"
            for half in range(2):
                sl = load_piece([(wov[:, :, half * 512:(half + 1) * 512], 8, 512, 0, 512)])
                for j in range(4):
                    oc = half * 4 + j
                    pb = psb[oc % 3]
                    for kc in range(8):
                        mm(pb.s(), sl.s(kc * 512 + j * 128, kc * 512 + (j + 1) * 128), OT_[kc].s(), kc == 0, kc == 7)
                    cp(yb[oc].s(), pb.s())
            resid_update(t, yb, gga)

        kvmod = B(C0 + 2624, 16)
        layer_mod(0)
        for l in range(min(nlayers, 2)):
            for t in range(NT):
                ret_tile(l, t)
                if t == 1 and l + 1 < nlayers:
                    layer_mod(l + 1)
                mlp_tile(l, t)
        if nlayers > 2:
            compute_mod(kvaw_d, kvab_d, 2 * D, kvmod)
            stt(kvgsc.s(), kvmod.s(8, 16), 1.0, kvgb.s(), ALU.add, ALU.mult)
            for t in range(NT):
                kv_tile(t)
            for l in range(2, nlayers):
                j_ = l - 2
                lam_init = 0.8 - 0.6 * math.exp(-0.3 * l)
                dma("sp", "misc", lamb.ap, lam_d[j_:j_ + 1, :].partition_broadcast(128), w=[lamb.s()])
                tt(lamt.s(0, 64), lamb.s(0, 64), lamb.s(64, 128), ALU.mult)
                tt(lamt.s(64, 128), lamb.s(128, 192), lamb.s(192, 256), ALU.mult)
                sc.add("dve", lambda e: e.reduce_sum(out=lams.ap[:, 0:1], in_=lamt.ap[:, 0:64], axis=mybir.AxisListType.X),
                       r=[lamt.s()], w=[lams.s()])
                sc.add("dve", lambda e: e.reduce_sum(out=lams.ap[:, 1:2], in_=lamt.ap[:, 64:128], axis=mybir.AxisListType.X),
                       r=[lamt.s()], w=[lams.s()])
                act(lams.s(2, 4), lams.s(0, 2), AF.Exp)
                tt(neglam.s(), lams.s(3, 4), lams.s(2, 3), ALU.subtract)
                ts(neglam.s(), neglam.s(), -lam_init, None, ALU.add)
                ts(subsc.s(), subg.s(j_, j_ + 1), 1.0 - lam_init, None, ALU.mult)
                for t in range(NT):
                    diff_tile(l, t)
                    if t == 1 and l + 1 < nlayers:
                        layer_mod(l + 1)
                    mlp_tile(l, t)

        sc.phase = "out"
        for tc in range(16):
            st = stage[tc % 2]
            t, cc = tc // 4, tc % 4
            for half in range(2):
                pb = psb[(tc * 2 + half) % 4]
                for q in range(4):
                    fc = half * 4 + q
                    tr(pb.s(q * 128, (q + 1) * 128), xT[fc][t].s(cc * 128, (cc + 1) * 128), identf.s())
                cp(st.s(half * 512, (half + 1) * 512), pb.s(), eng="act" if half else "dve")
            dma("sp", f"out{tc % 2}", out_d[tc * 128:(tc + 1) * 128, :], st.ap, r=[st.s()])

        sc.emit(nc, sems, dsems, block)
    return nc, sc


_NC_CACHE = {}


def _prep_inputs(inputs, b, CN):
    f = lambda a: np.ascontiguousarray(np.asarray(a, dtype=np.float32))
    m = {}
    m["x"] = f(inputs["x"][b])
    m["c"] = f(np.asarray(inputs["c"][b]).reshape(8, 128).T)
    m["pos"] = np.ascontiguousarray(np.asarray(inputs["positions"][b], dtype=np.int32).reshape(1, S))
    ng = np.asarray(inputs["norm_g"], dtype=np.float32).reshape(4, 4, 8, 128)
    m["norm_g"] = f(ng.transpose(3, 0, 1, 2).reshape(128, 128))
    m["ada_w"] = f(inputs["ada_w"])
    m["ada_b"] = f(inputs["ada_b"])
    m["ret_w_in"] = f(inputs["ret_w_in"])
    m["ret_w_out"] = f(inputs["ret_w_out"])
    m["kv_norm_g"] = f(np.asarray(inputs["kv_norm_g"]).reshape(8, 128).T)
    m["kv_ada_w"] = f(inputs["kv_ada_w"])
    m["kv_ada_b"] = f(np.asarray(inputs["kv_ada_b"]).reshape(1, 2 * D))
    m["kv_w"] = f(inputs["kv_w"])
    m["diff_w_q"] = f(inputs["diff_w_q"])
    m["diff_w_o"] = f(inputs["diff_w_o"])
    m["diff_lam"] = f(np.asarray(inputs["diff_lam"]).reshape(2, 256))
    m["diff_subln_g"] = f(np.asarray(inputs["diff_subln_g"]).T)
    m["mlp_w1"] = f(inputs["mlp_w1"])
    m["mlp_w2"] = f(inputs["mlp_w2"])
    m["k_identf"] = CN["identf"]
    m["k_tri"] = CN["tri"]
    m["k_perm"] = CN["perm"]
    sm = np.zeros((128, 8), np.float32)
    sm[:, 0:1] = CN["invr"]
    sm[:, 1:2] = CN["invd"]
    sm[:, 2:3] = CN["sgn"]
    sm[:, 3:7] = CN["zeta"]
    m["k_small"] = sm
    m["k_xirep"] = CN["xirep"]
    m["k_mask2"] = CN["mask2"]
    return m


def kernel(**inputs):
    CN = _consts()
    if "nc" not in _NC_CACHE:
        _NC_CACHE["nc"] = build(DEPTH)[0]
    nc = _NC_CACHE["nc"]
    shared = None
    in_maps = []
    for b in range(8):
        m = _prep_inputs(inputs, b, CN) if shared is None else None
        if shared is None:
            shared = m
            in_maps.append(m)
        else:
            mm_ = dict(shared)
            mm_["x"] = np.ascontiguousarray(np.asarray(inputs["x"][b], dtype=np.float32))
            mm_["c"] = np.ascontiguousarray(np.asarray(inputs["c"][b], dtype=np.float32).reshape(8, 128).T)
            mm_["pos"] = np.ascontiguousarray(np.asarray(inputs["positions"][b], dtype=np.int32).reshape(1, S))
            in_maps.append(mm_)
    res = run_bass_kernel_spmd(nc, in_maps, core_ids=list(range(8)))
    out = np.stack([np.asarray(res.results[b]["out"], dtype=np.float32) for b in range(8)], axis=0)
    return out
```

```python
import math
import numpy as np
import concourse.bass as bass
import concourse.mybir as mybir
from concourse.bass_utils import run_bass_kernel_spmd

F32 = mybir.dt.float32
BF16 = mybir.dt.bfloat16
I32 = mybir.dt.int32
AF = mybir.ActivationFunctionType
ALU = mybir.AluOpType

S = 2048
D = 1024
T = 512
NT = S // T
DFF = 4096
EPS = 1e-6
RH = 4
DEPTH = 4
NSLOT = 3
PG = 256

X0 = 0
U0 = 65536
SL0 = 131072
C0 = SL0 + NSLOT * 8192
WK0 = C0 + 6144
SB_BYTES = 207 * 1024
WK_BYTES = SB_BYTES - WK0


class Opd:
    __slots__ = ("ap", "keys")

    def __init__(self, ap, keys):
        self.ap = ap
        self.keys = keys


class Buf:
    def __init__(self, sbt, off, n, dt):
        self.esz = 4 if dt in (F32, I32) else 2
        assert off % 4 == 0
        self.off = off
        self.n = n
        self.dt = dt
        w0 = off // 4
        w1 = (off + n * self.esz + 3) // 4
        ap = sbt[:, w0:w1]
        if dt != F32:
            ap = ap.bitcast(dt)
        self.ap = ap

    def s(self, a=0, b=None, p0=0, p1=128):
        if b is None:
            b = self.n
        k0 = (self.off + a * self.esz) // PG
        k1 = (self.off + b * self.esz - 1) // PG
        return Opd(self.ap[p0:p1, a:b], [("sb", k) for k in range(k0, k1 + 1)])


class PBank:
    def __init__(self, pt, idx):
        self.ap = pt[:]
        self.idx = idx
        self.apb = pt[:].bitcast(BF16)

    def s(self, a=0, b=512, p0=0, p1=128):
        return Opd(self.ap[p0:p1, a:b], [("ps", self.idx)])

    def sb16(self, a=0, b=1024, p0=0, p1=128):
        return Opd(self.apb[p0:p1, a:b], [("ps", self.idx)])


class Sched:
    ENG = ("pe", "act", "dve", "pool", "sp")

    def __init__(self):
        self.ops = []
        self.lastw = {}
        self.readers = {}
        self.dma_cum = {}
        self.phase = "pro"

    def add(self, eng, fn, r=(), w=(), dma=None):
        deps = set()
        for o in r:
            for k in o.keys:
                lw = self.lastw.get(k)
                if lw is not None:
                    deps.add(lw)
                if k[0] == "ps":
                    rd = self.readers.get(k)
                    if rd:
                        deps.update(x for x in rd if self.ops[x]["eng"] != eng)
        for o in w:
            for k in o.keys:
                lw = self.lastw.get(k)
                if lw is not None:
                    deps.add(lw)
                rd = self.readers.get(k)
                if rd:
                    deps.update(rd)
        oid = len(self.ops)
        cdeps = {}
        ddeps = {}
        for d in deps:
            od = self.ops[d]
            if od["dma"] is not None:
                sk = od["dma"]
                ddeps[sk] = self.dma_cum[sk]
            else:
                if od["eng"] == "pe" and eng == "pe":
                    continue
                e = od["eng"]
                if e not in cdeps or cdeps[e] < d:
                    cdeps[e] = d
        if dma is not None:
            self.dma_cum[dma] = self.dma_cum.get(dma, 0) + 16
        self.ops.append(dict(eng=eng, fn=fn, cdeps=cdeps, ddeps=ddeps, dma=dma, sig=False, ph=self.phase))
        for o in r:
            for k in o.keys:
                self.readers.setdefault(k, set()).add(oid)
        for o in w:
            for k in o.keys:
                self.lastw[k] = oid
                self.readers[k] = set()
        return oid

    def emit(self, nc, sems, dsems, block):
        ops = self.ops
        for op in ops:
            for e, d in op["cdeps"].items():
                ops[d]["sig"] = True
        cnt = {e: 0 for e in self.ENG}
        for op in ops:
            if op["dma"] is None and op["sig"]:
                cnt[op["eng"]] += 1
                op["sigval"] = cnt[op["eng"]]
        self.sigcounts = dict(cnt)
        per = {e: [] for e in self.ENG}
        for op in ops:
            per[op["eng"]].append(op)

        def run(ename, eng):
            waited = {}
            for op in per[ename]:
                for e, d in op["cdeps"].items():
                    v = ops[d]["sigval"]
                    if waited.get(e, 0) < v:
                        eng.wait_ge(sems[e], v)
                        waited[e] = v
                for sk, v in op["ddeps"].items():
                    if waited.get(sk, 0) < v:
                        eng.wait_ge(dsems[sk], v)
                        waited[sk] = v
                ins = op["fn"](eng)
                if op["dma"] is not None:
                    ins.then_inc(dsems[op["dma"]], 16)
                elif op["sig"]:
                    ins.then_inc(sems[ename], 1)
            if ename == "sp":
                for sk, v in self.dma_cum.items():
                    if sk.startswith("out"):
                        eng.wait_ge(dsems[sk], v)

        @block.tensor
        def _(e):
            run("pe", e)

        @block.scalar
        def _(e):
            run("act", e)

        @block.vector
        def _(e):
            run("dve", e)

        @block.gpsimd
        def _(e):
            run("pool", e)

        @block.sync
        def _(e):
            run("sp", e)


def _consts():
    c = {}
    c["identf"] = np.eye(128, dtype=np.float32)
    tri = (np.arange(128)[None, :] >= np.arange(128)[:, None]).astype(np.float32)
    c["tri"] = tri
    perm = np.zeros((128, 128), np.float32)
    invd = np.zeros((128, 1), np.float32)
    sgn = np.ones((128, 1), np.float32)
    fr = (500000.0 ** (-np.arange(0, 16, 2, dtype=np.float32) / np.float32(16))).astype(np.float32)
    for p in range(128):
        dd = p % 64
        if dd < 8:
            perm[p + 8, p] = 1.0
            invd[p, 0] = fr[dd]
            sgn[p, 0] = -1.0
        elif dd < 16:
            perm[p - 8, p] = 1.0
            invd[p, 0] = fr[dd - 8]
            sgn[p, 0] = 1.0
    c["perm"] = perm
    c["invd"] = invd
    c["sgn"] = sgn
    c["invr"] = (10000.0 ** (-np.arange(0, 256, 2, dtype=np.float32) / np.float32(256))).astype(np.float32).reshape(128, 1)
    gam = 1.0 - 2.0 ** (-5.0 - np.arange(RH, dtype=np.float64))
    lg = np.log(gam)
    idx = np.arange(128, dtype=np.float64)
    xi = np.exp((idx + 1.0)[None, :] * lg[:, None])
    zeta = np.exp((127.0 - idx)[None, :] * lg[:, None])
    c["gC"] = [float(np.exp(128.0 * lg[h])) for h in range(RH)]
    xirep = np.tile(xi[:, None, :], (1, 4, 1)).reshape(RH * 512)
    c["xirep"] = np.broadcast_to(xirep[None, :], (128, RH * 512)).astype(np.float32).copy()
    c["zeta"] = (zeta.T / 16.0).astype(np.float32).copy()
    m2 = np.zeros((128, RH, 128), np.float64)
    for h in range(RH):
        m2[:, h, :] = tri * np.exp(-(idx + 1.0) * lg[h])[:, None] / 16.0
    c["mask2"] = m2.reshape(128, RH * 128).astype(np.float32)
    return c


TWO_PI = 2.0 * math.pi
CW1 = 6.28125
CW2 = TWO_PI - CW1
PI_LO = 3.1415925


def build(nlayers=DEPTH):
    CN = _consts()
    nc = bass.Bass("TRN2", target_bir_lowering=False)
    dram = {}

    def din(name, shape, dt=F32):
        dram[name] = nc.dram_tensor(name, list(shape), dt, kind="ExternalInput").ap()
        return dram[name]

    x_d = din("x", [S, D])
    c_d = din("c", [128, 8])
    pos_d = din("pos", [1, S], I32)
    ng_d = din("norm_g", [128, 16 * 8])
    adaw_d = din("ada_w", [DEPTH, D, 6 * D])
    adab_d = din("ada_b", [DEPTH, 6 * D])
    win_d = din("ret_w_in", [2, D, 6144])
    wout_d = din("ret_w_out", [2, 2048, D])
    kvg_d = din("kv_norm_g", [128, 8])
    kvaw_d = din("kv_ada_w", [D, 2 * D])
    kvab_d = din("kv_ada_b", [1, 2 * D])
    kvw_d = din("kv_w", [D, 2 * D])
    wq_d = din("diff_w_q", [2, D, D])
    wo_d = din("diff_w_o", [2, D, D])
    lam_d = din("diff_lam", [2, 256])
    sub_d = din("diff_subln_g", [128, 2])
    w1_d = din("mlp_w1", [DEPTH, D, DFF])
    w2_d = din("mlp_w2", [DEPTH, DFF, D])
    k_identf = din("k_identf", [128, 128])
    k_tri = din("k_tri", [128, 128])
    k_perm = din("k_perm", [128, 128])
    k_small = din("k_small", [128, 8])
    k_xirep = din("k_xirep", [128, RH * 512])
    k_mask2 = din("k_mask2", [128, RH * 128])
    out_d = nc.dram_tensor("out", [S, D], F32, kind="ExternalOutput").ap()

    sc = Sched()
    import contextlib
    es = contextlib.ExitStack()
    with es:
        sbt = es.enter_context(nc.sbuf_tensor("SB", [128, SB_BYTES // 4], F32))
        psb = [PBank(es.enter_context(nc.psum_tensor(f"ps{i}", [128, 512], F32)), i) for i in range(8)]
        sems = {e: es.enter_context(nc.semaphore("s_" + e)) for e in Sched.ENG}
        dkeys = [f"slot{i}" for i in range(NSLOT)] + ["misc", "stage0", "stage1", "pos", "bias0", "bias1", "out0", "out1"]
        dsems = {k: es.enter_context(nc.semaphore("d_" + k)) for k in dkeys}
        block = es.enter_context(nc.Block())

        def B(off, n, dt=F32):
            return Buf(sbt, off, n, dt)

        xT = [[B(X0 + (fc * S + t * T) * 4, T) for t in range(NT)] for fc in range(8)]
        slots = [B(SL0 + i * 8192, 4096, BF16) for i in range(NSLOT)]
        identb = B(C0 + 0, 128, BF16)
        onesb = B(C0 + 256, 128, BF16)
        trib = B(C0 + 512, 128, BF16)
        permb = B(C0 + 768, 128, BF16)
        identf = B(C0 + 1024, 128)
        onef = B(C0 + 1536, 4)
        ksm = B(C0 + 1552, 8)
        cT = B(C0 + 1584, 8, BF16)
        c32 = B(C0 + 1600, 8)
        MODS = [dict(modcol=B(C0 + 1664, 48), g0sc=B(C0 + 2368, 8), gga=B(C0 + 2400, 8), g2sc=B(C0 + 2432, 8),
                     ggm=B(C0 + 2464, 8)),
                dict(modcol=B(C0 + 5376, 48), g0sc=B(C0 + 5568, 8), gga=B(C0 + 5600, 8), g2sc=B(C0 + 5632, 8),
                     ggm=B(C0 + 5664, 8))]
        ngb = B(C0 + 1856, 128)
        kvgb = B(C0 + 2560, 8)
        kvgsc = B(C0 + 2592, 8)
        subg = B(C0 + 2720, 2)
        subsc = B(C0 + 2728, 1)
        neglam = B(C0 + 2732, 1)
        lamb = B(C0 + 3072, 256)
        lamt = B(C0 + 4096, 128)
        lams = B(C0 + 4608, 4)
        stg32 = B(C0 + 4864, 128)
        cosT = B(U0, S)
        sinT = B(U0 + 8192, S)
        xirep = B(U0 + 16384, RH * 512)
        R32 = [[B(U0 + 24576 + (h * 2 + dc) * 2048, 512) for dc in range(2)] for h in range(RH)]
        Rbf = [[B(U0 + 40960 + (h * 2 + dc) * 1024, 512, BF16) for dc in range(2)] for h in range(RH)]
        mask2 = B(U0 + 49152, RH * 128)
        o_sb = [B(U0 + 51200 + i * 2048, 512) for i in range(2)]
        osq = [B(U0 + 55296 + i * 1024, 512, BF16) for i in range(2)]
        PTb = [B(U0 + 57344 + i * 256, 128, BF16) for i in range(2)]
        ortmp = [B(U0 + 57856, 128), B(U0 + 63488 + 512, 128)]
        orstd = [B(U0 + 58368 + i * 512, 128) for i in range(2)]
        osd = B(U0 + 59392, 128)
        kTsh = [B(U0 + i * 4096, S, BF16) for i in range(8)]
        vsh = [B(U0 + 32768 + kb * 2048, 1024, BF16) for kb in range(16)]

        def mm(out, lhsT, rhs, start, stop):
            sc.add("pe", lambda e: e.matmul(out.ap, lhsT=lhsT.ap, rhs=rhs.ap, start=start, stop=stop),
                   r=[lhsT, rhs], w=[out])

        def tr(out, in_, ident):
            sc.add("pe", lambda e: e.transpose(out.ap, in_.ap, ident.ap), r=[in_, ident], w=[out])

        def act(out, in_, func, scale=None, bias=None, eng="act"):
            rr = [in_]
            kw = {}
            if scale is not None:
                if isinstance(scale, Opd):
                    rr.append(scale)
                    kw["scale"] = scale.ap
                else:
                    kw["scale"] = scale
            if bias is not None:
                if isinstance(bias, Opd):
                    rr.append(bias)
                    kw["bias"] = bias.ap
                else:
                    kw["bias"] = bias
            sc.add("act", lambda e: e.activation(out=out.ap, in_=in_.ap, func=func, **kw), r=rr, w=[out])

        def tt(out, in0, in1, op, eng="dve"):
            sc.add(eng, lambda e: e.tensor_tensor(out=out.ap, in0=in0.ap, in1=in1.ap, op=op), r=[in0, in1], w=[out])

        def ts(out, in0, s1, s2, op0, op1=None, eng="dve"):
            rr = [in0]
            a1 = s1
            if isinstance(s1, Opd):
                rr.append(s1)
                a1 = s1.ap
            a2 = s2
            if isinstance(s2, Opd):
                rr.append(s2)
                a2 = s2.ap
            if op1 is None:
                sc.add(eng, lambda e: e.tensor_single_scalar(out=out.ap, in_=in0.ap, scalar=a1, op=op0), r=rr, w=[out])
            else:
                sc.add(eng, lambda e: e.tensor_scalar(out=out.ap, in0=in0.ap, scalar1=a1, scalar2=a2, op0=op0, op1=op1),
                       r=rr, w=[out])

        def stt(out, in0, scalar, in1, op0, op1, eng="dve"):
            rr = [in0, in1]
            a = scalar
            if isinstance(scalar, Opd):
                rr.append(scalar)
                a = scalar.ap
            sc.add(eng, lambda e: e.scalar_tensor_tensor(out=out.ap, in0=in0.ap, scalar=a, in1=in1.ap, op0=op0, op1=op1),
                   r=rr, w=[out])

        def cp(out, in_, eng="dve"):
            if eng == "act":
                sc.add("act", lambda e: e.activation(out=out.ap, in_=in_.ap, func=AF.Identity), r=[in_], w=[out])
            else:
                sc.add(eng, lambda e: e.tensor_copy(out=out.ap, in_=in_.ap), r=[in_], w=[out])

        def recip(out, in_):
            sc.add("dve", lambda e: e.reciprocal(out=out.ap, in_=in_.ap), r=[in_], w=[out])

        def memset(out, val, eng="dve"):
            sc.add(eng, lambda e: e.memset(out.ap, val), w=[out])

        def dma(queue, semkey, out_ap, in_ap, r=(), w=()):
            sc.add(queue, lambda e: e.dma_start(out=out_ap, in_=in_ap), r=list(r), w=list(w), dma=semkey)

        piece_ctr = [0]

        def load_piece(srcs):
            i = piece_ctr[0]
            piece_ctr[0] += 1
            sl = slots[i % NSLOT]
            for (src, kcn, W, off, n) in srcs:
                dst = sl.ap.rearrange("p (k n) -> p k n", k=kcn)[:, :, off:off + n]
                dma("pool", f"slot{i % NSLOT}", dst, src, w=[sl.s(k * W + off, k * W + off + n) for k in range(kcn)])
            return sl

        def wview(d2, kcn):
            return d2.rearrange("(k p) n -> p k n", p=128)

        def small_load(dst, src_ap):
            dma("sp", "misc", dst.ap, src_ap, w=[dst.s()])

        small_load(identf, k_identf)
        small_load(ksm, k_small)
        small_load(c32, c_d)
        small_load(ngb, ng_d)
        small_load(kvgb, kvg_d)
        small_load(subg, sub_d)
        small_load(xirep, k_xirep)
        small_load(mask2, k_mask2)
        small_load(stg32, k_tri)
        cp(trib.s(), stg32.s())
        small_load(stg32, k_perm)
        cp(permb.s(), stg32.s())
        cp(identb.s(), identf.s())
        memset(onesb.s(), 1.0)
        memset(onef.s(), 1.0)
        act(cT.s(), c32.s(), AF.Silu)
        invr = ksm.s(0, 1)
        invd = ksm.s(1, 2)
        sgn = ksm.s(2, 3)

        import os
        SKIP = set(os.environ.get("KSKIP", "").split(","))
        stage = [B(WK0 + i * 4096, 1024) for i in range(2)]
        for tc in range(16 if "xload" not in SKIP else 0):
            st = stage[tc % 2]
            dma("sp", f"stage{tc % 2}", st.ap, x_d[tc * 128:(tc + 1) * 128, :], w=[st.s()])
            t, cc = tc // 4, tc % 4
            for half in range(2):
                pb = psb[(tc * 2 + half) % 4]
                for q in range(4):
                    fc = half * 4 + q
                    if "xtr" not in SKIP:
                        tr(pb.s(q * 128, (q + 1) * 128), st.s(fc * 128, (fc + 1) * 128), identf.s())
                    else:
                        mm(pb.s(q * 128, (q + 1) * 128), st.s(fc * 128, (fc + 1) * 128), identf.s(), True, True)
                for q in range(4):
                    fc = half * 4 + q
                    if "xcp" not in SKIP:
                        cp(xT[fc][t].s(cc * 128, (cc + 1) * 128), pb.s(q * 128, (q + 1) * 128),
                           eng="dve" if "xdve" in SKIP else ("act" if ("xact" in SKIP or q % 2) else "dve"))

        def sincos(posf, inv, ncol, cos_out, sin_out, tmpa, tmpb, tmpi, sin_scale=None):
            ang = tmpa
            ts(ang.s(0, ncol), posf.s(0, ncol), inv, None, ALU.mult)
            ts(tmpb.s(0, ncol), ang.s(0, ncol), 1.0 / TWO_PI, None, ALU.mult)
            cp(tmpi.s(0, ncol), tmpb.s(0, ncol))
            cp(tmpb.s(0, ncol), tmpi.s(0, ncol))
            stt(ang.s(0, ncol), tmpb.s(0, ncol), -CW1, ang.s(0, ncol), ALU.mult, ALU.add)
            stt(ang.s(0, ncol), tmpb.s(0, ncol), -CW2, ang.s(0, ncol), ALU.mult, ALU.add)
            ts(tmpb.s(0, ncol), ang.s(0, ncol), -PI_LO, PI_LO, ALU.max, ALU.min)
            act(sin_out.s(0, ncol), tmpb.s(0, ncol), AF.Sin, scale=sin_scale)
            ts(ang.s(0, ncol), ang.s(0, ncol), math.pi / 2.0, None, ALU.add)
            ts(tmpb.s(0, ncol), ang.s(0, ncol), math.pi, TWO_PI, ALU.is_gt, ALU.mult)
            tt(ang.s(0, ncol), ang.s(0, ncol), tmpb.s(0, ncol), ALU.subtract)
            ts(tmpb.s(0, ncol), ang.s(0, ncol), -PI_LO, PI_LO, ALU.max, ALU.min)
            act(cos_out.s(0, ncol), tmpb.s(0, ncol), AF.Sin)

        posi = B(WK0 + 8192, S, I32)
        posf = B(WK0 + 16384, S)
        tA = B(WK0 + 24576, S)
        tB = B(WK0 + 32768, S)
        if "tables" not in SKIP:
            dma("sp", "pos", posi.ap, pos_d.partition_broadcast(128), w=[posi.s()])
            cp(posf.s(), posi.s())
            sincos(posf, invr, S, cosT, sinT, tA, tB, posi)

        def compute_mod(wd, bd_row, ncols, dst):
            sc.phase = "mod"
            MS0 = WK0 + 47104
            rowc = [B(MS0 + i * 1024, 256) for i in range(2)]
            biasc = B(MS0 + 2048, 256)
            pcol = psb[7]
            wv = wview(wd, 8)
            for n in range(ncols // 512):
                sl = load_piece([(wv[:, :, n * 512:(n + 1) * 512], 8, 512, 0, 512)])
                for hh in range(2):
                    c0 = n * 512 + hh * 256
                    dma("sp", "bias0", biasc.ap[0:1, :], bd_row[0:1, c0:c0 + 256], w=[biasc.s(p0=0, p1=1)])
                    pr = psb[hh]
                    for kc in range(8):
                        mm(pr.s(0, 256, 0, 1), cT.s(kc, kc + 1), sl.s(kc * 512 + hh * 256, kc * 512 + hh * 256 + 256),
                           kc == 0, kc == 7)
                    rc_ = rowc[hh]
                    tt(rc_.s(p0=0, p1=1), pr.s(0, 256, 0, 1), biasc.s(p0=0, p1=1), ALU.add)
                    for j in range(2):
                        col = c0 // 128 + j
                        mm(Opd(pcol.ap[:, col:col + 1], pcol.s(0, 128).keys), rc_.s(j * 128, (j + 1) * 128, 0, 1),
                           onef.s(0, 1, 0, 1), True, True)
            cp(dst.s(0, ncols // 128), pcol.s(0, ncols // 128))

        def layer_mod(l):
            P = MODS[l % 2]
            modcol = P["modcol"]
            compute_mod(adaw_d[l], adab_d[l:l + 1, :], 6 * D, modcol)
            base = l * 32
            stt(P["g0sc"].s(), modcol.s(8, 16), 1.0, ngb.s(base + 0, base + 8), ALU.add, ALU.mult)
            stt(P["gga"].s(), modcol.s(16, 24), 1.0, ngb.s(base + 8, base + 16), ALU.add, ALU.mult)
            stt(P["g2sc"].s(), modcol.s(32, 40), 1.0, ngb.s(base + 16, base + 24), ALU.add, ALU.mult)
            stt(P["ggm"].s(), modcol.s(40, 48), 1.0, ngb.s(base + 24, base + 32), ALU.add, ALU.mult)


        nsq = [B(WK0 + i * 1024, T, BF16) for i in range(2)]
        nsd = B(WK0 + 2048, T)
        nrstd = B(WK0 + 4096, T)
        ntmp = [B(WK0 + 6144 + i * 2048, T) for i in range(2)]
        HT0 = WK0 + 10240
        hT = [B(HT0 + fc * 1024, T, BF16) for fc in range(8)]
        WK1 = WK0 + 18432

        def stat_one(i, n, s_, pbank):
            q = nsq[i % 2]
            if i % 2 == 0:
                act(q.s(), s_, AF.Square)
            else:
                tt(q.s(), s_, s_, ALU.mult)
            mm(pbank.s(), onesb.s(), q.s(), i == 0, i == n - 1)

        def stat_fin(pbank, dnorm):
            act(nsd.s(), pbank.s(), AF.Ln, scale=1.0 / dnorm, bias=EPS)
            act(nrstd.s(), nsd.s(), AF.Exp, scale=-0.5)

        def rms_stats(srcs, pbank, dnorm):
            n = len(srcs)
            for i, s_ in enumerate(srcs):
                stat_one(i, n, s_, pbank)
            stat_fin(pbank, dnorm)

        def make_h(t, gsc, shv, sh0):
            sc.phase = "make_h"
            rms_stats([xT[fc][t].s() for fc in range(8)], psb[3], D)
            for fc in range(8):
                tm = ntmp[fc % 2]
                stt(tm.s(), xT[fc][t].s(), gsc.s(fc, fc + 1), nrstd.s(), ALU.mult, ALU.mult)
                act(hT[fc].s(), tm.s(), AF.Identity, bias=shv.s(sh0 + fc, sh0 + fc + 1))

        def resid_update(t, ybuf, gg):
            sc.phase = "resid"
            stat_fin(psb[3], D)
            for oc in range(8):
                tm = ntmp[oc % 2]
                stt(tm.s(), ybuf[oc].s(), gg.s(oc, oc + 1), nrstd.s(), ALU.mult, ALU.mult)
                tt(xT[oc][t].s(), xT[oc][t].s(), tm.s(), ALU.add)

        def mlp_tile(l, t):
            P = MODS[l % 2]
            ggm = P["ggm"]
            make_h(t, P["g2sc"], P["modcol"], 24)
            uT = [B(WK1 + i * 1024, T, BF16) for i in range(8)]
            yb = [B(WK1 + 8192 + oc * 2048, T) for oc in range(8)]
            rtmp = [B(WK1 + 24576 + i * 2048, T) for i in range(2)]
            w1v = wview(w1_d[l], 8)
            for qf in range(4):
                sc.phase = "mlp_w1"
                for half in range(2):
                    c0 = qf * 1024 + half * 512
                    sl = load_piece([(w1v[:, :, c0:c0 + 512], 8, 512, 0, 512)])
                    for j in range(4):
                        pb = psb[(half * 4 + j) % 3]
                        for kc in range(8):
                            mm(pb.s(), sl.s(kc * 512 + j * 128, kc * 512 + (j + 1) * 128), hT[kc].s(), kc == 0, kc == 7)
                        rt = rtmp[j % 2]
                        act(rt.s(), pb.s(), AF.Relu)
                        act(uT[half * 4 + j].s(), rt.s(), AF.Square)
                w2v = w2_d[l][qf * 1024:(qf + 1) * 1024, :].rearrange("(k p) n -> p k n", p=128)
                sc.phase = "mlp_w2"
                for half in range(2):
                    sl = load_piece([(w2v[:, :, half * 512:(half + 1) * 512], 8, 512, 0, 512)])
                    for j in range(4):
                        oc = half * 4 + j
                        pb = psb[4 + (oc % 3)]
                        for fk in range(8):
                            mm(pb.s(), sl.s(fk * 512 + j * 128, fk * 512 + (j + 1) * 128), uT[fk].s(), fk == 0, fk == 7)
                        if qf == 0:
                            cp(yb[oc].s(), pb.s())
                        else:
                            tt(yb[oc].s(), yb[oc].s(), pb.s(), ALU.add)
                        if qf == 3:
                            stat_one(oc, 8, yb[oc].s(), psb[3])
            resid_update(t, yb, ggm)

        def ret_tile(l, t):
            P = MODS[l % 2]
            gga = P["gga"]
            make_h(t, P["g0sc"], P["modcol"], 0)
            qT_ = [B(WK1 + dc * 1024, T, BF16) for dc in range(2)]
            kT_ = [B(WK1 + 2048 + dc * 1024, T, BF16) for dc in range(2)]
            ktok = B(WK1 + 4096, 1024, BF16)
            vb = [B(WK1 + 6144 + c * 1024, 512, BF16) for c in range(4)]
            vz = [B(U0 + 59904 + c * 1024, 512, BF16) for c in range(4)]
            sg = [B(WK1 + 10240 + ec * 1024, T, BF16) for ec in range(4)]
            rp = ntmp
            GT0 = WK1 + 14336
            gatedT = [B(GT0 + i * 1024, T, BF16) for i in range(16)]
            winv = wview(win_d[l], 8)
            cs = cosT.s(t * T, (t + 1) * T)
            sn = sinT.s(t * T, (t + 1) * T)
            for hd in range(RH):
                sc.phase = "ret_qk"
                sl = load_piece([(winv[:, :, hd * 256:(hd + 1) * 256], 8, 512, 0, 256),
                                 (winv[:, :, 1024 + hd * 256:1024 + (hd + 1) * 256], 8, 512, 256, 256)])
                for which, dst in ((0, qT_), (1, kT_)):
                    pbs = [psb[0], psb[1]]
                    for dc in range(2):
                        c0 = which * 256 + dc * 128
                        for kc in range(8):
                            mm(pbs[dc].s(), sl.s(kc * 512 + c0, kc * 512 + c0 + 128), hT[kc].s(), kc == 0, kc == 7)
                    x1, x2 = pbs[0].s(), pbs[1].s()
                    tt(rp[0].s(), x1, cs, ALU.mult)
                    tt(rp[1].s(), x2, sn, ALU.mult)
                    tt(dst[0].s(), rp[0].s(), rp[1].s(), ALU.subtract)
                    tt(rp[0].s(), x2, cs, ALU.mult)
                    tt(rp[1].s(), x1, sn, ALU.mult)
                    tt(dst[1].s(), rp[0].s(), rp[1].s(), ALU.add)
                sc.phase = "ret_ktr"
                pbt = psb[2]
                for c in range(4):
                    for dc in range(2):
                        o0 = (c * 2 + dc) * 128
                        tr(pbt.sb16(o0, o0 + 128), kT_[dc].s(c * 128, (c + 1) * 128), identb.s())
                cp(ktok.s(), pbt.sb16(), eng="act")
                sc.phase = "ret_v"
                sl = load_piece([(winv[:, :, 2048 + hd * 512:2048 + (hd + 1) * 512], 8, 512, 0, 512)])
                for c in range(4):
                    pb = psb[c % 2]
                    for kc in range(8):
                        mm(pb.s(), hT[kc].s(c * 128, (c + 1) * 128), sl.s(kc * 512, (kc + 1) * 512), kc == 0, kc == 7)
                    cp(vb[c].s(), pb.s(), eng="act")
                    act(vz[c].s(), pb.s(), AF.Copy, scale=ksm.s(3 + hd, 4 + hd))
                sc.phase = "ret_g"
                sl = load_piece([(winv[:, :, 4096 + hd * 512:4096 + (hd + 1) * 512], 8, 512, 0, 512)])
                for ec in range(4):
                    pb = psb[ec % 2]
                    for kc in range(8):
                        mm(pb.s(), sl.s(kc * 512 + ec * 128, kc * 512 + (ec + 1) * 128), hT[kc].s(), kc == 0, kc == 7)
                    act(sg[ec].s(), pb.s(), AF.Silu)
                sc.phase = "ret_chunk"
                gC = CN["gC"][hd]

                def finish(c, n):
                    csl = (c * 128, (c + 1) * 128)
                    ob = o_sb[n % 2]
                    oq = osq[n % 2]
                    pst = psb[n % 2].s(0, 128)
                    for ec in range(4):
                        mm(pst, onesb.s(), oq.s(ec * 128, (ec + 1) * 128), ec == 0, ec == 3)
                    act(osd.s(), pst, AF.Ln, scale=1.0 / 512.0, bias=EPS)
                    rs = orstd[n % 2]
                    act(rs.s(), osd.s(), AF.Exp, scale=-0.5)
                    for ec in range(4):
                        tt(ortmp[ec % 2].s(), ob.s(ec * 128, (ec + 1) * 128), rs.s(), ALU.mult)
                        tt(gatedT[hd * 4 + ec].s(*csl), ortmp[ec % 2].s(), sg[ec].s(*csl), ALU.mult)

                for c in range(4):
                    n = t * 4 + c
                    csl = (c * 128, (c + 1) * 128)
                    ps_s = psb[2 + (n % 2)].s(0, 128)
                    for dc in range(2):
                        mm(ps_s, kT_[dc].s(*csl), qT_[dc].s(*csl), dc == 0, dc == 1)
                    PT = PTb[n % 2]
                    tt(PT.s(), ps_s, mask2.s(hd * 128, (hd + 1) * 128), ALU.mult)
                    if n < 15:
                        for dc in range(2):
                            pa = psb[6 + dc]
                            mm(pa.s(), ktok.s((c * 2 + dc) * 128, (c * 2 + dc + 1) * 128), vz[c].s(), True, True)
                    po = psb[4 + (n % 2)]
                    for ec in range(4):
                        osl = po.s(ec * 128, (ec + 1) * 128)
                        mm(osl, vb[c].s(ec * 128, (ec + 1) * 128), PT.s(), True, n == 0)
                        if n > 0:
                            for dc in range(2):
                                mm(osl, Rbf[hd][dc].s(ec * 128, (ec + 1) * 128), qT_[dc].s(*csl), False, dc == 1)
                    if n < 15:
                        for dc in range(2):
                            pa = psb[6 + dc]
                            if n == 0:
                                cp(R32[hd][dc].s(), pa.s())
                            else:
                                stt(R32[hd][dc].s(), R32[hd][dc].s(), gC, pa.s(), ALU.mult, ALU.add)
                            cp(Rbf[hd][dc].s(), R32[hd][dc].s(), eng="act")
                    if c > 0:
                        finish(c - 1, n - 1)
                    ob = o_sb[n % 2]
                    tt(ob.s(), po.s(), xirep.s(hd * 512, (hd + 1) * 512), ALU.mult)
                    act(osq[n % 2].s(), ob.s(), AF.Square)
                finish(3, t * 4 + 3)
            sc.phase = "ret_wout"
            yb = [B(HT0 + oc * 2048, T) for oc in range(8)]
            woutv = wout_d[l].rearrange("(k p) n -> p k n", p=128)
            for pi_ in range(4):
                sl = load_piece([(woutv[:, :, pi_ * 256:(pi_ + 1) * 256], 16, 256, 0, 256)])
                for j in range(2):
                    oc = pi_ * 2 + j
                    pb = psb[oc % 3]
                    for kc in range(16):
                        mm(pb.s(), sl.s(kc * 256 + j * 128, kc * 256 + (j + 1) * 128), gatedT[kc].s(), kc == 0, kc == 15)
                    cp(yb[oc].s(), pb.s())
                    stat_one(oc, 8, yb[oc].s(), psb[3])
            resid_update(t, yb, gga)

        DT0 = WK1
        dC = B(DT0, T)
        dS = B(DT0 + 2048, T)
        dposi = B(DT0 + 4096, T, I32)
        dposf = B(DT0 + 6144, T)
        dtA = B(DT0 + 8192, T)
        dtB = B(DT0 + 10240, T)
        qraw = [B(DT0 + 4096 + i * 1024, T, BF16) for i in range(2)]
        drp = [B(DT0 + 8192 + i * 2048, T) for i in range(2)]

        def diff_tables(t):
            sc.phase = "dtables"
            dma("sp", "pos", dposi.ap, pos_d[0:1, t * T:(t + 1) * T].partition_broadcast(128), w=[dposi.s()])
            cp(dposf.s(), dposi.s())
            sincos(dposf, invd, T, dC, dS, dtA, dtB, dposi, sin_scale=sgn)

        def rope_diff(pb, dst, idx):
            qr = qraw[idx % 2]
            cp(qr.s(), pb.s(), eng="act")
            p2 = psb[3]
            mm(p2.s(), permb.s(), qr.s(), True, True)
            tt(drp[0].s(), pb.s(), dC.s(), ALU.mult)
            tt(drp[1].s(), p2.s(), dS.s(), ALU.mult)
            tt(dst, drp[0].s(), drp[1].s(), ALU.add)

        def kv_tile(t):
            make_h(t, kvgsc, kvmod, 0)
            diff_tables(t)
            kvv = wview(kvw_d, 8)
            sc.phase = "kv"
            for half in range(2):
                sl = load_piece([(kvv[:, :, half * 512:(half + 1) * 512], 8, 512, 0, 512)])
                for j in range(4):
                    i = half * 4 + j
                    pb = psb[i % 3]
                    for kc in range(8):
                        mm(pb.s(), sl.s(kc * 512 + j * 128, kc * 512 + (j + 1) * 128), hT[kc].s(), kc == 0, kc == 7)
                    rope_diff(pb, kTsh[i].s(t * T, (t + 1) * T), i)
            for half in range(2):
                sl = load_piece([(kvv[:, :, 1024 + half * 512:1024 + (half + 1) * 512], 8, 512, 0, 512)])
                for c in range(4):
                    pb = psb[c % 3]
                    for kc in range(8):
                        mm(pb.s(), hT[kc].s(c * 128, (c + 1) * 128), sl.s(kc * 512, (kc + 1) * 512), kc == 0, kc == 7)
                    cp(vsh[t * 4 + c].s(half * 512, (half + 1) * 512), pb.s(), eng="act" if c % 2 else "dve")

        def diff_tile(l, t):
            j_ = l - 2
            P = MODS[l % 2]
            gga = P["gga"]
            make_h(t, P["g0sc"], P["modcol"], 0)
            diff_tables(t)
            QT0 = DT0 + 12288
            qT_ = [B(QT0 + i * 1024, T, BF16) for i in range(8)]
            OT_ = [B(QT0 + 8192 + i * 1024, T, BF16) for i in range(8)]
            E0 = DT0
            Eb = [B(E0 + i * 1024, T, BF16) for i in range(4)]
            Zr = [B(E0 + 4096 + i * 2048, T) for i in range(2)]
            t01 = [B(E0 + 8192 + i * 2048, T) for i in range(2)]
            ofp = ntmp[0]
            wqv = wview(wq_d[j_], 8)
            sc.phase = "diff_q"
            for half in range(2):
                sl = load_piece([(wqv[:, :, half * 512:(half + 1) * 512], 8, 512, 0, 512)])
                for j in range(4):
                    i = half * 4 + j
                    pb = psb[i % 3]
                    for kc in range(8):
                        mm(pb.s(), sl.s(kc * 512 + j * 128, kc * 512 + (j + 1) * 128), hT[kc].s(), kc == 0, kc == 7)
                    rope_diff(pb, qT_[i].s(), i)
            nkb = 4 * t + 4
            sc.phase = "diff_attn"
            Ob = [psb[4], psb[5]]
            Zb = [psb[6], psb[7]]
            pairs = [(i, kb) for i in range(8) for kb in range(nkb)]

            def emitS(pi_):
                i, kb = pairs[pi_]
                r = kb - 4 * t
                q0 = 0 if r < 0 else r * 128
                for a in range(2):
                    pS = psb[2 * (pi_ % 2) + a]
                    mm(pS.s(q0, T), kTsh[i].s(kb * 128, (kb + 1) * 128, 64 * a, 64 * a + 64),
                       qT_[i].s(q0, T, 64 * a, 64 * a + 64), True, True)
                for a in range(2):
                    pS = psb[2 * (pi_ % 2) + a]
                    Et = Eb[2 * (pi_ % 2) + a]
                    act(Et.s(q0, T), pS.s(q0, T), AF.Exp, scale=0.125)
                    if r >= 0:
                        tt(Et.s(q0, q0 + 128), Et.s(q0, q0 + 128), trib.s(), ALU.mult)

            def emitPV(pi_):
                i, kb = pairs[pi_]
                r = kb - 4 * t
                q0 = 0 if r < 0 else r * 128
                for a in range(2):
                    Et = Eb[2 * (pi_ % 2) + a]
                    mm(Ob[a].s(q0, T), vsh[kb].s(i * 128, (i + 1) * 128), Et.s(q0, T), kb == 0, kb == nkb - 1)
                    mm(Zb[a].s(q0, T), onesb.s(), Et.s(q0, T), kb == 0, kb == nkb - 1)
                if kb == nkb - 1:
                    combine(i)

            def combine(i):
                for a in range(2):
                    act(Zr[a].s(), Zb[a].s(), AF.Ln)
                    act(Zr[a].s(), Zr[a].s(), AF.Exp, scale=-1.0)
                    tt(t01[a].s(), Ob[a].s(), Zr[a].s(), ALU.mult)
                stt(ofp.s(), t01[1].s(), neglam.s(), t01[0].s(), ALU.mult, ALU.add)
                q = nsq[i % 2]
                act(q.s(), ofp.s(), AF.Square)
                pst = psb[3]
                mm(pst.s(), onesb.s(), q.s(), True, True)
                act(nsd.s(), pst.s(), AF.Ln, scale=1.0 / 128.0, bias=EPS)
                act(nrstd.s(), nsd.s(), AF.Exp, scale=-0.5)
                stt(OT_[i].s(), ofp.s(), subsc.s(), nrstd.s(), ALU.mult, ALU.mult)

            NU = len(pairs)
            LOOK = 1
            for u in range(min(LOOK, NU)):
                emitS(u)
            for u in range(NU):
                if u + LOOK < NU:
                    emitS(u + LOOK)
                emitPV(u)
            Y0 = DT0
            yb = [B(Y0 + oc * 2048, T) for oc in range(8)]
            wov = wview(wo_d[j_], 8)
            sc.phase = "diff_wo"
            for half in range(2):
                sl = load_piece([(wov[:, :, half * 512:(half + 1) * 512], 8, 512, 0, 512)])
                for j in range(4):
                    oc = half * 4 + j
                    pb = psb[oc % 3]
                    for kc in range(8):
                        mm(pb.s(), sl.s(kc * 512 + j * 128, kc * 512 + (j + 1) * 128), OT_[kc].s(), kc == 0, kc == 7)
                    cp(yb[oc].s(), pb.s())
                    stat_one(oc, 8, yb[oc].s(), psb[3])
            resid_update(t, yb, gga)

        kvmod = B(C0 + 2624, 16)
        layer_mod(0)
        for l in range(min(nlayers, 2)):
            for t in range(NT):
                ret_tile(l, t)
                if t == 1 and l + 1 < nlayers:
                    layer_mod(l + 1)
                mlp_tile(l, t)
        if nlayers > 2:
            compute_mod(kvaw_d, kvab_d, 2 * D, kvmod)
            stt(kvgsc.s(), kvmod.s(8, 16), 1.0, kvgb.s(), ALU.add, ALU.mult)
            for t in range(NT):
                kv_tile(t)
            for l in range(2, nlayers):
                j_ = l - 2
                lam_init = 0.8 - 0.6 * math.exp(-0.3 * l)
                dma("sp", "misc", lamb.ap, lam_d[j_:j_ + 1, :].partition_broadcast(128), w=[lamb.s()])
                tt(lamt.s(0, 64), lamb.s(0, 64), lamb.s(64, 128), ALU.mult)
                tt(lamt.s(64, 128), lamb.s(128, 192), lamb.s(192, 256), ALU.mult)
                sc.add("dve", lambda e: e.reduce_sum(out=lams.ap[:, 0:1], in_=lamt.ap[:, 0:64], axis=mybir.AxisListType.X),
                       r=[lamt.s()], w=[lams.s()])
                sc.add("dve", lambda e: e.reduce_sum(out=lams.ap[:, 1:2], in_=lamt.ap[:, 64:128], axis=mybir.AxisListType.X),
                       r=[lamt.s()], w=[lams.s()])
                act(lams.s(2, 4), lams.s(0, 2), AF.Exp)
                tt(neglam.s(), lams.s(3, 4), lams.s(2, 3), ALU.subtract)
                ts(neglam.s(), neglam.s(), -lam_init, None, ALU.add)
                ts(subsc.s(), subg.s(j_, j_ + 1), 1.0 - lam_init, None, ALU.mult)
                for t in range(NT):
                    diff_tile(l, t)
                    if t == 1 and l + 1 < nlayers:
                        layer_mod(l + 1)
                    mlp_tile(l, t)

        sc.phase = "out"
        for tc in range(16):
            st = stage[tc % 2]
            t, cc = tc // 4, tc % 4
            for half in range(2):
                pb = psb[(tc * 2 + half) % 4]
                for q in range(4):
                    fc = half * 4 + q
                    tr(pb.s(q * 128, (q + 1) * 128), xT[fc][t].s(cc * 128, (cc + 1) * 128), identf.s())
                cp(st.s(half * 512, (half + 1) * 512), pb.s(), eng="act" if half else "dve")
            dma("sp", f"out{tc % 2}", out_d[tc * 128:(tc + 1) * 128, :], st.ap, r=[st.s()])

        sc.emit(nc, sems, dsems, block)
    return nc, sc


_NC_CACHE = {}


def _prep_inputs(inputs, b, CN):
    f = lambda a: np.ascontiguousarray(np.asarray(a, dtype=np.float32))
    m = {}
    m["x"] = f(inputs["x"][b])
    m["c"] = f(np.asarray(inputs["c"][b]).reshape(8, 128).T)
    m["pos"] = np.ascontiguousarray(np.asarray(inputs["positions"][b], dtype=np.int32).reshape(1, S))
    ng = np.asarray(inputs["norm_g"], dtype=np.float32).reshape(4, 4, 8, 128)
    m["norm_g"] = f(ng.transpose(3, 0, 1, 2).reshape(128, 128))
    m["ada_w"] = f(inputs["ada_w"])
    m["ada_b"] = f(inputs["ada_b"])
    m["ret_w_in"] = f(inputs["ret_w_in"])
    m["ret_w_out"] = f(inputs["ret_w_out"])
    m["kv_norm_g"] = f(np.asarray(inputs["kv_norm_g"]).reshape(8, 128).T)
    m["kv_ada_w"] = f(inputs["kv_ada_w"])
    m["kv_ada_b"] = f(np.asarray(inputs["kv_ada_b"]).reshape(1, 2 * D))
    m["kv_w"] = f(inputs["kv_w"])
    m["diff_w_q"] = f(inputs["diff_w_q"])
    m["diff_w_o"] = f(inputs["diff_w_o"])
    m["diff_lam"] = f(np.asarray(inputs["diff_lam"]).reshape(2, 256))
    m["diff_subln_g"] = f(np.asarray(inputs["diff_subln_g"]).T)
    m["mlp_w1"] = f(inputs["mlp_w1"])
    m["mlp_w2"] = f(inputs["mlp_w2"])
    m["k_identf"] = CN["identf"]
    m["k_tri"] = CN["tri"]
    m["k_perm"] = CN["perm"]
    sm = np.zeros((128, 8), np.float32)
    sm[:, 0:1] = CN["invr"]
    sm[:, 1:2] = CN["invd"]
    sm[:, 2:3] = CN["sgn"]
    sm[:, 3:7] = CN["zeta"]
    m["k_small"] = sm
    m["k_xirep"] = CN["xirep"]
    m["k_mask2"] = CN["mask2"]
    return m


def kernel(**inputs):
    CN = _consts()
    if "nc" not in _NC_CACHE:
        _NC_CACHE["nc"] = build(DEPTH)[0]
    nc = _NC_CACHE["nc"]
    shared = None
    in_maps = []
    for b in range(8):
        m = _prep_inputs(inputs, b, CN) if shared is None else None
        if shared is None:
            shared = m
            in_maps.append(m)
        else:
            mm_ = dict(shared)
            mm_["x"] = np.ascontiguousarray(np.asarray(inputs["x"][b], dtype=np.float32))
            mm_["c"] = np.ascontiguousarray(np.asarray(inputs["c"][b], dtype=np.float32).reshape(8, 128).T)
            mm_["pos"] = np.ascontiguousarray(np.asarray(inputs["positions"][b], dtype=np.int32).reshape(1, S))
            in_maps.append(mm_)
    res = run_bass_kernel_spmd(nc, in_maps, core_ids=list(range(8)))
    out = np.stack([np.asarray(res.results[b]["out"], dtype=np.float32) for b in range(8)], axis=0)
    return out
```

```python
import math
import numpy as np
import concourse.bass as bass
import concourse.mybir as mybir
from concourse.bass_utils import run_bass_kernel_spmd

F32 = mybir.dt.float32
BF16 = mybir.dt.bfloat16
I32 = mybir.dt.int32
AF = mybir.ActivationFunctionType
ALU = mybir.AluOpType

S = 2048
D = 1024
T = 512
NT = S // T
DFF = 4096
EPS = 1e-6
RH = 4
DEPTH = 4
NSLOT = 3
PG = 256

X0 = 0
U0 = 65536
SL0 = 131072
C0 = SL0 + NSLOT * 8192
WK0 = C0 + 6144
SB_BYTES = 207 * 1024
WK_BYTES = SB_BYTES - WK0


class Opd:
    __slots__ = ("ap", "keys")

    def __init__(self, ap, keys):
        self.ap = ap
        self.keys = keys


class Buf:
    def __init__(self, sbt, off, n, dt):
        self.esz = 4 if dt in (F32, I32) else 2
        assert off % 4 == 0
        self.off = off
        self.n = n
        self.dt = dt
        w0 = off // 4
        w1 = (off + n * self.esz + 3) // 4
        ap = sbt[:, w0:w1]
        if dt != F32:
            ap = ap.bitcast(dt)
        self.ap = ap

    def s(self, a=0, b=None, p0=0, p1=128):
        if b is None:
            b = self.n
        k0 = (self.off + a * self.esz) // PG
        k1 = (self.off + b * self.esz - 1) // PG
        return Opd(self.ap[p0:p1, a:b], [("sb", k) for k in range(k0, k1 + 1)])


class PBank:
    def __init__(self, pt, idx):
        self.ap = pt[:]
        self.idx = idx
        self.apb = pt[:].bitcast(BF16)

    def s(self, a=0, b=512, p0=0, p1=128):
        return Opd(self.ap[p0:p1, a:b], [("ps", self.idx)])

    def sb16(self, a=0, b=1024, p0=0, p1=128):
        return Opd(self.apb[p0:p1, a:b], [("ps", self.idx)])


class Sched:
    ENG = ("pe", "act", "dve", "pool", "sp")

    def __init__(self):
        self.ops = []
        self.lastw = {}
        self.readers = {}
        self.dma_cum = {}
        self.phase = "pro"

    def add(self, eng, fn, r=(), w=(), dma=None):
        deps = set()
        for o in r:
            for k in o.keys:
                lw = self.lastw.get(k)
                if lw is not None:
                    deps.add(lw)
                if k[0] == "ps":
                    rd = self.readers.get(k)
                    if rd:
                        deps.update(x for x in rd if self.ops[x]["eng"] != eng)
        for o in w:
            for k in o.keys:
                lw = self.lastw.get(k)
                if lw is not None:
                    deps.add(lw)
                rd = self.readers.get(k)
                if rd:
                    deps.update(rd)
        oid = len(self.ops)
        cdeps = {}
        ddeps = {}
        for d in deps:
            od = self.ops[d]
            if od["dma"] is not None:
                sk = od["dma"]
                ddeps[sk] = self.dma_cum[sk]
            else:
                if od["eng"] == "pe" and eng == "pe":
                    continue
                e = od["eng"]
                if e not in cdeps or cdeps[e] < d:
                    cdeps[e] = d
        if dma is not None:
            self.dma_cum[dma] = self.dma_cum.get(dma, 0) + 16
        self.ops.append(dict(eng=eng, fn=fn, cdeps=cdeps, ddeps=ddeps, dma=dma, sig=False, ph=self.phase))
        for o in r:
            for k in o.keys:
                self.readers.setdefault(k, set()).add(oid)
        for o in w:
            for k in o.keys:
                self.lastw[k] = oid
                self.readers[k] = set()
        return oid

    def emit(self, nc, sems, dsems, block):
        ops = self.ops
        for op in ops:
            for e, d in op["cdeps"].items():
                ops[d]["sig"] = True
        cnt = {e: 0 for e in self.ENG}
        for op in ops:
            if op["dma"] is None and op["sig"]:
                cnt[op["eng"]] += 1
                op["sigval"] = cnt[op["eng"]]
        self.sigcounts = dict(cnt)
        per = {e: [] for e in self.ENG}
        for op in ops:
            per[op["eng"]].append(op)

        def run(ename, eng):
            waited = {}
            for op in per[ename]:
                for e, d in op["cdeps"].items():
                    v = ops[d]["sigval"]
                    if waited.get(e, 0) < v:
                        eng.wait_ge(sems[e], v)
                        waited[e] = v
                for sk, v in op["ddeps"].items():
                    if waited.get(sk, 0) < v:
                        eng.wait_ge(dsems[sk], v)
                        waited[sk] = v
                ins = op["fn"](eng)
                if op["dma"] is not None:
                    ins.then_inc(dsems[op["dma"]], 16)
                elif op["sig"]:
                    ins.then_inc(sems[ename], 1)
            if ename == "sp":
                for sk, v in self.dma_cum.items():
                    if sk.startswith("out"):
                        eng.wait_ge(dsems[sk], v)

        @block.tensor
        def _(e):
            run("pe", e)

        @block.scalar
        def _(e):
            run("act", e)

        @block.vector
        def _(e):
            run("dve", e)

        @block.gpsimd
        def _(e):
            run("pool", e)

        @block.sync
        def _(e):
            run("sp", e)


def _consts():
    c = {}
    c["identf"] = np.eye(128, dtype=np.float32)
    tri = (np.arange(128)[None, :] >= np.arange(128)[:, None]).astype(np.float32)
    c["tri"] = tri
    perm = np.zeros((128, 128), np.float32)
    invd = np.zeros((128, 1), np.float32)
    sgn = np.ones((128, 1), np.float32)
    fr = (500000.0 ** (-np.arange(0, 16, 2, dtype=np.float32) / np.float32(16))).astype(np.float32)
    for p in range(128):
        dd = p % 64
        if dd < 8:
            perm[p + 8, p] = 1.0
            invd[p, 0] = fr[dd]
            sgn[p, 0] = -1.0
        elif dd < 16:
            perm[p - 8, p] = 1.0
            invd[p, 0] = fr[dd - 8]
            sgn[p, 0] = 1.0
    c["perm"] = perm
    c["invd"] = invd
    c["sgn"] = sgn
    c["invr"] = (10000.0 ** (-np.arange(0, 256, 2, dtype=np.float32) / np.float32(256))).astype(np.float32).reshape(128, 1)
    gam = 1.0 - 2.0 ** (-5.0 - np.arange(RH, dtype=np.float64))
    lg = np.log(gam)
    idx = np.arange(128, dtype=np.float64)
    xi = np.exp((idx + 1.0)[None, :] * lg[:, None])
    zeta = np.exp((127.0 - idx)[None, :] * lg[:, None])
    c["gC"] = [float(np.exp(128.0 * lg[h])) for h in range(RH)]
    xirep = np.tile(xi[:, None, :], (1, 4, 1)).reshape(RH * 512)
    c["xirep"] = np.broadcast_to(xirep[None, :], (128, RH * 512)).astype(np.float32).copy()
    c["zeta"] = (zeta.T / 16.0).astype(np.float32).copy()
    m2 = np.zeros((128, RH, 128), np.float64)
    for h in range(RH):
        m2[:, h, :] = tri * np.exp(-(idx + 1.0) * lg[h])[:, None] / 16.0
    c["mask2"] = m2.reshape(128, RH * 128).astype(np.float32)
    return c


TWO_PI = 2.0 * math.pi
CW1 = 6.28125
CW2 = TWO_PI - CW1
PI_LO = 3.1415925


def build(nlayers=DEPTH):
    CN = _consts()
    nc = bass.Bass("TRN2", target_bir_lowering=False)
    dram = {}

    def din(name, shape, dt=F32):
        dram[name] = nc.dram_tensor(name, list(shape), dt, kind="ExternalInput").ap()
        return dram[name]

    x_d = din("x", [S, D])
    c_d = din("c", [128, 8])
    pos_d = din("pos", [1, S], I32)
    ng_d = din("norm_g", [128, 16 * 8])
    adaw_d = din("ada_w", [DEPTH, D, 6 * D])
    adab_d = din("ada_b", [DEPTH, 6 * D])
    win_d = din("ret_w_in", [2, D, 6144])
    wout_d = din("ret_w_out", [2, 2048, D])
    kvg_d = din("kv_norm_g", [128, 8])
    kvaw_d = din("kv_ada_w", [D, 2 * D])
    kvab_d = din("kv_ada_b", [1, 2 * D])
    kvw_d = din("kv_w", [D, 2 * D])
    wq_d = din("diff_w_q", [2, D, D])
    wo_d = din("diff_w_o", [2, D, D])
    lam_d = din("diff_lam", [2, 256])
    sub_d = din("diff_subln_g", [128, 2])
    w1_d = din("mlp_w1", [DEPTH, D, DFF])
    w2_d = din("mlp_w2", [DEPTH, DFF, D])
    k_identf = din("k_identf", [128, 128])
    k_tri = din("k_tri", [128, 128])
    k_perm = din("k_perm", [128, 128])
    k_small = din("k_small", [128, 8])
    k_xirep = din("k_xirep", [128, RH * 512])
    k_mask2 = din("k_mask2", [128, RH * 128])
    out_d = nc.dram_tensor("out", [S, D], F32, kind="ExternalOutput").ap()

    sc = Sched()
    import contextlib
    es = contextlib.ExitStack()
    with es:
        sbt = es.enter_context(nc.sbuf_tensor("SB", [128, SB_BYTES // 4], F32))
        psb = [PBank(es.enter_context(nc.psum_tensor(f"ps{i}", [128, 512], F32)), i) for i in range(8)]
        sems = {e: es.enter_context(nc.semaphore("s_" + e)) for e in Sched.ENG}
        dkeys = [f"slot{i}" for i in range(NSLOT)] + ["misc", "stage0", "stage1", "pos", "bias0", "bias1", "out0", "out1"]
        dsems = {k: es.enter_context(nc.semaphore("d_" + k)) for k in dkeys}
        block = es.enter_context(nc.Block())

        def B(off, n, dt=F32):
            return Buf(sbt, off, n, dt)

        xT = [[B(X0 + (fc * S + t * T) * 4, T) for t in range(NT)] for fc in range(8)]
        slots = [B(SL0 + i * 8192, 4096, BF16) for i in range(NSLOT)]
        identb = B(C0 + 0, 128, BF16)
        onesb = B(C0 + 256, 128, BF16)
        trib = B(C0 + 512, 128, BF16)
        permb = B(C0 + 768, 128, BF16)
        identf = B(C0 + 1024, 128)
        onef = B(C0 + 1536, 4)
        ksm = B(C0 + 1552, 8)
        cT = B(C0 + 1584, 8, BF16)
        c32 = B(C0 + 1600, 8)
        MODS = [dict(modcol=B(C0 + 1664, 48), g0sc=B(C0 + 2368, 8), gga=B(C0 + 2400, 8), g2sc=B(C0 + 2432, 8),
                     ggm=B(C0 + 2464, 8)),
                dict(modcol=B(C0 + 5376, 48), g0sc=B(C0 + 5568, 8), gga=B(C0 + 5600, 8), g2sc=B(C0 + 5632, 8),
                     ggm=B(C0 + 5664, 8))]
        ngb = B(C0 + 1856, 128)
        kvgb = B(C0 + 2560, 8)
        kvgsc = B(C0 + 2592, 8)
        subg = B(C0 + 2720, 2)
        subsc = B(C0 + 2728, 1)
        neglam = B(C0 + 2732, 1)
        lamb = B(C0 + 3072, 256)
        lamt = B(C0 + 4096, 128)
        lams = B(C0 + 4608, 4)
        stg32 = B(C0 + 4864, 128)
        cosT = B(U0, S)
        sinT = B(U0 + 8192, S)
        xirep = B(U0 + 16384, RH * 512)
        R32 = [[B(U0 + 24576 + (h * 2 + dc) * 2048, 512) for dc in range(2)] for h in range(RH)]
        Rbf = [[B(U0 + 40960 + (h * 2 + dc) * 1024, 512, BF16) for dc in range(2)] for h in range(RH)]
        mask2 = B(U0 + 49152, RH * 128)
        o_sb = [B(U0 + 51200 + i * 2048, 512) for i in range(2)]
        osq = [B(U0 + 55296 + i * 1024, 512, BF16) for i in range(2)]
        PTb = [B(U0 + 57344 + i * 256, 128, BF16) for i in range(2)]
        ortmp = [B(U0 + 57856, 128), B(U0 + 63488 + 512, 128)]
        orstd = [B(U0 + 58368 + i * 512, 128) for i in range(2)]
        osd = B(U0 + 59392, 128)
        kTsh = [B(U0 + i * 4096, S, BF16) for i in range(8)]
        vsh = [B(U0 + 32768 + kb * 2048, 1024, BF16) for kb in range(16)]

        def mm(out, lhsT, rhs, start, stop):
            sc.add("pe", lambda e: e.matmul(out.ap, lhsT=lhsT.ap, rhs=rhs.ap, start=start, stop=stop),
                   r=[lhsT, rhs], w=[out])

        def tr(out, in_, ident):
            sc.add("pe", lambda e: e.transpose(out.ap, in_.ap, ident.ap), r=[in_, ident], w=[out])

        def act(out, in_, func, scale=None, bias=None, eng="act"):
            rr = [in_]
            kw = {}
            if scale is not None:
                if isinstance(scale, Opd):
                    rr.append(scale)
                    kw["scale"] = scale.ap
                else:
                    kw["scale"] = scale
            if bias is not None:
                if isinstance(bias, Opd):
                    rr.append(bias)
                    kw["bias"] = bias.ap
                else:
                    kw["bias"] = bias
            sc.add("act", lambda e: e.activation(out=out.ap, in_=in_.ap, func=func, **kw), r=rr, w=[out])

        def tt(out, in0, in1, op, eng="dve"):
            sc.add(eng, lambda e: e.tensor_tensor(out=out.ap, in0=in0.ap, in1=in1.ap, op=op), r=[in0, in1], w=[out])

        def ts(out, in0, s1, s2, op0, op1=None, eng="dve"):
            rr = [in0]
            a1 = s1
            if isinstance(s1, Opd):
                rr.append(s1)
                a1 = s1.ap
            a2 = s2
            if isinstance(s2, Opd):
                rr.append(s2)
                a2 = s2.ap
            if op1 is None:
                sc.add(eng, lambda e: e.tensor_single_scalar(out=out.ap, in_=in0.ap, scalar=a1, op=op0), r=rr, w=[out])
            else:
                sc.add(eng, lambda e: e.tensor_scalar(out=out.ap, in0=in0.ap, scalar1=a1, scalar2=a2, op0=op0, op1=op1),
                       r=rr, w=[out])

        def stt(out, in0, scalar, in1, op0, op1, eng="dve"):
            rr = [in0, in1]
            a = scalar
            if isinstance(scalar, Opd):
                rr.append(scalar)
                a = scalar.ap
            sc.add(eng, lambda e: e.scalar_tensor_tensor(out=out.ap, in0=in0.ap, scalar=a, in1=in1.ap, op0=op0, op1=op1),
                   r=rr, w=[out])

        def cp(out, in_, eng="dve"):
            if eng == "act":
                sc.add("act", lambda e: e.activation(out=out.ap, in_=in_.ap, func=AF.Identity), r=[in_], w=[out])
            else:
                sc.add(eng, lambda e: e.tensor_copy(out=out.ap, in_=in_.ap), r=[in_], w=[out])

        def recip(out, in_):
            sc.add("dve", lambda e: e.reciprocal(out=out.ap, in_=in_.ap), r=[in_], w=[out])

        def memset(out, val, eng="dve"):
            sc.add(eng, lambda e: e.memset(out.ap, val), w=[out])

        def dma(queue, semkey, out_ap, in_ap, r=(), w=()):
            sc.add(queue, lambda e: e.dma_start(out=out_ap, in_=in_ap), r=list(r), w=list(w), dma=semkey)

        piece_ctr = [0]

        def load_piece(srcs):
            i = piece_ctr[0]
            piece_ctr[0] += 1
            sl = slots[i % NSLOT]
            for (src, kcn, W, off, n) in srcs:
                dst = sl.ap.rearrange("p (k n) -> p k n", k=kcn)[:, :, off:off + n]
                dma("pool", f"slot{i % NSLOT}", dst, src, w=[sl.s(k * W + off, k * W + off + n) for k in range(kcn)])
            return sl

        def wview(d2, kcn):
            return d2.rearrange("(k p) n -> p k n", p=128)

        def small_load(dst, src_ap):
            dma("sp", "misc", dst.ap, src_ap, w=[dst.s()])

        small_load(identf, k_identf)
        small_load(ksm, k_small)
        small_load(c32, c_d)
        small_load(ngb, ng_d)
        small_load(kvgb, kvg_d)
        small_load(subg, sub_d)
        small_load(xirep, k_xirep)
        small_load(mask2, k_mask2)
        small_load(stg32, k_tri)
        cp(trib.s(), stg32.s())
        small_load(stg32, k_perm)
        cp(permb.s(), stg32.s())
        cp(identb.s(), identf.s())
        memset(onesb.s(), 1.0)
        memset(onef.s(), 1.0)
        act(cT.s(), c32.s(), AF.Silu)
        invr = ksm.s(0, 1)
        invd = ksm.s(1, 2)
        sgn = ksm.s(2, 3)

        import os
        SKIP = set(os.environ.get("KSKIP", "").split(","))
        stage = [B(WK0 + i * 4096, 1024) for i in range(2)]
        for tc in range(16 if "xload" not in SKIP else 0):
            st = stage[tc % 2]
            dma("sp", f"stage{tc % 2}", st.ap, x_d[tc * 128:(tc + 1) * 128, :], w=[st.s()])
            t, cc = tc // 4, tc % 4
            for half in range(2):
                pb = psb[(tc * 2 + half) % 4]
                for q in range(4):
                    fc = half * 4 + q
                    if "xtr" not in SKIP:
                        tr(pb.s(q * 128, (q + 1) * 128), st.s(fc * 128, (fc + 1) * 128), identf.s())
                    else:
                        mm(pb.s(q * 128, (q + 1) * 128), st.s(fc * 128, (fc + 1) * 128), identf.s(), True, True)
                for q in range(4):
                    fc = half * 4 + q
                    if "xcp" not in SKIP:
                        cp(xT[fc][t].s(cc * 128, (cc + 1) * 128), pb.s(q * 128, (q + 1) * 128),
                           eng="dve" if "xdve" in SKIP else ("act" if ("xact" in SKIP or q % 2) else "dve"))

        def sincos(posf, inv, ncol, cos_out, sin_out, tmpa, tmpb, tmpi, sin_scale=None):
            ang = tmpa
            ts(ang.s(0, ncol), posf.s(0, ncol), inv, None, ALU.mult)
            ts(tmpb.s(0, ncol), ang.s(0, ncol), 1.0 / TWO_PI, None, ALU.mult)
            cp(tmpi.s(0, ncol), tmpb.s(0, ncol))
            cp(tmpb.s(0, ncol), tmpi.s(0, ncol))
            stt(ang.s(0, ncol), tmpb.s(0, ncol), -CW1, ang.s(0, ncol), ALU.mult, ALU.add)
            stt(ang.s(0, ncol), tmpb.s(0, ncol), -CW2, ang.s(0, ncol), ALU.mult, ALU.add)
            ts(tmpb.s(0, ncol), ang.s(0, ncol), -PI_LO, PI_LO, ALU.max, ALU.min)
            act(sin_out.s(0, ncol), tmpb.s(0, ncol), AF.Sin, scale=sin_scale)
            ts(ang.s(0, ncol), ang.s(0, ncol), math.pi / 2.0, None, ALU.add)
            ts(tmpb.s(0, ncol), ang.s(0, ncol), math.pi, TWO_PI, ALU.is_gt, ALU.mult)
            tt(ang.s(0, ncol), ang.s(0, ncol), tmpb.s(0, ncol), ALU.subtract)
            ts(tmpb.s(0, ncol), ang.s(0, ncol), -PI_LO, PI_LO, ALU.max, ALU.min)
            act(cos_out.s(0, ncol), tmpb.s(0, ncol), AF.Sin)

        posi = B(WK0 + 8192, S, I32)
        posf = B(WK0 + 16384, S)
        tA = B(WK0 + 24576, S)
        tB = B(WK0 + 32768, S)
        if "tables" not in SKIP:
            dma("sp", "pos", posi.ap, pos_d.partition_broadcast(128), w=[posi.s()])
            cp(posf.s(), posi.s())
            sincos(posf, invr, S, cosT, sinT, tA, tB, posi)

        def compute_mod(wd, bd_row, ncols, dst):
            sc.phase = "mod"
            MS0 = WK0 + 47104
            rowc = [B(MS0 + i * 1024, 256) for i in range(2)]
            biasc = B(MS0 + 2048, 256)
            pcol = psb[7]
            wv = wview(wd, 8)
            for n in range(ncols // 512):
                sl = load_piece([(wv[:, :, n * 512:(n + 1) * 512], 8, 512, 0, 512)])
                for hh in range(2):
                    c0 = n * 512 + hh * 256
                    dma("sp", "bias0", biasc.ap[0:1, :], bd_row[0:1, c0:c0 + 256], w=[biasc.s(p0=0, p1=1)])
                    pr = psb[hh]
                    for kc in range(8):
                        mm(pr.s(0, 256, 0, 1), cT.s(kc, kc + 1), sl.s(kc * 512 + hh * 256, kc * 512 + hh * 256 + 256),
                           kc == 0, kc == 7)
                    rc_ = rowc[hh]
                    tt(rc_.s(p0=0, p1=1), pr.s(0, 256, 0, 1), biasc.s(p0=0, p1=1), ALU.add)
                    for j in range(2):
                        col = c0 // 128 + j
                        mm(Opd(pcol.ap[:, col:col + 1], pcol.s(0, 128).keys), rc_.s(j * 128, (j + 1) * 128, 0, 1),
                           onef.s(0, 1, 0, 1), True, True)
            cp(dst.s(0, ncols // 128), pcol.s(0, ncols // 128))

        def layer_mod(l):
            P = MODS[l % 2]
            modcol = P["modcol"]
            compute_mod(adaw_d[l], adab_d[l:l + 1, :], 6 * D, modcol)
            base = l * 32
            stt(P["g0sc"].s(), modcol.s(8, 16), 1.0, ngb.s(base + 0, base + 8), ALU.add, ALU.mult)
            stt(P["gga"].s(), modcol.s(16, 24), 1.0, ngb.s(base + 8, base + 16), ALU.add, ALU.mult)
            stt(P["g2sc"].s(), modcol.s(32, 40), 1.0, ngb.s(base + 16, base + 24), ALU.add, ALU.mult)
            stt(P["ggm"].s(), modcol.s(40, 48), 1.0, ngb.s(base + 24, base + 32), ALU.add, ALU.mult)


        nsq = [B(WK0 + i * 1024, T, BF16) for i in range(2)]
        nsd = B(WK0 + 2048, T)
        nrstd = B(WK0 + 4096, T)
        ntmp = [B(WK0 + 6144 + i * 2048, T) for i in range(2)]
        HT0 = WK0 + 10240
        hT = [B(HT0 + fc * 1024, T, BF16) for fc in range(8)]
        WK1 = WK0 + 18432

        def stat_one(i, n, s_, pbank):
            q = nsq[i % 2]
            if i % 2 == 0:
                act(q.s(), s_, AF.Square)
            else:
                tt(q.s(), s_, s_, ALU.mult)
            mm(pbank.s(), onesb.s(), q.s(), i == 0, i == n - 1)

        def stat_fin(pbank, dnorm):
            act(nsd.s(), pbank.s(), AF.Ln, scale=1.0 / dnorm, bias=EPS)
            act(nrstd.s(), nsd.s(), AF.Exp, scale=-0.5)

        def rms_stats(srcs, pbank, dnorm):
            n = len(srcs)
            for i, s_ in enumerate(srcs):
                stat_one(i, n, s_, pbank)
            stat_fin(pbank, dnorm)

        pre_h = {}

        def make_h(t, gsc, shv, sh0, bank=None):
            sc.phase = "make_h"
            rms_stats([xT[fc][t].s() for fc in range(8)], bank if bank is not None else psb[3], D)
            for fc in range(8):
                tm = ntmp[fc % 2]
                stt(tm.s(), xT[fc][t].s(), gsc.s(fc, fc + 1), nrstd.s(), ALU.mult, ALU.mult)
                act(hT[fc].s(), tm.s(), AF.Identity, bias=shv.s(sh0 + fc, sh0 + fc + 1))

        def resid_update(t, ybuf, gg):
            sc.phase = "resid"
            stat_fin(psb[3], D)
            for oc in range(8):
                tm = ntmp[oc % 2]
                stt(tm.s(), ybuf[oc].s(), gg.s(oc, oc + 1), nrstd.s(), ALU.mult, ALU.mult)
                tt(xT[oc][t].s(), xT[oc][t].s(), tm.s(), ALU.add)

        def mlp_tile(l, t, next_mix=None):
            P = MODS[l % 2]
            ggm = P["ggm"]
            make_h(t, P["g2sc"], P["modcol"], 24)
            uT = [B(WK1 + i * 1024, T, BF16) for i in range(8)]
            yb = [B(WK1 + 8192 + oc * 2048, T) for oc in range(8)]
            rtmp = [B(WK1 + 24576 + i * 2048, T) for i in range(2)]
            w1v = wview(w1_d[l], 8)
            for qf in range(4):
                sc.phase = "mlp_w1"
                for half in range(2):
                    c0 = qf * 1024 + half * 512
                    sl = load_piece([(w1v[:, :, c0:c0 + 512], 8, 512, 0, 512)])
                    for j in range(4):
                        pb = psb[(half * 4 + j) % 3]
                        for kc in range(8):
                            mm(pb.s(), sl.s(kc * 512 + j * 128, kc * 512 + (j + 1) * 128), hT[kc].s(), kc == 0, kc == 7)
                        rt = rtmp[j % 2]
                        act(rt.s(), pb.s(), AF.Relu)
                        act(uT[half * 4 + j].s(), rt.s(), AF.Square)
                w2v = w2_d[l][qf * 1024:(qf + 1) * 1024, :].rearrange("(k p) n -> p k n", p=128)
                sc.phase = "mlp_w2"
                for half in range(2):
                    sl = load_piece([(w2v[:, :, half * 512:(half + 1) * 512], 8, 512, 0, 512)])
                    for j in range(4):
                        oc = half * 4 + j
                        pb = psb[4 + (oc % 3)]
                        for fk in range(8):
                            mm(pb.s(), sl.s(fk * 512 + j * 128, fk * 512 + (j + 1) * 128), uT[fk].s(), fk == 0, fk == 7)
                        if qf == 0:
                            cp(yb[oc].s(), pb.s())
                        else:
                            tt(yb[oc].s(), yb[oc].s(), pb.s(), ALU.add)
                        if qf == 3:
                            stat_one(oc, 8, yb[oc].s(), psb[3])
            if next_mix is not None:
                l2, t2 = next_mix
                P2 = MODS[l2 % 2]
                make_h(t2, P2["g0sc"], P2["modcol"], 0, bank=psb[7])
                pre_h[(l2, t2)] = True
            resid_update(t, yb, ggm)

        def ret_tile(l, t):
            P = MODS[l % 2]
            gga = P["gga"]
            if not pre_h.pop((l, t), False):
                make_h(t, P["g0sc"], P["modcol"], 0)
            qT_ = [B(WK1 + dc * 1024, T, BF16) for dc in range(2)]
            kT_ = [B(WK1 + 2048 + dc * 1024, T, BF16) for dc in range(2)]
            ktok = B(WK1 + 4096, 1024, BF16)
            vb = [B(WK1 + 6144 + c * 1024, 512, BF16) for c in range(4)]
            vz = [B(U0 + 59904 + c * 1024, 512, BF16) for c in range(4)]
            sg = [B(WK1 + 10240 + ec * 1024, T, BF16) for ec in range(4)]
            rp = ntmp
            GT0 = WK1 + 14336
            gatedT = [B(GT0 + i * 1024, T, BF16) for i in range(16)]
            gT_h = [B(GT0 + h_ * 4096, 4 * T, BF16) for h_ in range(RH)]
            sg_all = B(WK1 + 10240, 4 * T, BF16)
            winv = wview(win_d[l], 8)
            cs = cosT.s(t * T, (t + 1) * T)
            sn = sinT.s(t * T, (t + 1) * T)
            for hd in range(RH):
                sc.phase = "ret_qk"
                sl = load_piece([(winv[:, :, hd * 256:(hd + 1) * 256], 8, 512, 0, 256),
                                 (winv[:, :, 1024 + hd * 256:1024 + (hd + 1) * 256], 8, 512, 256, 256)])
                for which, dst in ((0, qT_), (1, kT_)):
                    pbs = [psb[0], psb[1]]
                    for dc in range(2):
                        c0 = which * 256 + dc * 128
                        for kc in range(8):
                            mm(pbs[dc].s(), sl.s(kc * 512 + c0, kc * 512 + c0 + 128), hT[kc].s(), kc == 0, kc == 7)
                    x1, x2 = pbs[0].s(), pbs[1].s()
                    tt(rp[0].s(), x1, cs, ALU.mult)
                    tt(rp[1].s(), x2, sn, ALU.mult)
                    tt(dst[0].s(), rp[0].s(), rp[1].s(), ALU.subtract)
                    tt(rp[0].s(), x2, cs, ALU.mult)
                    tt(rp[1].s(), x1, sn, ALU.mult)
                    tt(dst[1].s(), rp[0].s(), rp[1].s(), ALU.add)
                sc.phase = "ret_ktr"
                pbt = psb[2]
                for c in range(4):
                    for dc in range(2):
                        o0 = (c * 2 + dc) * 128
                        tr(pbt.sb16(o0, o0 + 128), kT_[dc].s(c * 128, (c + 1) * 128), identb.s())
                cp(ktok.s(), pbt.sb16(), eng="act")
                sc.phase = "ret_v"
                sl = load_piece([(winv[:, :, 2048 + hd * 512:2048 + (hd + 1) * 512], 8, 512, 0, 512)])
                for c in range(4):
                    pb = psb[c % 2]
                    for kc in range(8):
                        mm(pb.s(), hT[kc].s(c * 128, (c + 1) * 128), sl.s(kc * 512, (kc + 1) * 512), kc == 0, kc == 7)
                    cp(vb[c].s(), pb.s(), eng="act")
                    act(vz[c].s(), pb.s(), AF.Copy, scale=ksm.s(3 + hd, 4 + hd))
                sc.phase = "ret_g"
                sl = load_piece([(winv[:, :, 4096 + hd * 512:4096 + (hd + 1) * 512], 8, 512, 0, 512)])
                for ec in range(4):
                    pb = psb[ec % 2]
                    for kc in range(8):
                        mm(pb.s(), sl.s(kc * 512 + ec * 128, kc * 512 + (ec + 1) * 128), hT[kc].s(), kc == 0, kc == 7)
                    act(sg[ec].s(), pb.s(), AF.Silu)
                sc.phase = "ret_chunk"
                gC = CN["gC"][hd]

                def finish(c, n):
                    csl = (c * 128, (c + 1) * 128)
                    ob = o_sb[n % 2]
                    oq = osq[n % 2]
                    pst = psb[n % 2].s(0, 128)
                    for ec in range(4):
                        mm(pst, onesb.s(), oq.s(ec * 128, (ec + 1) * 128), ec == 0, ec == 3)
                    act(osd.s(), pst, AF.Ln, scale=1.0 / 512.0, bias=EPS)
                    rs = orstd[n % 2]
                    act(rs.s(), osd.s(), AF.Exp, scale=-0.5)
                    ob3 = Opd(ob.ap.rearrange("p (a b) -> p a b", a=4), ob.s().keys)
                    rs3 = Opd(rs.ap.unsqueeze(1).broadcast_to([128, 4, 128]), rs.s().keys)
                    tt(ob3, ob3, rs3, ALU.mult)
                    gk = []
                    sk = []
                    for ec in range(4):
                        gk += gatedT[hd * 4 + ec].s(*csl).keys
                        sk += sg[ec].s(*csl).keys
                    g3 = Opd(gT_h[hd].ap.rearrange("p (a b) -> p a b", a=4)[:, :, csl[0]:csl[1]], gk)
                    s3 = Opd(sg_all.ap.rearrange("p (a b) -> p a b", a=4)[:, :, csl[0]:csl[1]], sk)
                    tt(g3, ob3, s3, ALU.mult)

                for c in range(4):
                    n = t * 4 + c
                    csl = (c * 128, (c + 1) * 128)
                    ps_s = psb[2 + (n % 2)].s(0, 128)
                    for dc in range(2):
                        mm(ps_s, kT_[dc].s(*csl), qT_[dc].s(*csl), dc == 0, dc == 1)
                    PT = PTb[n % 2]
                    tt(PT.s(), ps_s, mask2.s(hd * 128, (hd + 1) * 128), ALU.mult)
                    if n < 15:
                        for dc in range(2):
                            pa = psb[6 + dc]
                            mm(pa.s(), ktok.s((c * 2 + dc) * 128, (c * 2 + dc + 1) * 128), vz[c].s(), True, True)
                    po = psb[4 + (n % 2)]
                    for ec in range(4):
                        osl = po.s(ec * 128, (ec + 1) * 128)
                        mm(osl, vb[c].s(ec * 128, (ec + 1) * 128), PT.s(), True, n == 0)
                        if n > 0:
                            for dc in range(2):
                                mm(osl, Rbf[hd][dc].s(ec * 128, (ec + 1) * 128), qT_[dc].s(*csl), False, dc == 1)
                    if n < 15:
                        for dc in range(2):
                            pa = psb[6 + dc]
                            if n == 0:
                                cp(R32[hd][dc].s(), pa.s())
                            else:
                                stt(R32[hd][dc].s(), R32[hd][dc].s(), gC, pa.s(), ALU.mult, ALU.add)
                            cp(Rbf[hd][dc].s(), R32[hd][dc].s(), eng="act")
                    ob = o_sb[n % 2]
                    tt(ob.s(), po.s(), xirep.s(hd * 512, (hd + 1) * 512), ALU.mult)
                    act(osq[n % 2].s(), ob.s(), AF.Square)
                    if c > 0:
                        finish(c - 1, n - 1)
                finish(3, t * 4 + 3)
            sc.phase = "ret_wout"
            yb = [B(HT0 + oc * 2048, T) for oc in range(8)]
            woutv = wout_d[l].rearrange("(k p) n -> p k n", p=128)
            for pi_ in range(4):
                sl = load_piece([(woutv[:, :, pi_ * 256:(pi_ + 1) * 256], 16, 256, 0, 256)])
                for j in range(2):
                    oc = pi_ * 2 + j
                    pb = psb[oc % 3]
                    for kc in range(16):
                        mm(pb.s(), sl.s(kc * 256 + j * 128, kc * 256 + (j + 1) * 128), gatedT[kc].s(), kc == 0, kc == 15)
                    cp(yb[oc].s(), pb.s())
                    stat_one(oc, 8, yb[oc].s(), psb[3])
            resid_update(t, yb, gga)

        DT0 = WK1
        dC = B(DT0, T)
        dS = B(DT0 + 2048, T)
        dposi = B(DT0 + 4096, T, I32)
        dposf = B(DT0 + 6144, T)
        dtA = B(DT0 + 8192, T)
        dtB = B(DT0 + 10240, T)
        qraw = [B(DT0 + 4096 + i * 1024, T, BF16) for i in range(2)]
        drp = [B(DT0 + 8192 + i * 2048, T) for i in range(2)]

        def diff_tables(t):
            sc.phase = "dtables"
            dma("sp", "pos", dposi.ap, pos_d[0:1, t * T:(t + 1) * T].partition_broadcast(128), w=[dposi.s()])
            cp(dposf.s(), dposi.s())
            sincos(dposf, invd, T, dC, dS, dtA, dtB, dposi, sin_scale=sgn)

        def rope_diff(pb, dst, idx):
            qr = qraw[idx % 2]
            cp(qr.s(), pb.s(), eng="act")
            p2 = psb[3]
            mm(p2.s(), permb.s(), qr.s(), True, True)
            tt(drp[0].s(), pb.s(), dC.s(), ALU.mult)
            tt(drp[1].s(), p2.s(), dS.s(), ALU.mult)
            tt(dst, drp[0].s(), drp[1].s(), ALU.add)

        def kv_tile(t):
            make_h(t, kvgsc, kvmod, 0)
            diff_tables(t)
            kvv = wview(kvw_d, 8)
            sc.phase = "kv"
            for half in range(2):
                sl = load_piece([(kvv[:, :, half * 512:(half + 1) * 512], 8, 512, 0, 512)])
                for j in range(4):
                    i = half * 4 + j
                    pb = psb[i % 3]
                    for kc in range(8):
                        mm(pb.s(), sl.s(kc * 512 + j * 128, kc * 512 + (j + 1) * 128), hT[kc].s(), kc == 0, kc == 7)
                    rope_diff(pb, kTsh[i].s(t * T, (t + 1) * T), i)
            for half in range(2):
                sl = load_piece([(kvv[:, :, 1024 + half * 512:1024 + (half + 1) * 512], 8, 512, 0, 512)])
                for c in range(4):
                    pb = psb[c % 3]
                    for kc in range(8):
                        mm(pb.s(), hT[kc].s(c * 128, (c + 1) * 128), sl.s(kc * 512, (kc + 1) * 512), kc == 0, kc == 7)
                    cp(vsh[t * 4 + c].s(half * 512, (half + 1) * 512), pb.s(), eng="act" if c % 2 else "dve")

        def diff_tile(l, t):
            j_ = l - 2
            P = MODS[l % 2]
            gga = P["gga"]
            if not pre_h.pop((l, t), False):
                make_h(t, P["g0sc"], P["modcol"], 0)
            diff_tables(t)
            QT0 = DT0 + 12288
            qT_ = [B(QT0 + i * 1024, T, BF16) for i in range(8)]
            OT_ = [B(QT0 + 8192 + i * 1024, T, BF16) for i in range(8)]
            E0 = DT0
            Eb = [B(E0 + i * 1024, T, BF16) for i in range(4)]
            Zr = [B(E0 + 4096 + i * 2048, T) for i in range(2)]
            t01 = [B(E0 + 8192 + i * 2048, T) for i in range(2)]
            ofp = ntmp[0]
            wqv = wview(wq_d[j_], 8)
            sc.phase = "diff_q"
            for half in range(2):
                sl = load_piece([(wqv[:, :, half * 512:(half + 1) * 512], 8, 512, 0, 512)])
                for j in range(4):
                    i = half * 4 + j
                    pb = psb[i % 3]
                    for kc in range(8):
                        mm(pb.s(), sl.s(kc * 512 + j * 128, kc * 512 + (j + 1) * 128), hT[kc].s(), kc == 0, kc == 7)
                    rope_diff(pb, qT_[i].s(), i)
            nkb = 4 * t + 4
            sc.phase = "diff_attn"
            Ob = [psb[4], psb[5]]
            Zb = [psb[6], psb[7]]
            pairs = [(i, kb) for i in range(8) for kb in range(nkb)]

            def emitS(pi_):
                i, kb = pairs[pi_]
                r = kb - 4 * t
                q0 = 0 if r < 0 else r * 128
                for a in range(2):
                    pS = psb[2 * (pi_ % 2) + a]
                    mm(pS.s(q0, T), kTsh[i].s(kb * 128, (kb + 1) * 128, 64 * a, 64 * a + 64),
                       qT_[i].s(q0, T, 64 * a, 64 * a + 64), True, True)
                for a in range(2):
                    pS = psb[2 * (pi_ % 2) + a]
                    Et = Eb[2 * (pi_ % 2) + a]
                    act(Et.s(q0, T), pS.s(q0, T), AF.Exp, scale=0.125)
                    if r >= 0:
                        tt(Et.s(q0, q0 + 128), Et.s(q0, q0 + 128), trib.s(), ALU.mult)

            def emitPV(pi_):
                i, kb = pairs[pi_]
                r = kb - 4 * t
                q0 = 0 if r < 0 else r * 128
                for a in range(2):
                    Et = Eb[2 * (pi_ % 2) + a]
                    mm(Ob[a].s(q0, T), vsh[kb].s(i * 128, (i + 1) * 128), Et.s(q0, T), kb == 0, kb == nkb - 1)
                    mm(Zb[a].s(q0, T), onesb.s(), Et.s(q0, T), kb == 0, kb == nkb - 1)
                if kb == nkb - 1:
                    combine(i)

            def combine(i):
                for a in range(2):
                    act(Zr[a].s(), Zb[a].s(), AF.Ln)
                    act(Zr[a].s(), Zr[a].s(), AF.Exp, scale=-1.0)
                    tt(t01[a].s(), Ob[a].s(), Zr[a].s(), ALU.mult)
                stt(ofp.s(), t01[1].s(), neglam.s(), t01[0].s(), ALU.mult, ALU.add)
                q = nsq[i % 2]
                act(q.s(), ofp.s(), AF.Square)
                pst = psb[3]
                mm(pst.s(), onesb.s(), q.s(), True, True)
                act(nsd.s(), pst.s(), AF.Ln, scale=1.0 / 128.0, bias=EPS)
                act(nrstd.s(), nsd.s(), AF.Exp, scale=-0.5)
                stt(OT_[i].s(), ofp.s(), subsc.s(), nrstd.s(), ALU.mult, ALU.mult)

            NU = len(pairs)
            LOOK = 1
            for u in range(min(LOOK, NU)):
                emitS(u)
            for u in range(NU):
                if u + LOOK < NU:
                    emitS(u + LOOK)
                emitPV(u)
            Y0 = DT0
            yb = [B(Y0 + oc * 2048, T) for oc in range(8)]
            wov = wview(wo_d[j_], 8)
            sc.phase = "diff_wo"
            for half in range(2):
                sl = load_piece([(wov[:, :, half * 512:(half + 1) * 512], 8, 512, 0, 512)])
                for j in range(4):
                    oc = half * 4 + j
                    pb = psb[oc % 3]
                    for kc in range(8):
                        mm(pb.s(), sl.s(kc * 512 + j * 128, kc * 512 + (j + 1) * 128), OT_[kc].s(), kc == 0, kc == 7)
                    cp(yb[oc].s(), pb.s())
                    stat_one(oc, 8, yb[oc].s(), psb[3])
            resid_update(t, yb, gga)

        kvmod = B(C0 + 2624, 16)
        layer_mod(0)
        for l in range(min(nlayers, 2)):
            for t in range(NT):
                ret_tile(l, t)
                if t == 1 and l + 1 < nlayers:
                    layer_mod(l + 1)
                nm = (l, t + 1) if t + 1 < NT else ((l + 1, 0) if (l + 1 < min(nlayers, 2)) else None)
                mlp_tile(l, t, nm)
        if nlayers > 2:
            compute_mod(kvaw_d, kvab_d, 2 * D, kvmod)
            stt(kvgsc.s(), kvmod.s(8, 16), 1.0, kvgb.s(), ALU.add, ALU.mult)
            for t in range(NT):
                kv_tile(t)
            for l in range(2, nlayers):
                j_ = l - 2
                lam_init = 0.8 - 0.6 * math.exp(-0.3 * l)
                dma("sp", "misc", lamb.ap, lam_d[j_:j_ + 1, :].partition_broadcast(128), w=[lamb.s()])
                tt(lamt.s(0, 64), lamb.s(0, 64), lamb.s(64, 128), ALU.mult)
                tt(lamt.s(64, 128), lamb.s(128, 192), lamb.s(192, 256), ALU.mult)
                sc.add("dve", lambda e: e.reduce_sum(out=lams.ap[:, 0:1], in_=lamt.ap[:, 0:64], axis=mybir.AxisListType.X),
                       r=[lamt.s()], w=[lams.s()])
                sc.add("dve", lambda e: e.reduce_sum(out=lams.ap[:, 1:2], in_=lamt.ap[:, 64:128], axis=mybir.AxisListType.X),
                       r=[lamt.s()], w=[lams.s()])
                act(lams.s(2, 4), lams.s(0, 2), AF.Exp)
                tt(neglam.s(), lams.s(3, 4), lams.s(2, 3), ALU.subtract)
                ts(neglam.s(), neglam.s(), -lam_init, None, ALU.add)
                ts(subsc.s(), subg.s(j_, j_ + 1), 1.0 - lam_init, None, ALU.mult)
                for t in range(NT):
                    diff_tile(l, t)
                    if t == 1 and l + 1 < nlayers:
                        layer_mod(l + 1)
                    nm = (l, t + 1) if t + 1 < NT else ((l + 1, 0) if l + 1 < nlayers else None)
                    mlp_tile(l, t, nm)

        sc.phase = "out"
        for tc in range(16):
            st = stage[tc % 2]
            t, cc = tc // 4, tc % 4
            for half in range(2):
                pb = psb[(tc * 2 + half) % 4]
                for q in range(4):
                    fc = half * 4 + q
                    tr(pb.s(q * 128, (q + 1) * 128), xT[fc][t].s(cc * 128, (cc + 1) * 128), identf.s())
                cp(st.s(half * 512, (half + 1) * 512), pb.s(), eng="act" if half else "dve")
            dma("sp", f"out{tc % 2}", out_d[tc * 128:(tc + 1) * 128, :], st.ap, r=[st.s()])

        sc.emit(nc, sems, dsems, block)
    return nc, sc


_NC_CACHE = {}


def _prep_inputs(inputs, b, CN):
    f = lambda a: np.ascontiguousarray(np.asarray(a, dtype=np.float32))
    m = {}
    m["x"] = f(inputs["x"][b])
    m["c"] = f(np.asarray(inputs["c"][b]).reshape(8, 128).T)
    m["pos"] = np.ascontiguousarray(np.asarray(inputs["positions"][b], dtype=np.int32).reshape(1, S))
    ng = np.asarray(inputs["norm_g"], dtype=np.float32).reshape(4, 4, 8, 128)
    m["norm_g"] = f(ng.transpose(3, 0, 1, 2).reshape(128, 128))
    m["ada_w"] = f(inputs["ada_w"])
    m["ada_b"] = f(inputs["ada_b"])
    m["ret_w_in"] = f(inputs["ret_w_in"])
    m["ret_w_out"] = f(inputs["ret_w_out"])
    m["kv_norm_g"] = f(np.asarray(inputs["kv_norm_g"]).reshape(8, 128).T)
    m["kv_ada_w"] = f(inputs["kv_ada_w"])
    m["kv_ada_b"] = f(np.asarray(inputs["kv_ada_b"]).reshape(1, 2 * D))
    m["kv_w"] = f(inputs["kv_w"])
    m["diff_w_q"] = f(inputs["diff_w_q"])
    m["diff_w_o"] = f(inputs["diff_w_o"])
    m["diff_lam"] = f(np.asarray(inputs["diff_lam"]).reshape(2, 256))
    m["diff_subln_g"] = f(np.asarray(inputs["diff_subln_g"]).T)
    m["mlp_w1"] = f(inputs["mlp_w1"])
    m["mlp_w2"] = f(inputs["mlp_w2"])
    m["k_identf"] = CN["identf"]
    m["k_tri"] = CN["tri"]
    m["k_perm"] = CN["perm"]
    sm = np.zeros((128, 8), np.float32)
    sm[:, 0:1] = CN["invr"]
    sm[:, 1:2] = CN["invd"]
    sm[:, 2:3] = CN["sgn"]
    sm[:, 3:7] = CN["zeta"]
    m["k_small"] = sm
    m["k_xirep"] = CN["xirep"]
    m["k_mask2"] = CN["mask2"]
    return m


def kernel(**inputs):
    CN = _consts()
    if "nc" not in _NC_CACHE:
        _NC_CACHE["nc"] = build(DEPTH)[0]
    nc = _NC_CACHE["nc"]
    shared = None
    in_maps = []
    for b in range(8):
        m = _prep_inputs(inputs, b, CN) if shared is None else None
        if shared is None:
            shared = m
            in_maps.append(m)
        else:
            mm_ = dict(shared)
            mm_["x"] = np.ascontiguousarray(np.asarray(inputs["x"][b], dtype=np.float32))
            mm_["c"] = np.ascontiguousarray(np.asarray(inputs["c"][b], dtype=np.float32).reshape(8, 128).T)
            mm_["pos"] = np.ascontiguousarray(np.asarray(inputs["positions"][b], dtype=np.int32).reshape(1, S))
            in_maps.append(mm_)
    res = run_bass_kernel_spmd(nc, in_maps, core_ids=list(range(8)))
    out = np.stack([np.asarray(res.results[b]["out"], dtype=np.float32) for b in range(8)], axis=0)
    return out
```

```python
import math
import numpy as np
import concourse.bass as bass
import concourse.mybir as mybir
from concourse.bass_utils import run_bass_kernel_spmd

F32 = mybir.dt.float32
BF16 = mybir.dt.bfloat16
I32 = mybir.dt.int32
AF = mybir.ActivationFunctionType
ALU = mybir.AluOpType

S = 2048
D = 1024
T = 512
NT = S // T
DFF = 4096
EPS = 1e-6
RH = 4
DEPTH = 4
NSLOT = 3
PG = 256

X0 = 0
U0 = 65536
SL0 = 131072
C0 = SL0 + NSLOT * 8192
WK0 = C0 + 6144
SB_BYTES = 207 * 1024
WK_BYTES = SB_BYTES - WK0


class Opd:
    __slots__ = ("ap", "keys")

    def __init__(self, ap, keys):
        self.ap = ap
        self.keys = keys


class Buf:
    def __init__(self, sbt, off, n, dt):
        self.esz = 4 if dt in (F32, I32) else 2
        assert off % 4 == 0
        self.off = off
        self.n = n
        self.dt = dt
        w0 = off // 4
        w1 = (off + n * self.esz + 3) // 4
        ap = sbt[:, w0:w1]
        if dt != F32:
            ap = ap.bitcast(dt)
        self.ap = ap

    def s(self, a=0, b=None, p0=0, p1=128):
        if b is None:
            b = self.n
        k0 = (self.off + a * self.esz) // PG
        k1 = (self.off + b * self.esz - 1) // PG
        return Opd(self.ap[p0:p1, a:b], [("sb", k) for k in range(k0, k1 + 1)])


class PBank:
    def __init__(self, pt, idx):
        self.ap = pt[:]
        self.idx = idx
        self.apb = pt[:].bitcast(BF16)

    def s(self, a=0, b=512, p0=0, p1=128):
        return Opd(self.ap[p0:p1, a:b], [("ps", self.idx)])

    def sb16(self, a=0, b=1024, p0=0, p1=128):
        return Opd(self.apb[p0:p1, a:b], [("ps", self.idx)])


class Sched:
    ENG = ("pe", "act", "dve", "pool", "sp")

    def __init__(self):
        self.ops = []
        self.lastw = {}
        self.readers = {}
        self.dma_cum = {}
        self.phase = "pro"

    def add(self, eng, fn, r=(), w=(), dma=None):
        deps = set()
        for o in r:
            for k in o.keys:
                lw = self.lastw.get(k)
                if lw is not None:
                    deps.add(lw)
                if k[0] == "ps":
                    rd = self.readers.get(k)
                    if rd:
                        deps.update(x for x in rd if self.ops[x]["eng"] != eng)
        for o in w:
            for k in o.keys:
                lw = self.lastw.get(k)
                if lw is not None:
                    deps.add(lw)
                rd = self.readers.get(k)
                if rd:
                    deps.update(rd)
        oid = len(self.ops)
        cdeps = {}
        ddeps = {}
        for d in deps:
            od = self.ops[d]
            if od["dma"] is not None:
                sk = od["dma"]
                ddeps[sk] = self.dma_cum[sk]
            else:
                if od["eng"] == "pe" and eng == "pe":
                    continue
                e = od["eng"]
                if e not in cdeps or cdeps[e] < d:
                    cdeps[e] = d
        if dma is not None:
            self.dma_cum[dma] = self.dma_cum.get(dma, 0) + 16
        self.ops.append(dict(eng=eng, fn=fn, cdeps=cdeps, ddeps=ddeps, dma=dma, sig=False, ph=self.phase))
        for o in r:
            for k in o.keys:
                self.readers.setdefault(k, set()).add(oid)
        for o in w:
            for k in o.keys:
                self.lastw[k] = oid
                self.readers[k] = set()
        return oid

    def emit(self, nc, sems, dsems, block):
        ops = self.ops
        for op in ops:
            for e, d in op["cdeps"].items():
                ops[d]["sig"] = True
        cnt = {e: 0 for e in self.ENG}
        for op in ops:
            if op["dma"] is None and op["sig"]:
                cnt[op["eng"]] += 1
                op["sigval"] = cnt[op["eng"]]
        self.sigcounts = dict(cnt)
        per = {e: [] for e in self.ENG}
        for op in ops:
            per[op["eng"]].append(op)

        def run(ename, eng):
            waited = {}
            for op in per[ename]:
                for e, d in op["cdeps"].items():
                    v = ops[d]["sigval"]
                    if waited.get(e, 0) < v:
                        eng.wait_ge(sems[e], v)
                        waited[e] = v
                for sk, v in op["ddeps"].items():
                    if waited.get(sk, 0) < v:
                        eng.wait_ge(dsems[sk], v)
                        waited[sk] = v
                ins = op["fn"](eng)
                if op["dma"] is not None:
                    ins.then_inc(dsems[op["dma"]], 16)
                elif op["sig"]:
                    ins.then_inc(sems[ename], 1)
            if ename == "sp":
                for sk, v in self.dma_cum.items():
                    if sk.startswith("out"):
                        eng.wait_ge(dsems[sk], v)

        @block.tensor
        def _(e):
            run("pe", e)

        @block.scalar
        def _(e):
            run("act", e)

        @block.vector
        def _(e):
            run("dve", e)

        @block.gpsimd
        def _(e):
            run("pool", e)

        @block.sync
        def _(e):
            run("sp", e)


def _consts():
    c = {}
    c["identf"] = np.eye(128, dtype=np.float32)
    tri = (np.arange(128)[None, :] >= np.arange(128)[:, None]).astype(np.float32)
    c["tri"] = tri
    perm = np.zeros((128, 128), np.float32)
    invd = np.zeros((128, 1), np.float32)
    sgn = np.ones((128, 1), np.float32)
    fr = (500000.0 ** (-np.arange(0, 16, 2, dtype=np.float32) / np.float32(16))).astype(np.float32)
    for p in range(128):
        dd = p % 64
        if dd < 8:
            perm[p + 8, p] = 1.0
            invd[p, 0] = fr[dd]
            sgn[p, 0] = -1.0
        elif dd < 16:
            perm[p - 8, p] = 1.0
            invd[p, 0] = fr[dd - 8]
            sgn[p, 0] = 1.0
    c["perm"] = perm
    c["invd"] = invd
    c["sgn"] = sgn
    c["invr"] = (10000.0 ** (-np.arange(0, 256, 2, dtype=np.float32) / np.float32(256))).astype(np.float32).reshape(128, 1)
    gam = 1.0 - 2.0 ** (-5.0 - np.arange(RH, dtype=np.float64))
    lg = np.log(gam)
    idx = np.arange(128, dtype=np.float64)
    xi = np.exp((idx + 1.0)[None, :] * lg[:, None])
    zeta = np.exp((127.0 - idx)[None, :] * lg[:, None])
    c["gC"] = [float(np.exp(128.0 * lg[h])) for h in range(RH)]
    xirep = np.tile(xi[:, None, :], (1, 4, 1)).reshape(RH * 512)
    c["xirep"] = np.broadcast_to(xirep[None, :], (128, RH * 512)).astype(np.float32).copy()
    c["zeta"] = (zeta.T / 16.0).astype(np.float32).copy()
    m2 = np.zeros((128, RH, 128), np.float64)
    for h in range(RH):
        m2[:, h, :] = tri * np.exp(-(idx + 1.0) * lg[h])[:, None] / 16.0
    c["mask2"] = m2.reshape(128, RH * 128).astype(np.float32)
    return c


TWO_PI = 2.0 * math.pi
CW1 = 6.28125
CW2 = TWO_PI - CW1
PI_LO = 3.1415925


def build(nlayers=DEPTH):
    CN = _consts()
    nc = bass.Bass("TRN2", target_bir_lowering=False)
    dram = {}

    def din(name, shape, dt=F32):
        dram[name] = nc.dram_tensor(name, list(shape), dt, kind="ExternalInput").ap()
        return dram[name]

    x_d = din("x", [S, D])
    c_d = din("c", [128, 8])
    pos_d = din("pos", [1, S], I32)
    ng_d = din("norm_g", [128, 16 * 8])
    adaw_d = din("ada_w", [DEPTH, D, 6 * D])
    adab_d = din("ada_b", [DEPTH, 6 * D])
    win_d = din("ret_w_in", [2, D, 6144])
    wout_d = din("ret_w_out", [2, 2048, D])
    kvg_d = din("kv_norm_g", [128, 8])
    kvaw_d = din("kv_ada_w", [D, 2 * D])
    kvab_d = din("kv_ada_b", [1, 2 * D])
    kvw_d = din("kv_w", [D, 2 * D])
    wq_d = din("diff_w_q", [2, D, D])
    wo_d = din("diff_w_o", [2, D, D])
    lam_d = din("diff_lam", [2, 256])
    sub_d = din("diff_subln_g", [128, 2])
    w1_d = din("mlp_w1", [DEPTH, D, DFF])
    w2_d = din("mlp_w2", [DEPTH, DFF, D])
    k_identf = din("k_identf", [128, 128])
    k_tri = din("k_tri", [128, 128])
    k_perm = din("k_perm", [128, 128])
    k_small = din("k_small", [128, 8])
    k_xirep = din("k_xirep", [128, RH * 512])
    k_mask2 = din("k_mask2", [128, RH * 128])
    out_d = nc.dram_tensor("out", [S, D], F32, kind="ExternalOutput").ap()

    sc = Sched()
    import contextlib
    es = contextlib.ExitStack()
    with es:
        sbt = es.enter_context(nc.sbuf_tensor("SB", [128, SB_BYTES // 4], F32))
        psb = [PBank(es.enter_context(nc.psum_tensor(f"ps{i}", [128, 512], F32)), i) for i in range(8)]
        sems = {e: es.enter_context(nc.semaphore("s_" + e)) for e in Sched.ENG}
        dkeys = [f"slot{i}" for i in range(NSLOT)] + ["misc", "stage0", "stage1", "pos", "bias0", "bias1", "out0", "out1"]
        dsems = {k: es.enter_context(nc.semaphore("d_" + k)) for k in dkeys}
        block = es.enter_context(nc.Block())

        def B(off, n, dt=F32):
            return Buf(sbt, off, n, dt)

        xT = [[B(X0 + (fc * S + t * T) * 4, T) for t in range(NT)] for fc in range(8)]
        slots = [B(SL0 + i * 8192, 4096, BF16) for i in range(NSLOT)]
        identb = B(C0 + 0, 128, BF16)
        onesb = B(C0 + 256, 128, BF16)
        trib = B(C0 + 512, 128, BF16)
        permb = B(C0 + 768, 128, BF16)
        identf = B(C0 + 1024, 128)
        onef = B(C0 + 1536, 4)
        ksm = B(C0 + 1552, 8)
        cT = B(C0 + 1584, 8, BF16)
        c32 = B(C0 + 1600, 8)
        MODS = [dict(modcol=B(C0 + 1664, 48), g0sc=B(C0 + 2368, 8), gga=B(C0 + 2400, 8), g2sc=B(C0 + 2432, 8),
                     ggm=B(C0 + 2464, 8)),
                dict(modcol=B(C0 + 5376, 48), g0sc=B(C0 + 5568, 8), gga=B(C0 + 5600, 8), g2sc=B(C0 + 5632, 8),
                     ggm=B(C0 + 5664, 8))]
        ngb = B(C0 + 1856, 128)
        kvgb = B(C0 + 2560, 8)
        kvgsc = B(C0 + 2592, 8)
        subg = B(C0 + 2720, 2)
        subsc = B(C0 + 2728, 1)
        neglam = B(C0 + 2732, 1)
        lamb = B(C0 + 3072, 256)
        lamt = B(C0 + 4096, 128)
        lams = B(C0 + 4608, 4)
        stg32 = B(C0 + 4864, 128)
        cosT = B(U0, S)
        sinT = B(U0 + 8192, S)
        xirep = B(U0 + 16384, RH * 512)
        R32 = [[B(U0 + 24576 + (h * 2 + dc) * 2048, 512) for dc in range(2)] for h in range(RH)]
        Rbf = [[B(U0 + 40960 + (h * 2 + dc) * 1024, 512, BF16) for dc in range(2)] for h in range(RH)]
        mask2 = B(U0 + 49152, RH * 128)
        o_sb = [B(U0 + 51200 + i * 2048, 512) for i in range(2)]
        osq = [B(U0 + 55296 + i * 1024, 512, BF16) for i in range(2)]
        PTb = [B(U0 + 57344 + i * 256, 128, BF16) for i in range(2)]
        ortmp = [B(U0 + 57856, 128), B(U0 + 63488 + 512, 128)]
        orstd = [B(U0 + 58368 + i * 512, 128) for i in range(2)]
        osd = B(U0 + 59392, 128)
        kTsh = [B(U0 + i * 4096, S, BF16) for i in range(8)]
        vsh = [B(U0 + 32768 + kb * 2048, 1024, BF16) for kb in range(16)]

        def mm(out, lhsT, rhs, start, stop):
            sc.add("pe", lambda e: e.matmul(out.ap, lhsT=lhsT.ap, rhs=rhs.ap, start=start, stop=stop),
                   r=[lhsT, rhs], w=[out])

        def tr(out, in_, ident):
            sc.add("pe", lambda e: e.transpose(out.ap, in_.ap, ident.ap), r=[in_, ident], w=[out])

        def act(out, in_, func, scale=None, bias=None, eng="act"):
            rr = [in_]
            kw = {}
            if scale is not None:
                if isinstance(scale, Opd):
                    rr.append(scale)
                    kw["scale"] = scale.ap
                else:
                    kw["scale"] = scale
            if bias is not None:
                if isinstance(bias, Opd):
                    rr.append(bias)
                    kw["bias"] = bias.ap
                else:
                    kw["bias"] = bias
            sc.add("act", lambda e: e.activation(out=out.ap, in_=in_.ap, func=func, **kw), r=rr, w=[out])

        def tt(out, in0, in1, op, eng="dve"):
            sc.add(eng, lambda e: e.tensor_tensor(out=out.ap, in0=in0.ap, in1=in1.ap, op=op), r=[in0, in1], w=[out])

        def ts(out, in0, s1, s2, op0, op1=None, eng="dve"):
            rr = [in0]
            a1 = s1
            if isinstance(s1, Opd):
                rr.append(s1)
                a1 = s1.ap
            a2 = s2
            if isinstance(s2, Opd):
                rr.append(s2)
                a2 = s2.ap
            if op1 is None:
                sc.add(eng, lambda e: e.tensor_single_scalar(out=out.ap, in_=in0.ap, scalar=a1, op=op0), r=rr, w=[out])
            else:
                sc.add(eng, lambda e: e.tensor_scalar(out=out.ap, in0=in0.ap, scalar1=a1, scalar2=a2, op0=op0, op1=op1),
                       r=rr, w=[out])

        def stt(out, in0, scalar, in1, op0, op1, eng="dve"):
            rr = [in0, in1]
            a = scalar
            if isinstance(scalar, Opd):
                rr.append(scalar)
                a = scalar.ap
            sc.add(eng, lambda e: e.scalar_tensor_tensor(out=out.ap, in0=in0.ap, scalar=a, in1=in1.ap, op0=op0, op1=op1),
                   r=rr, w=[out])

        def cp(out, in_, eng="dve"):
            if eng == "act":
                sc.add("act", lambda e: e.activation(out=out.ap, in_=in_.ap, func=AF.Identity), r=[in_], w=[out])
            else:
                sc.add(eng, lambda e: e.tensor_copy(out=out.ap, in_=in_.ap), r=[in_], w=[out])

        def recip(out, in_):
            sc.add("dve", lambda e: e.reciprocal(out=out.ap, in_=in_.ap), r=[in_], w=[out])

        def memset(out, val, eng="dve"):
            sc.add(eng, lambda e: e.memset(out.ap, val), w=[out])

        def dma(queue, semkey, out_ap, in_ap, r=(), w=()):
            sc.add(queue, lambda e: e.dma_start(out=out_ap, in_=in_ap), r=list(r), w=list(w), dma=semkey)

        piece_ctr = [0]

        def load_piece(srcs):
            i = piece_ctr[0]
            piece_ctr[0] += 1
            sl = slots[i % NSLOT]
            for (src, kcn, W, off, n) in srcs:
                dst = sl.ap.rearrange("p (k n) -> p k n", k=kcn)[:, :, off:off + n]
                dma("pool", f"slot{i % NSLOT}", dst, src, w=[sl.s(k * W + off, k * W + off + n) for k in range(kcn)])
            return sl

        def wview(d2, kcn):
            return d2.rearrange("(k p) n -> p k n", p=128)

        def small_load(dst, src_ap):
            dma("sp", "misc", dst.ap, src_ap, w=[dst.s()])

        small_load(identf, k_identf)
        small_load(ksm, k_small)
        small_load(c32, c_d)
        small_load(ngb, ng_d)
        small_load(kvgb, kvg_d)
        small_load(subg, sub_d)
        small_load(xirep, k_xirep)
        small_load(mask2, k_mask2)
        small_load(stg32, k_tri)
        cp(trib.s(), stg32.s())
        small_load(stg32, k_perm)
        cp(permb.s(), stg32.s())
        cp(identb.s(), identf.s())
        memset(onesb.s(), 1.0)
        memset(onef.s(), 1.0)
        act(cT.s(), c32.s(), AF.Silu)
        invr = ksm.s(0, 1)
        invd = ksm.s(1, 2)
        sgn = ksm.s(2, 3)

        import os
        SKIP = set(os.environ.get("KSKIP", "").split(","))
        stage = [B(WK0 + i * 4096, 1024) for i in range(2)]
        for tc in range(16 if "xload" not in SKIP else 0):
            st = stage[tc % 2]
            dma("sp", f"stage{tc % 2}", st.ap, x_d[tc * 128:(tc + 1) * 128, :], w=[st.s()])
            t, cc = tc // 4, tc % 4
            for half in range(2):
                pb = psb[(tc * 2 + half) % 4]
                for q in range(4):
                    fc = half * 4 + q
                    if "xtr" not in SKIP:
                        tr(pb.s(q * 128, (q + 1) * 128), st.s(fc * 128, (fc + 1) * 128), identf.s())
                    else:
                        mm(pb.s(q * 128, (q + 1) * 128), st.s(fc * 128, (fc + 1) * 128), identf.s(), True, True)
                for q in range(4):
                    fc = half * 4 + q
                    if "xcp" not in SKIP:
                        cp(xT[fc][t].s(cc * 128, (cc + 1) * 128), pb.s(q * 128, (q + 1) * 128),
                           eng="dve" if "xdve" in SKIP else ("act" if ("xact" in SKIP or q % 2) else "dve"))

        def sincos(posf, inv, ncol, cos_out, sin_out, tmpa, tmpb, tmpi, sin_scale=None):
            ang = tmpa
            ts(ang.s(0, ncol), posf.s(0, ncol), inv, None, ALU.mult)
            ts(tmpb.s(0, ncol), ang.s(0, ncol), 1.0 / TWO_PI, None, ALU.mult)
            cp(tmpi.s(0, ncol), tmpb.s(0, ncol))
            cp(tmpb.s(0, ncol), tmpi.s(0, ncol))
            stt(ang.s(0, ncol), tmpb.s(0, ncol), -CW1, ang.s(0, ncol), ALU.mult, ALU.add)
            stt(ang.s(0, ncol), tmpb.s(0, ncol), -CW2, ang.s(0, ncol), ALU.mult, ALU.add)
            ts(tmpb.s(0, ncol), ang.s(0, ncol), -PI_LO, PI_LO, ALU.max, ALU.min)
            act(sin_out.s(0, ncol), tmpb.s(0, ncol), AF.Sin, scale=sin_scale)
            ts(ang.s(0, ncol), ang.s(0, ncol), math.pi / 2.0, None, ALU.add)
            ts(tmpb.s(0, ncol), ang.s(0, ncol), math.pi, TWO_PI, ALU.is_gt, ALU.mult)
            tt(ang.s(0, ncol), ang.s(0, ncol), tmpb.s(0, ncol), ALU.subtract)
            ts(tmpb.s(0, ncol), ang.s(0, ncol), -PI_LO, PI_LO, ALU.max, ALU.min)
            act(cos_out.s(0, ncol), tmpb.s(0, ncol), AF.Sin)

        posi = B(WK0 + 8192, S, I32)
        posf = B(WK0 + 16384, S)
        tA = B(WK0 + 24576, S)
        tB = B(WK0 + 32768, S)
        if "tables" not in SKIP:
            dma("sp", "pos", posi.ap, pos_d.partition_broadcast(128), w=[posi.s()])
            cp(posf.s(), posi.s())
            sincos(posf, invr, S, cosT, sinT, tA, tB, posi)

        def compute_mod(wd, bd_row, ncols, dst):
            sc.phase = "mod"
            MS0 = WK0 + 47104
            rowc = [B(MS0 + i * 1024, 256) for i in range(2)]
            biasc = B(MS0 + 2048, 256)
            pcol = psb[7]
            wv = wview(wd, 8)
            for n in range(ncols // 512):
                sl = load_piece([(wv[:, :, n * 512:(n + 1) * 512], 8, 512, 0, 512)])
                for hh in range(2):
                    c0 = n * 512 + hh * 256
                    dma("sp", "bias0", biasc.ap[0:1, :], bd_row[0:1, c0:c0 + 256], w=[biasc.s(p0=0, p1=1)])
                    pr = psb[hh]
                    for kc in range(8):
                        mm(pr.s(0, 256, 0, 1), cT.s(kc, kc + 1), sl.s(kc * 512 + hh * 256, kc * 512 + hh * 256 + 256),
                           kc == 0, kc == 7)
                    rc_ = rowc[hh]
                    tt(rc_.s(p0=0, p1=1), pr.s(0, 256, 0, 1), biasc.s(p0=0, p1=1), ALU.add)
                    for j in range(2):
                        col = c0 // 128 + j
                        mm(Opd(pcol.ap[:, col:col + 1], pcol.s(0, 128).keys), rc_.s(j * 128, (j + 1) * 128, 0, 1),
                           onef.s(0, 1, 0, 1), True, True)
            cp(dst.s(0, ncols // 128), pcol.s(0, ncols // 128))

        def layer_mod(l):
            P = MODS[l % 2]
            modcol = P["modcol"]
            compute_mod(adaw_d[l], adab_d[l:l + 1, :], 6 * D, modcol)
            base = l * 32
            stt(P["g0sc"].s(), modcol.s(8, 16), 1.0, ngb.s(base + 0, base + 8), ALU.add, ALU.mult)
            stt(P["gga"].s(), modcol.s(16, 24), 1.0, ngb.s(base + 8, base + 16), ALU.add, ALU.mult)
            stt(P["g2sc"].s(), modcol.s(32, 40), 1.0, ngb.s(base + 16, base + 24), ALU.add, ALU.mult)
            stt(P["ggm"].s(), modcol.s(40, 48), 1.0, ngb.s(base + 24, base + 32), ALU.add, ALU.mult)


        nsq = [B(WK0 + i * 1024, T, BF16) for i in range(2)]
        nsd = B(WK0 + 2048, T)
        nrstd = B(WK0 + 4096, T)
        ntmp = [B(WK0 + 6144 + i * 2048, T) for i in range(2)]
        HT0 = WK0 + 10240
        hT = [B(HT0 + fc * 1024, T, BF16) for fc in range(8)]
        WK1 = WK0 + 18432

        def stat_one(i, n, s_, pbank):
            q = nsq[i % 2]
            if i % 2 == 0:
                act(q.s(), s_, AF.Square)
            else:
                tt(q.s(), s_, s_, ALU.mult)
            mm(pbank.s(), onesb.s(), q.s(), i == 0, i == n - 1)

        def stat_fin(pbank, dnorm):
            act(nsd.s(), pbank.s(), AF.Ln, scale=1.0 / dnorm, bias=EPS)
            act(nrstd.s(), nsd.s(), AF.Exp, scale=-0.5)

        def rms_stats(srcs, pbank, dnorm):
            n = len(srcs)
            for i, s_ in enumerate(srcs):
                stat_one(i, n, s_, pbank)
            stat_fin(pbank, dnorm)

        pre_h = {}

        def make_h(t, gsc, shv, sh0, bank=None):
            sc.phase = "make_h"
            rms_stats([xT[fc][t].s() for fc in range(8)], bank if bank is not None else psb[3], D)
            for fc in range(8):
                tm = ntmp[fc % 2]
                stt(tm.s(), xT[fc][t].s(), gsc.s(fc, fc + 1), nrstd.s(), ALU.mult, ALU.mult)
                act(hT[fc].s(), tm.s(), AF.Identity, bias=shv.s(sh0 + fc, sh0 + fc + 1))

        def resid_update(t, ybuf, gg):
            sc.phase = "resid"
            stat_fin(psb[3], D)
            for oc in range(8):
                tm = ntmp[oc % 2]
                stt(tm.s(), ybuf[oc].s(), gg.s(oc, oc + 1), nrstd.s(), ALU.mult, ALU.mult)
                tt(xT[oc][t].s(), xT[oc][t].s(), tm.s(), ALU.add)

        def mlp_tile(l, t, next_mix=None):
            P = MODS[l % 2]
            ggm = P["ggm"]
            make_h(t, P["g2sc"], P["modcol"], 24)
            uT = [B(WK1 + i * 1024, T, BF16) for i in range(8)]
            yb = [B(WK1 + 8192 + oc * 2048, T) for oc in range(8)]
            rtmp = [B(WK1 + 24576 + i * 2048, T) for i in range(2)]
            w1v = wview(w1_d[l], 8)
            for qf in range(4):
                sc.phase = "mlp_w1"
                for half in range(2):
                    c0 = qf * 1024 + half * 512
                    sl = load_piece([(w1v[:, :, c0:c0 + 512], 8, 512, 0, 512)])
                    for j in range(4):
                        pb = psb[(half * 4 + j) % 3]
                        for kc in range(8):
                            mm(pb.s(), sl.s(kc * 512 + j * 128, kc * 512 + (j + 1) * 128), hT[kc].s(), kc == 0, kc == 7)
                        rt = rtmp[j % 2]
                        act(rt.s(), pb.s(), AF.Relu)
                        act(uT[half * 4 + j].s(), rt.s(), AF.Square)
                w2v = w2_d[l][qf * 1024:(qf + 1) * 1024, :].rearrange("(k p) n -> p k n", p=128)
                sc.phase = "mlp_w2"
                for half in range(2):
                    sl = load_piece([(w2v[:, :, half * 512:(half + 1) * 512], 8, 512, 0, 512)])
                    for j in range(4):
                        oc = half * 4 + j
                        pb = psb[4 + (oc % 3)]
                        for fk in range(8):
                            mm(pb.s(), sl.s(fk * 512 + j * 128, fk * 512 + (j + 1) * 128), uT[fk].s(), fk == 0, fk == 7)
                        if qf == 0:
                            cp(yb[oc].s(), pb.s())
                        else:
                            tt(yb[oc].s(), yb[oc].s(), pb.s(), ALU.add)
                        if qf == 3:
                            stat_one(oc, 8, yb[oc].s(), psb[3])
            if next_mix is not None:
                l2, t2 = next_mix
                P2 = MODS[l2 % 2]
                make_h(t2, P2["g0sc"], P2["modcol"], 0, bank=psb[7])
                pre_h[(l2, t2)] = True
            resid_update(t, yb, ggm)

        def ret_tile(l, t):
            P = MODS[l % 2]
            gga = P["gga"]
            if not pre_h.pop((l, t), False):
                make_h(t, P["g0sc"], P["modcol"], 0)
            qT_ = [B(WK1 + dc * 1024, T, BF16) for dc in range(2)]
            kT_ = [B(WK1 + 2048 + dc * 1024, T, BF16) for dc in range(2)]
            ktok = B(WK1 + 4096, 1024, BF16)
            vb = [B(WK1 + 6144 + c * 1024, 512, BF16) for c in range(4)]
            vz = [B(U0 + 59904 + c * 1024, 512, BF16) for c in range(4)]
            sg = [B(WK1 + 10240 + ec * 1024, T, BF16) for ec in range(4)]
            rp = ntmp
            GT0 = WK1 + 14336
            gatedT = [B(GT0 + i * 1024, T, BF16) for i in range(16)]
            gT_h = [B(GT0 + h_ * 4096, 4 * T, BF16) for h_ in range(RH)]
            sg_all = B(WK1 + 10240, 4 * T, BF16)
            winv = wview(win_d[l], 8)
            cs = cosT.s(t * T, (t + 1) * T)
            sn = sinT.s(t * T, (t + 1) * T)
            for hd in range(RH):
                sc.phase = "ret_qk"
                sl = load_piece([(winv[:, :, hd * 256:(hd + 1) * 256], 8, 512, 0, 256),
                                 (winv[:, :, 1024 + hd * 256:1024 + (hd + 1) * 256], 8, 512, 256, 256)])
                for which, dst in ((0, qT_), (1, kT_)):
                    pbs = [psb[2 * which], psb[2 * which + 1]]
                    for dc in range(2):
                        c0 = which * 256 + dc * 128
                        for kc in range(8):
                            mm(pbs[dc].s(), sl.s(kc * 512 + c0, kc * 512 + c0 + 128), hT[kc].s(), kc == 0, kc == 7)
                    x1, x2 = pbs[0].s(), pbs[1].s()
                    tt(rp[0].s(), x1, cs, ALU.mult)
                    tt(rp[1].s(), x2, sn, ALU.mult)
                    tt(dst[0].s(), rp[0].s(), rp[1].s(), ALU.subtract)
                    tt(rp[0].s(), x2, cs, ALU.mult)
                    tt(rp[1].s(), x1, sn, ALU.mult)
                    tt(dst[1].s(), rp[0].s(), rp[1].s(), ALU.add)
                sc.phase = "ret_v"
                sl = load_piece([(winv[:, :, 2048 + hd * 512:2048 + (hd + 1) * 512], 8, 512, 0, 512)])
                for c in range(4):
                    pb = psb[c % 2]
                    for kc in range(8):
                        mm(pb.s(), hT[kc].s(c * 128, (c + 1) * 128), sl.s(kc * 512, (kc + 1) * 512), kc == 0, kc == 7)
                    cp(vb[c].s(), pb.s(), eng="act")
                    act(vz[c].s(), pb.s(), AF.Copy, scale=ksm.s(3 + hd, 4 + hd))
                sc.phase = "ret_g"
                sl = load_piece([(winv[:, :, 4096 + hd * 512:4096 + (hd + 1) * 512], 8, 512, 0, 512)])
                for ec in range(4):
                    pb = psb[ec % 2]
                    for kc in range(8):
                        mm(pb.s(), sl.s(kc * 512 + ec * 128, kc * 512 + (ec + 1) * 128), hT[kc].s(), kc == 0, kc == 7)
                    act(sg[ec].s(), pb.s(), AF.Silu)
                sc.phase = "ret_ktr"
                pbt = psb[2]
                for c in range(4):
                    for dc in range(2):
                        o0 = (c * 2 + dc) * 128
                        tr(pbt.sb16(o0, o0 + 128), kT_[dc].s(c * 128, (c + 1) * 128), identb.s())
                cp(ktok.s(), pbt.sb16(), eng="act")
                sc.phase = "ret_chunk"
                gC = CN["gC"][hd]

                def finish(c, n):
                    csl = (c * 128, (c + 1) * 128)
                    ob = o_sb[n % 2]
                    oq = osq[n % 2]
                    pst = psb[n % 2].s(0, 128)
                    for ec in range(4):
                        mm(pst, onesb.s(), oq.s(ec * 128, (ec + 1) * 128), ec == 0, ec == 3)
                    act(osd.s(), pst, AF.Ln, scale=1.0 / 512.0, bias=EPS)
                    rs = orstd[n % 2]
                    act(rs.s(), osd.s(), AF.Exp, scale=-0.5)
                    ob3 = Opd(ob.ap.rearrange("p (a b) -> p a b", a=4), ob.s().keys)
                    rs3 = Opd(rs.ap.unsqueeze(1).broadcast_to([128, 4, 128]), rs.s().keys)
                    tt(ob3, ob3, rs3, ALU.mult)
                    gk = []
                    sk = []
                    for ec in range(4):
                        gk += gatedT[hd * 4 + ec].s(*csl).keys
                        sk += sg[ec].s(*csl).keys
                    g3 = Opd(gT_h[hd].ap.rearrange("p (a b) -> p a b", a=4)[:, :, csl[0]:csl[1]], gk)
                    s3 = Opd(sg_all.ap.rearrange("p (a b) -> p a b", a=4)[:, :, csl[0]:csl[1]], sk)
                    tt(g3, ob3, s3, ALU.mult)

                for c in range(4):
                    n = t * 4 + c
                    csl = (c * 128, (c + 1) * 128)
                    ps_s = psb[2 + (n % 2)].s(0, 128)
                    for dc in range(2):
                        mm(ps_s, kT_[dc].s(*csl), qT_[dc].s(*csl), dc == 0, dc == 1)
                    PT = PTb[n % 2]
                    tt(PT.s(), ps_s, mask2.s(hd * 128, (hd + 1) * 128), ALU.mult)
                    if n < 15:
                        for dc in range(2):
                            pa = psb[6 + dc]
                            mm(pa.s(), ktok.s((c * 2 + dc) * 128, (c * 2 + dc + 1) * 128), vz[c].s(), True, True)
                    po = psb[4 + (n % 2)]
                    for ec in range(4):
                        osl = po.s(ec * 128, (ec + 1) * 128)
                        mm(osl, vb[c].s(ec * 128, (ec + 1) * 128), PT.s(), True, n == 0)
                        if n > 0:
                            for dc in range(2):
                                mm(osl, Rbf[hd][dc].s(ec * 128, (ec + 1) * 128), qT_[dc].s(*csl), False, dc == 1)
                    if n < 15:
                        for dc in range(2):
                            pa = psb[6 + dc]
                            if n == 0:
                                cp(R32[hd][dc].s(), pa.s())
                            else:
                                stt(R32[hd][dc].s(), R32[hd][dc].s(), gC, pa.s(), ALU.mult, ALU.add)
                            cp(Rbf[hd][dc].s(), R32[hd][dc].s(), eng="act")
                    ob = o_sb[n % 2]
                    tt(ob.s(), po.s(), xirep.s(hd * 512, (hd + 1) * 512), ALU.mult)
                    act(osq[n % 2].s(), ob.s(), AF.Square)
                    if c > 0:
                        finish(c - 1, n - 1)
                finish(3, t * 4 + 3)
            sc.phase = "ret_wout"
            yb = [B(HT0 + oc * 2048, T) for oc in range(8)]
            woutv = wout_d[l].rearrange("(k p) n -> p k n", p=128)
            for pi_ in range(4):
                sl = load_piece([(woutv[:, :, pi_ * 256:(pi_ + 1) * 256], 16, 256, 0, 256)])
                for j in range(2):
                    oc = pi_ * 2 + j
                    pb = psb[oc % 3]
                    for kc in range(16):
                        mm(pb.s(), sl.s(kc * 256 + j * 128, kc * 256 + (j + 1) * 128), gatedT[kc].s(), kc == 0, kc == 15)
                    cp(yb[oc].s(), pb.s())
                    stat_one(oc, 8, yb[oc].s(), psb[3])
            resid_update(t, yb, gga)

        DT0 = WK1
        dC = B(DT0, T)
        dS = B(DT0 + 2048, T)
        dposi = B(DT0 + 4096, T, I32)
        dposf = B(DT0 + 6144, T)
        dtA = B(DT0 + 8192, T)
        dtB = B(DT0 + 10240, T)
        qraw = [B(DT0 + 4096 + i * 1024, T, BF16) for i in range(2)]
        drp = [B(DT0 + 8192 + i * 2048, T) for i in range(2)]

        def diff_tables(t):
            sc.phase = "dtables"
            dma("sp", "pos", dposi.ap, pos_d[0:1, t * T:(t + 1) * T].partition_broadcast(128), w=[dposi.s()])
            cp(dposf.s(), dposi.s())
            sincos(dposf, invd, T, dC, dS, dtA, dtB, dposi, sin_scale=sgn)

        def rope_pre(pb, idx):
            cp(qraw[idx % 2].s(), pb.s(), eng="act")

        def rope_diff(pb, dst, idx):
            qr = qraw[idx % 2]
            p2 = psb[3]
            mm(p2.s(), permb.s(), qr.s(), True, True)
            tt(drp[0].s(), pb.s(), dC.s(), ALU.mult)
            tt(drp[1].s(), p2.s(), dS.s(), ALU.mult)
            tt(dst, drp[0].s(), drp[1].s(), ALU.add)

        def kv_tile(t):
            make_h(t, kvgsc, kvmod, 0)
            diff_tables(t)
            kvv = wview(kvw_d, 8)
            sc.phase = "kv"
            pend = []
            for half in range(2):
                sl = load_piece([(kvv[:, :, half * 512:(half + 1) * 512], 8, 512, 0, 512)])
                for j in range(4):
                    i = half * 4 + j
                    pb = psb[i % 3]
                    for kc in range(8):
                        mm(pb.s(), sl.s(kc * 512 + j * 128, kc * 512 + (j + 1) * 128), hT[kc].s(), kc == 0, kc == 7)
                    rope_pre(pb, i)
                    if pend:
                        rope_diff(*pend.pop())
                    pend.append((pb, kTsh[i].s(t * T, (t + 1) * T), i))
            rope_diff(*pend.pop())
            for half in range(2):
                sl = load_piece([(kvv[:, :, 1024 + half * 512:1024 + (half + 1) * 512], 8, 512, 0, 512)])
                for c in range(4):
                    pb = psb[c % 3]
                    for kc in range(8):
                        mm(pb.s(), hT[kc].s(c * 128, (c + 1) * 128), sl.s(kc * 512, (kc + 1) * 512), kc == 0, kc == 7)
                    cp(vsh[t * 4 + c].s(half * 512, (half + 1) * 512), pb.s(), eng="act" if c % 2 else "dve")

        def diff_tile(l, t):
            j_ = l - 2
            P = MODS[l % 2]
            gga = P["gga"]
            if not pre_h.pop((l, t), False):
                make_h(t, P["g0sc"], P["modcol"], 0)
            diff_tables(t)
            QT0 = DT0 + 12288
            qT_ = [B(QT0 + i * 1024, T, BF16) for i in range(8)]
            OT_ = [B(QT0 + 8192 + i * 1024, T, BF16) for i in range(8)]
            E0 = DT0
            Eb = [B(E0 + i * 1024, T, BF16) for i in range(4)]
            Zr = [B(E0 + 4096 + i * 2048, T) for i in range(2)]
            t01 = [B(E0 + 8192 + i * 2048, T) for i in range(2)]
            ofp = ntmp[0]
            wqv = wview(wq_d[j_], 8)
            sc.phase = "diff_q"
            pend = []
            for half in range(2):
                sl = load_piece([(wqv[:, :, half * 512:(half + 1) * 512], 8, 512, 0, 512)])
                for j in range(4):
                    i = half * 4 + j
                    pb = psb[i % 3]
                    for kc in range(8):
                        mm(pb.s(), sl.s(kc * 512 + j * 128, kc * 512 + (j + 1) * 128), hT[kc].s(), kc == 0, kc == 7)
                    rope_pre(pb, i)
                    if pend:
                        rope_diff(*pend.pop())
                    pend.append((pb, qT_[i].s(), i))
            rope_diff(*pend.pop())
            nkb = 4 * t + 4
            sc.phase = "diff_attn"
            Ob = [psb[4], psb[5]]
            Zb = [psb[6], psb[7]]
            pairs = [(i, kb) for i in range(8) for kb in range(nkb)]

            def emitS(pi_):
                i, kb = pairs[pi_]
                r = kb - 4 * t
                q0 = 0 if r < 0 else r * 128
                for a in range(2):
                    pS = psb[2 * (pi_ % 2) + a]
                    mm(pS.s(q0, T), kTsh[i].s(kb * 128, (kb + 1) * 128, 64 * a, 64 * a + 64),
                       qT_[i].s(q0, T, 64 * a, 64 * a + 64), True, True)
                for a in range(2):
                    pS = psb[2 * (pi_ % 2) + a]
                    Et = Eb[2 * (pi_ % 2) + a]
                    act(Et.s(q0, T), pS.s(q0, T), AF.Exp, scale=0.125)
                    if r >= 0:
                        tt(Et.s(q0, q0 + 128), Et.s(q0, q0 + 128), trib.s(), ALU.mult)

            def emitPV(pi_):
                i, kb = pairs[pi_]
                r = kb - 4 * t
                q0 = 0 if r < 0 else r * 128
                for a in range(2):
                    Et = Eb[2 * (pi_ % 2) + a]
                    mm(Ob[a].s(q0, T), vsh[kb].s(i * 128, (i + 1) * 128), Et.s(q0, T), kb == 0, kb == nkb - 1)
                    mm(Zb[a].s(q0, T), onesb.s(), Et.s(q0, T), kb == 0, kb == nkb - 1)
                if kb == nkb - 1:
                    combine(i)

            def combine(i):
                act(Zr[0].s(), Zb[0].s(), AF.Ln)
                cp(t01[0].s(), Ob[0].s())
                act(Zr[1].s(), Zb[1].s(), AF.Ln)
                act(t01[1].s(), Ob[1].s(), AF.Identity)
                for a in range(2):
                    act(Zr[a].s(), Zr[a].s(), AF.Exp, scale=-1.0)
                    tt(t01[a].s(), t01[a].s(), Zr[a].s(), ALU.mult)
                stt(ofp.s(), t01[1].s(), neglam.s(), t01[0].s(), ALU.mult, ALU.add)
                q = nsq[i % 2]
                act(q.s(), ofp.s(), AF.Square)
                pst = psb[3]
                mm(pst.s(), onesb.s(), q.s(), True, True)
                act(nsd.s(), pst.s(), AF.Ln, scale=1.0 / 128.0, bias=EPS)
                act(nrstd.s(), nsd.s(), AF.Exp, scale=-0.5)
                stt(OT_[i].s(), ofp.s(), subsc.s(), nrstd.s(), ALU.mult, ALU.mult)

            NU = len(pairs)
            LOOK = 1
            for u in range(min(LOOK, NU)):
                emitS(u)
            for u in range(NU):
                if u + LOOK < NU:
                    emitS(u + LOOK)
                emitPV(u)
            Y0 = DT0
            yb = [B(Y0 + oc * 2048, T) for oc in range(8)]
            wov = wview(wo_d[j_], 8)
            sc.phase = "diff_wo"
            for half in range(2):
                sl = load_piece([(wov[:, :, half * 512:(half + 1) * 512], 8, 512, 0, 512)])
                for j in range(4):
                    oc = half * 4 + j
                    pb = psb[oc % 3]
                    for kc in range(8):
                        mm(pb.s(), sl.s(kc * 512 + j * 128, kc * 512 + (j + 1) * 128), OT_[kc].s(), kc == 0, kc == 7)
                    cp(yb[oc].s(), pb.s())
                    stat_one(oc, 8, yb[oc].s(), psb[3])
            resid_update(t, yb, gga)

        kvmod = B(C0 + 2624, 16)
        layer_mod(0)
        for l in range(min(nlayers, 2)):
            for t in range(NT):
                ret_tile(l, t)
                if t == 1 and l + 1 < nlayers:
                    layer_mod(l + 1)
                nm = (l, t + 1) if t + 1 < NT else ((l + 1, 0) if (l + 1 < min(nlayers, 2)) else None)
                mlp_tile(l, t, nm)
        if nlayers > 2:
            compute_mod(kvaw_d, kvab_d, 2 * D, kvmod)
            stt(kvgsc.s(), kvmod.s(8, 16), 1.0, kvgb.s(), ALU.add, ALU.mult)
            for t in range(NT):
                kv_tile(t)
            for l in range(2, nlayers):
                j_ = l - 2
                lam_init = 0.8 - 0.6 * math.exp(-0.3 * l)
                dma("sp", "misc", lamb.ap, lam_d[j_:j_ + 1, :].partition_broadcast(128), w=[lamb.s()])
                tt(lamt.s(0, 64), lamb.s(0, 64), lamb.s(64, 128), ALU.mult)
                tt(lamt.s(64, 128), lamb.s(128, 192), lamb.s(192, 256), ALU.mult)
                sc.add("dve", lambda e: e.reduce_sum(out=lams.ap[:, 0:1], in_=lamt.ap[:, 0:64], axis=mybir.AxisListType.X),
                       r=[lamt.s()], w=[lams.s()])
                sc.add("dve", lambda e: e.reduce_sum(out=lams.ap[:, 1:2], in_=lamt.ap[:, 64:128], axis=mybir.AxisListType.X),
                       r=[lamt.s()], w=[lams.s()])
                act(lams.s(2, 4), lams.s(0, 2), AF.Exp)
                tt(neglam.s(), lams.s(3, 4), lams.s(2, 3), ALU.subtract)
                ts(neglam.s(), neglam.s(), -lam_init, None, ALU.add)
                ts(subsc.s(), subg.s(j_, j_ + 1), 1.0 - lam_init, None, ALU.mult)
                for t in range(NT):
                    diff_tile(l, t)
                    if t == 1 and l + 1 < nlayers:
                        layer_mod(l + 1)
                    nm = (l, t + 1) if t + 1 < NT else ((l + 1, 0) if l + 1 < nlayers else None)
                    mlp_tile(l, t, nm)

        sc.phase = "out"
        for tc in range(16):
            st = stage[tc % 2]
            t, cc = tc // 4, tc % 4
            for half in range(2):
                pb = psb[(tc * 2 + half) % 4]
                for q in range(4):
                    fc = half * 4 + q
                    tr(pb.s(q * 128, (q + 1) * 128), xT[fc][t].s(cc * 128, (cc + 1) * 128), identf.s())
                cp(st.s(half * 512, (half + 1) * 512), pb.s(), eng="act" if half else "dve")
            dma("sp", f"out{tc % 2}", out_d[tc * 128:(tc + 1) * 128, :], st.ap, r=[st.s()])

        sc.emit(nc, sems, dsems, block)
    return nc, sc


_NC_CACHE = {}


def _prep_inputs(inputs, b, CN):
    f = lambda a: np.ascontiguousarray(np.asarray(a, dtype=np.float32))
    m = {}
    m["x"] = f(inputs["x"][b])
    m["c"] = f(np.asarray(inputs["c"][b]).reshape(8, 128).T)
    m["pos"] = np.ascontiguousarray(np.asarray(inputs["positions"][b], dtype=np.int32).reshape(1, S))
    ng = np.asarray(inputs["norm_g"], dtype=np.float32).reshape(4, 4, 8, 128)
    m["norm_g"] = f(ng.transpose(3, 0, 1, 2).reshape(128, 128))
    m["ada_w"] = f(inputs["ada_w"])
    m["ada_b"] = f(inputs["ada_b"])
    m["ret_w_in"] = f(inputs["ret_w_in"])
    m["ret_w_out"] = f(inputs["ret_w_out"])
    m["kv_norm_g"] = f(np.asarray(inputs["kv_norm_g"]).reshape(8, 128).T)
    m["kv_ada_w"] = f(inputs["kv_ada_w"])
    m["kv_ada_b"] = f(np.asarray(inputs["kv_ada_b"]).reshape(1, 2 * D))
    m["kv_w"] = f(inputs["kv_w"])
    m["diff_w_q"] = f(inputs["diff_w_q"])
    m["diff_w_o"] = f(inputs["diff_w_o"])
    m["diff_lam"] = f(np.asarray(inputs["diff_lam"]).reshape(2, 256))
    m["diff_subln_g"] = f(np.asarray(inputs["diff_subln_g"]).T)
    m["mlp_w1"] = f(inputs["mlp_w1"])
    m["mlp_w2"] = f(inputs["mlp_w2"])
    m["k_identf"] = CN["identf"]
    m["k_tri"] = CN["tri"]
    m["k_perm"] = CN["perm"]
    sm = np.zeros((128, 8), np.float32)
    sm[:, 0:1] = CN["invr"]
    sm[:, 1:2] = CN["invd"]
    sm[:, 2:3] = CN["sgn"]
    sm[:, 3:7] = CN["zeta"]
    m["k_small"] = sm
    m["k_xirep"] = CN["xirep"]
    m["k_mask2"] = CN["mask2"]
    return m


def kernel(**inputs):
    CN = _consts()
    if "nc" not in _NC_CACHE:
        _NC_CACHE["nc"] = build(DEPTH)[0]
    nc = _NC_CACHE["nc"]
    shared = None
    in_maps = []
    for b in range(8):
        m = _prep_inputs(inputs, b, CN) if shared is None else None
        if shared is None:
            shared = m
            in_maps.append(m)
        else:
            mm_ = dict(shared)
            mm_["x"] = np.ascontiguousarray(np.asarray(inputs["x"][b], dtype=np.float32))
            mm_["c"] = np.ascontiguousarray(np.asarray(inputs["c"][b], dtype=np.float32).reshape(8, 128).T)
            mm_["pos"] = np.ascontiguousarray(np.asarray(inputs["positions"][b], dtype=np.int32).reshape(1, S))
            in_maps.append(mm_)
    res = run_bass_kernel_spmd(nc, in_maps, core_ids=list(range(8)))
    out = np.stack([np.asarray(res.results[b]["out"], dtype=np.float32) for b in range(8)], axis=0)
    return out
```
